# Optimizing a Trainium2 kernel written in Bass

```python
import math
import jax, jax.numpy as jnp
from jax import lax
import numpy as np

D_MODEL = 1024
BATCH = 8
SEQ = 4096
DEPTH = 2

CHUNK = 64
N_META = 16
Q_BLOCK = 128
EPS = 1e-6
N_BRANCH = 4

CONV_W = D_MODEL // 4
CONV_K = 3
MLSTM_HEADS = 4
MLSTM_HEAD_DIM = D_MODEL // 16
MLSTM_W = MLSTM_HEADS * MLSTM_HEAD_DIM
S5_GROUP_CH = 16
S5_W = D_MODEL // 4
S5_GROUPS = S5_W // S5_GROUP_CH
S5_STATE = 64
S5_DT_MIN = 1e-3
S5_DT_MAX = 1e-1
S5_C_SCALE = 0.5
MLA_HEADS = 8
MLA_NOPE = 64
MLA_ROPE = 32
MLA_V = 64
MLA_Q_LORA = 3 * D_MODEL // 8
MLA_KV_LORA = D_MODEL // 4
ROPE_THETA = 10000.0
MLA_W = MLA_HEADS * MLA_V

BRANCH_WIDTHS = (CONV_W, MLSTM_W, S5_W, MLA_W)
MIX_W = CONV_W + MLSTM_W + S5_W + MLA_W
D_FF = 4 * D_MODEL

IN_SPLITS = (
    N_BRANCH * D_MODEL,
    CONV_W, CONV_W, CONV_W,
    MLSTM_W, MLSTM_W, MLSTM_W, MLSTM_W,
    MLSTM_HEADS, MLSTM_HEADS,
    S5_W,
    MLA_Q_LORA, MLA_KV_LORA, MLA_ROPE,
)
IN_W = sum(IN_SPLITS)

kernel_name = "chunk_causal_gated_hybrid_encoder"


def rmsnorm(x, g):
    xf = x.astype(jnp.float32)
    var = jnp.mean(xf * xf, axis=-1, keepdims=True)
    return (xf * lax.rsqrt(var + EPS) * g.astype(jnp.float32)).astype(x.dtype)


def split_cols(h):
    parts, start = [], 0
    for w in IN_SPLITS:
        parts.append(h[..., start:start + w])
        start += w
    return parts


def apply_rope(x, cos, sin):
    xf = x.astype(jnp.float32)
    half = xf.shape[-1] // 2
    x1, x2 = xf[..., :half], xf[..., half:]
    return jnp.concatenate([x1 * cos - x2 * sin, x2 * cos + x1 * sin], axis=-1).astype(x.dtype)


def short_conv_mixer(b_gate, c_gate, val, conv_w):
    u = c_gate * val
    y = lax.conv_general_dilated(
        u, conv_w[:, None, :].astype(u.dtype), window_strides=(1,),
        padding=[(CONV_K - 1, 0)], dimension_numbers=("NWC", "WIO", "NWC"),
        feature_group_count=CONV_W)
    return b_gate * y


def mlstm_mixer(q, k, v, o, ig_pre, fg_pre, gate_b, norm_g):
    bsz, L, _ = q.shape
    H, d = MLSTM_HEADS, MLSTM_HEAD_DIM
    n_chunks = L // CHUNK
    f32 = jnp.float32
    qf = q.astype(f32).reshape(bsz, L, H, d)
    kf = k.astype(f32).reshape(bsz, L, H, d) * (d ** -0.5)
    vf = v.astype(f32).reshape(bsz, L, H, d)
    gb = gate_b.astype(f32)
    ig = ig_pre.astype(f32) + gb[:H]
    lf = jax.nn.log_sigmoid(fg_pre.astype(f32) + gb[H:])

    def to_chunks(t):
        return t.reshape(bsz, n_chunks, CHUNK, H, d).transpose(1, 0, 3, 2, 4)

    def gate_chunks(t):
        return t.reshape(bsz, n_chunks, CHUNK, H).transpose(1, 0, 3, 2)

    tril = jnp.tril(jnp.ones((CHUNK, CHUNK), dtype=bool))

    def step(carry, inp):
        C, n, m = carry
        qc, kc, vc, igc, lfc = inp
        b = jnp.cumsum(lfc, axis=-1)
        logw = b[..., :, None] - b[..., None, :] + igc[..., None, :]
        logw = jnp.where(tril, logw, -jnp.inf)
        inter = b + m[..., None]
        m_s = jnp.maximum(inter, jnp.max(logw, axis=-1))
        w_intra = jnp.exp(logw - m_s[..., None])
        w_inter = jnp.exp(inter - m_s)
        s = jnp.einsum("bhsd,bhrd->bhsr", qc, kc) * w_intra
        num = (jnp.einsum("bhsr,bhrd->bhsd", s, vc)
               + w_inter[..., None] * jnp.einsum("bhvk,bhsk->bhsv", C, qc))
        den = jnp.sum(s, axis=-1) + w_inter * jnp.einsum("bhsk,bhk->bhs", qc, n)
        h = num / jnp.maximum(jnp.abs(den), jnp.exp(-m_s))[..., None]
        b_last = b[..., -1]
        g = b_last[..., None] - b + igc
        m_new = jnp.maximum(b_last + m, jnp.max(g, axis=-1))
        wr = jnp.exp(g - m_new[..., None])
        decay = jnp.exp(b_last + m - m_new)
        C_new = decay[..., None, None] * C + jnp.einsum("bhr,bhrv,bhrk->bhvk", wr, vc, kc)
        n_new = decay[..., None] * n + jnp.einsum("bhr,bhrk->bhk", wr, kc)
        return (C_new, n_new, m_new), h

    init = (jnp.zeros((bsz, H, d, d), f32), jnp.zeros((bsz, H, d), f32), jnp.zeros((bsz, H), f32))
    _, hs = lax.scan(step, init, (to_chunks(qf), to_chunks(kf), to_chunks(vf),
                                  gate_chunks(ig), gate_chunks(lf)))
    h = hs.transpose(1, 0, 3, 2, 4).reshape(bsz, L, H, d)
    h = h * jax.nn.sigmoid(o.astype(f32).reshape(bsz, L, H, d))
    h = h * lax.rsqrt(jnp.mean(h * h, axis=-1, keepdims=True) + EPS)
    h = h.reshape(bsz, L, MLSTM_W) * norm_g.astype(f32)
    return h.astype(q.dtype)


def s5_mixer(u, a_re, a_im, log_step, b_re, b_im, c_re, c_im, d_skip, w_glu):
    bsz, L, _ = u.shape
    f32 = jnp.float32
    uf = u.astype(f32)
    a_re, a_im = a_re.astype(f32), a_im.astype(f32)
    b_re, b_im = b_re.astype(f32), b_im.astype(f32)
    dt = jnp.exp(log_step.astype(f32))[:, None]
    mag = jnp.exp(a_re * dt)
    lb_re, lb_im = mag * jnp.cos(a_im * dt), mag * jnp.sin(a_im * dt)
    den = a_re * a_re + a_im * a_im
    xr, xi = lb_re - 1.0, lb_im
    z_re = (xr * a_re + xi * a_im) / den
    z_im = (xi * a_re - xr * a_im) / den
    bb_re = z_re[..., None] * b_re - z_im[..., None] * b_im
    bb_im = z_re[..., None] * b_im + z_im[..., None] * b_re
    ug = uf.reshape(bsz, L, S5_GROUPS, S5_GROUP_CH)
    bu_re = jnp.einsum("blgh,gph->blgp", ug, bb_re)
    bu_im = jnp.einsum("blgh,gph->blgp", ug, bb_im)
    at_re = jnp.broadcast_to(lb_re, bu_re.shape)
    at_im = jnp.broadcast_to(lb_im, bu_im.shape)

    def combine(e1, e2):
        ar1, ai1, br1, bi1 = e1
        ar2, ai2, br2, bi2 = e2
        return (ar2 * ar1 - ai2 * ai1, ar2 * ai1 + ai2 * ar1,
                ar2 * br1 - ai2 * bi1 + br2, ar2 * bi1 + ai2 * br1 + bi2)

    _, _, s_re, s_im = lax.associative_scan(combine, (at_re, at_im, bu_re, bu_im), axis=1)
    y = (jnp.einsum("blgp,ghp->blgh", s_re, c_re.astype(f32))
         - jnp.einsum("blgp,ghp->blgh", s_im, c_im.astype(f32)))
    y = y.reshape(bsz, L, S5_W) + d_skip.astype(f32) * uf
    ag = y @ w_glu.astype(f32)
    out = ag[..., :S5_W] * jax.nn.sigmoid(ag[..., S5_W:])
    return out.astype(u.dtype)


def mla_mixer(cq, ckv, kr, q_norm, kv_norm, w_uq, w_ukv, cos, sin, cid):
    bsz, L, _ = cq.shape
    H = MLA_HEADS
    q = (rmsnorm(cq, q_norm) @ w_uq).reshape(bsz, L, H, MLA_NOPE + MLA_ROPE)
    qn = q[..., :MLA_NOPE]
    qr = apply_rope(q[..., MLA_NOPE:], cos[None, :, None, :], sin[None, :, None, :])
    kv = (rmsnorm(ckv, kv_norm) @ w_ukv).reshape(bsz, L, H, MLA_NOPE + MLA_V)
    kn, v = kv[..., :MLA_NOPE], kv[..., MLA_NOPE:]
    krr = apply_rope(kr, cos[None], sin[None])
    scale = (MLA_NOPE + MLA_ROPE) ** -0.5
    nb = L // Q_BLOCK
    qnb = qn.reshape(bsz, nb, Q_BLOCK, H, MLA_NOPE).transpose(1, 0, 2, 3, 4)
    qrb = qr.reshape(bsz, nb, Q_BLOCK, H, MLA_ROPE).transpose(1, 0, 2, 3, 4)
    cidb = cid.reshape(nb, Q_BLOCK)

    def attend(blk):
        qn_b, qr_b, cq_b = blk
        s = (jnp.einsum("bqhd,bkhd->bhqk", qn_b, kn, preferred_element_type=jnp.float32)
             + jnp.einsum("bqhr,bkr->bhqk", qr_b, krr, preferred_element_type=jnp.float32)) * scale
        mask = cid[None, :] <= cq_b[:, None]
        p = jax.nn.softmax(jnp.where(mask, s, -jnp.inf), axis=-1).astype(v.dtype)
        return jnp.einsum("bhqk,bkhd->bqhd", p, v)

    o = lax.map(attend, (qnb, qrb, cidb))
    return o.transpose(1, 0, 2, 3, 4).reshape(bsz, L, MLA_W)


def hybrid_layer(h, cid, cos, sin, norm_g, w_in, conv_w, gate_b, m_norm, a_re, a_im, log_step,
                 b_re, b_im, c_re, c_im, d_skip, w_glu, q_norm, kv_norm, w_uq, w_ukv,
                 w_branch, w_out, w1, w2):
    bsz, L, D = h.shape
    hn = rmsnorm(h, norm_g[0])
    (gate_pre, cb, cc, cv, mq, mk, mv, mo, mi, mf, su, cq, ckv, kr) = split_cols(hn @ w_in)
    gates = jax.nn.sigmoid(gate_pre).reshape(bsz, L, N_BRANCH, D)
    ys = (short_conv_mixer(cb, cc, cv, conv_w),
          mlstm_mixer(mq, mk, mv, mo, mi, mf, gate_b, m_norm),
          s5_mixer(su, a_re, a_im, log_step, b_re, b_im, c_re, c_im, d_skip, w_glu),
          mla_mixer(cq, ckv, kr, q_norm, kv_norm, w_uq, w_ukv, cos, sin, cid))
    start = 0
    merged = None
    for bi in range(N_BRANCH):
        w = BRANCH_WIDTHS[bi]
        term = gates[:, :, bi] * (ys[bi] @ w_branch[start:start + w])
        start += w
        merged = term if merged is None else merged + term
    h = h + rmsnorm(merged @ w_out, norm_g[1])
    hn = rmsnorm(h, norm_g[2])
    ff = jnp.square(jax.nn.relu(hn @ w1)) @ w2
    return h + rmsnorm(ff, norm_g[3])


def setup_inputs(seed: int = 0) -> dict:
    key = jax.random.key(seed)
    ks = jax.random.split(key, 24)
    f32 = jnp.float32

    def nrm(k, shape, scale):
        return jax.random.normal(k, shape, f32) * scale

    H, G, P, Hg = MLSTM_HEADS, S5_GROUPS, S5_STATE, S5_GROUP_CH
    x = nrm(ks[0], (BATCH, SEQ, D_MODEL), 1.0)
    meta = nrm(ks[1], (N_META, D_MODEL), 1.0)
    norm_gains = 1.0 + nrm(ks[2], (DEPTH, 4, D_MODEL), 0.05)
    w_in = nrm(ks[3], (DEPTH, D_MODEL, IN_W), D_MODEL ** -0.5)
    conv_w = nrm(ks[4], (DEPTH, CONV_K, CONV_W), CONV_K ** -0.5)
    gate_base = jnp.concatenate([jnp.zeros((H,), f32), jnp.linspace(3.0, 6.0, H, dtype=f32)])
    mlstm_gate_b = gate_base + nrm(ks[5], (DEPTH, 2 * H), 0.1)
    mlstm_norm = 1.0 + nrm(ks[6], (DEPTH, MLSTM_W), 0.05)
    s5_a_re = -0.5 + nrm(ks[7], (DEPTH, G, P), 0.01)
    s5_a_im = math.pi * jnp.arange(P, dtype=f32) + nrm(ks[8], (DEPTH, G, P), 0.01)
    s5_log_step = jax.random.uniform(ks[9], (DEPTH, G), f32, math.log(S5_DT_MIN), math.log(S5_DT_MAX))
    s5_b_re = nrm(ks[10], (DEPTH, G, P, Hg), (2 * Hg) ** -0.5)
    s5_b_im = nrm(ks[11], (DEPTH, G, P, Hg), (2 * Hg) ** -0.5)
    s5_c_re = nrm(ks[12], (DEPTH, G, Hg, P), S5_C_SCALE)
    s5_c_im = nrm(ks[13], (DEPTH, G, Hg, P), S5_C_SCALE)
    s5_d = nrm(ks[14], (DEPTH, S5_W), 0.5)
    s5_glu = nrm(ks[15], (DEPTH, S5_W, 2 * S5_W), S5_W ** -0.5)
    mla_q_norm = 1.0 + nrm(ks[16], (DEPTH, MLA_Q_LORA), 0.05)
    mla_kv_norm = 1.0 + nrm(ks[17], (DEPTH, MLA_KV_LORA), 0.05)
    mla_w_uq = nrm(ks[18], (DEPTH, MLA_Q_LORA, MLA_HEADS * (MLA_NOPE + MLA_ROPE)), MLA_Q_LORA ** -0.5)
    mla_w_ukv = nrm(ks[19], (DEPTH, MLA_KV_LORA, MLA_HEADS * (MLA_NOPE + MLA_V)), MLA_KV_LORA ** -0.5)
    bks = jax.random.split(ks[20], N_BRANCH)
    w_branch = jnp.concatenate(
        [nrm(bks[i], (DEPTH, BRANCH_WIDTHS[i], D_MODEL), BRANCH_WIDTHS[i] ** -0.5) for i in range(N_BRANCH)],
        axis=1)
    w_out = nrm(ks[21], (DEPTH, D_MODEL, D_MODEL), D_MODEL ** -0.5)
    mlp_w1 = nrm(ks[22], (DEPTH, D_MODEL, D_FF), D_MODEL ** -0.5)
    mlp_w2 = nrm(ks[23], (DEPTH, D_FF, D_MODEL), D_FF ** -0.5)
    return {"x": x, "meta": meta, "norm_gains": norm_gains, "w_in": w_in, "conv_w": conv_w,
            "mlstm_gate_b": mlstm_gate_b, "mlstm_norm": mlstm_norm,
            "s5_a_re": s5_a_re, "s5_a_im": s5_a_im, "s5_log_step": s5_log_step,
            "s5_b_re": s5_b_re, "s5_b_im": s5_b_im, "s5_c_re": s5_c_re, "s5_c_im": s5_c_im,
            "s5_d": s5_d, "s5_glu": s5_glu, "mla_q_norm": mla_q_norm, "mla_kv_norm": mla_kv_norm,
            "mla_w_uq": mla_w_uq, "mla_w_ukv": mla_w_ukv, "w_branch": w_branch, "w_out": w_out,
            "mlp_w1": mlp_w1, "mlp_w2": mlp_w2}


def reference(x, meta, norm_gains, w_in, conv_w, mlstm_gate_b, mlstm_norm, s5_a_re, s5_a_im,
              s5_log_step, s5_b_re, s5_b_im, s5_c_re, s5_c_im, s5_d, s5_glu, mla_q_norm,
              mla_kv_norm, mla_w_uq, mla_w_ukv, w_branch, w_out, mlp_w1, mlp_w2):
    bsz, S, D = x.shape
    L = S + N_META
    Lp = ((L + Q_BLOCK - 1) // Q_BLOCK) * Q_BLOCK
    h = jnp.concatenate([
        jnp.broadcast_to(meta.astype(x.dtype)[None], (bsz, N_META, D)),
        x,
        jnp.zeros((bsz, Lp - L, D), x.dtype)], axis=1)
    pos = jnp.arange(Lp)
    cid = (pos + CHUNK - N_META) // CHUNK
    inv_freq = ROPE_THETA ** (-jnp.arange(0, MLA_ROPE, 2, dtype=jnp.float32) / MLA_ROPE)
    ang = pos.astype(jnp.float32)[:, None] * inv_freq[None, :]
    cos, sin = jnp.cos(ang), jnp.sin(ang)
    for l in range(DEPTH):
        h = hybrid_layer(h, cid, cos, sin, norm_gains[l], w_in[l], conv_w[l], mlstm_gate_b[l],
                         mlstm_norm[l], s5_a_re[l], s5_a_im[l], s5_log_step[l], s5_b_re[l],
                         s5_b_im[l], s5_c_re[l], s5_c_im[l], s5_d[l], s5_glu[l], mla_q_norm[l],
                         mla_kv_norm[l], mla_w_uq[l], mla_w_ukv[l], w_branch[l], w_out[l],
                         mlp_w1[l], mlp_w2[l])
    return h[:, N_META:N_META + S]
```

```python
import contextlib
import math
import numpy as np
import ml_dtypes
import concourse.bass as bass
import concourse.mybir as mybir
from concourse.bass_utils import run_bass_kernel_spmd
from concourse.alu_op_type import AluOpType as ALU

F32 = mybir.dt.float32
BF16 = mybir.dt.bfloat16
AF = mybir.ActivationFunctionType
AX = mybir.AxisListType

D = 1024
S_LEN = 4096
NMETA = 16
L = 4224
NT = L // 128
DEPTH = 2
EPS = 1e-6
IN_W = 6824
DFF = 4096
BLOCKS = [(i * 512, 512) for i in range(8)] + [(4096, 128)]

C_GATE, C_CB, C_CC, C_CV = 0, 4096, 4352, 4608
C_MQ, C_MK, C_MV, C_MO, C_MI, C_MF = 4864, 5120, 5376, 5632, 5888, 5892
C_SU, C_CQ, C_CKV, C_KR = 5896, 6152, 6536, 6792
TM_C0, TM_W = C_MK, 776


class Sched:
    EPOCH = 16000
    R = 8

    def __init__(self, nc, es):
        self.nc = nc
        self.es = es
        self.eng = {"pe": nc.tensor, "act": nc.scalar, "dve": nc.vector, "pool": nc.gpsimd, "sp": nc.sync}
        self.cnt = {e: 0 for e in self.eng}
        self.csem = {}
        self.dcnt = {e: 0 for e in self.eng}
        self.dsem = {}
        self.waited = {e: {} for e in self.eng}
        self.last_w = {}
        self.readers = {}
        self.open_dmas = []
        self.last_tok = {e: None for e in self.eng}
        self.nsem = 0

    def _sem(self, key):
        tab = self.csem if key[0] == "c" else self.dsem
        if key not in tab:
            tab[key] = self.es.enter_context(self.nc.semaphore("s_%s_%s_%s" % key))
            self.nsem += 1
        return tab[key]

    def _wait(self, eng, tok):
        key, val = tok
        if self.waited[eng].get(key, 0) >= val:
            return
        self.eng[eng].wait_ge(self._sem(key), val)
        self.waited[eng][key] = val

    def op(self, eng, fns, r=(), w=(), dma=False):
        deps = []
        for k in r:
            t = self.last_w.get(k)
            if t is not None:
                deps.append((t, True))
        for k in w:
            t = self.last_w.get(k)
            if t is not None:
                deps.append((t, False))
            for t in self.readers.get(k, ()):
                deps.append((t, False))
        for (t, raw) in deps:
            teng, tdma, tok = t
            if (not tdma) and teng == eng and eng == "pe" and not dma:
                continue
            self._wait(eng, tok)
        if not isinstance(fns, (list, tuple)):
            fns = [fns]
        if dma:
            k = self.dcnt[eng]
            slot = k % self.R
            key = ("d", eng, slot)
            if k >= self.R:
                self._wait(eng, (key, 16 * (k // self.R)))
            ins = None
            for f in fns:
                ins = f()
            ins.then_inc(self._sem(key), 16)
            self.dcnt[eng] = k + 1
            tok = (key, 16 * (k // self.R + 1))
            me = (eng, True, tok)
            self.open_dmas.append(me)
        else:
            ins = None
            for f in fns:
                ins = f()
            n = self.cnt[eng]
            key = ("c", eng, n // self.EPOCH)
            ins.then_inc(self._sem(key), 1)
            self.cnt[eng] = n + 1
            tok = (key, n % self.EPOCH + 1)
            me = (eng, False, tok)
            self.last_tok[eng] = me
        for k in r:
            self.readers.setdefault(k, []).append(me)
        for k in w:
            self.last_w[k] = me
            self.readers[k] = []
        return me

    def barrier(self, engines=None):
        toks = [t for t in self.last_tok.values() if t is not None] + self.open_dmas
        for e in (engines or self.eng):
            for (_, _, tok) in toks:
                self._wait(e, tok)
        self.open_dmas = []
        self.last_w = {}
        self.readers = {}


class Builder:
    def __init__(self, debug_taps=()):
        self.debug_taps = set(debug_taps)
        self.nc = bass.Bass("TRN2", target_bir_lowering=False)
        self.es = contextlib.ExitStack()

    def dram_in(self, name, shape, dt=F32):
        return self.nc.dram_tensor(name, list(shape), dt, kind="ExternalInput").ap()

    def dram_scr(self, name, shape, dt):
        kind = "ExternalOutput" if name in self.debug_taps else "Internal"
        return self.nc.dram_tensor(name, list(shape), dt, kind=kind).ap()

    def sb(self, stack, name, shape, dt):
        self._uid = getattr(self, "_uid", 0) + 1
        return stack.enter_context(self.nc.sbuf_tensor("%s_%d" % (name, self._uid), list(shape), dt))

    def build(self, n_layers=DEPTH, stop_after=None):
        nc = self.nc
        with self.es as es:
            self.S = Sched(nc, es)
            self._declare_io()
            self._globals(es)
            import os
            if "prologue" not in os.environ.get("K_SKIP", "").split(","):
                self._prologue()
            for l in range(n_layers):
                self._layer(l, stop_after)
                if stop_after is not None:
                    break
            if stop_after is None:
                self._epilogue()
            self.S.barrier()
        return nc

    def _declare_io(self):
        di = self.dram_in
        self.x = di("x", [S_LEN, D])
        self.meta = di("meta", [NMETA, D])
        self.norm_gains = di("norm_gains", [DEPTH, 4, D])
        self.w_in = di("w_in", [DEPTH, D, IN_W])
        self.conv_w = di("conv_w", [DEPTH, 3, 256])
        self.mlstm_gate_b = di("mlstm_gate_b", [DEPTH, 8])
        self.mlstm_norm = di("mlstm_norm", [DEPTH, 256])
        self.s5_a_re = di("s5_a_re", [DEPTH, 16, 64])
        self.s5_a_im = di("s5_a_im", [DEPTH, 16, 64])
        self.s5_log_step = di("s5_log_step", [DEPTH, 16])
        self.s5_b_re = di("s5_b_re", [DEPTH, 16, 64, 16])
        self.s5_b_im = di("s5_b_im", [DEPTH, 16, 64, 16])
        self.s5_c_re = di("s5_c_re", [DEPTH, 16, 16, 64])
        self.s5_c_im = di("s5_c_im", [DEPTH, 16, 16, 64])
        self.s5_d = di("s5_d", [DEPTH, 256])
        self.s5_glu = di("s5_glu", [DEPTH, 256, 512])
        self.mla_q_norm = di("mla_q_norm", [DEPTH, 384])
        self.mla_kv_norm = di("mla_kv_norm", [DEPTH, 256])
        self.mla_w_uq = di("mla_w_uq", [DEPTH, 384, 768])
        self.mla_w_ukv = di("mla_w_ukv", [DEPTH, 256, 1024])
        self.w_branch = di("w_branch", [DEPTH, 1280, D])
        self.w_out = di("w_out", [DEPTH, D, D])
        self.mlp_w1 = di("mlp_w1", [DEPTH, D, DFF])
        self.mlp_w2 = di("mlp_w2", [DEPTH, DFF, D])
        self.c_ident = di("c_ident", [128, 128])
        self.c_cos = di("c_cos", [32, L])
        self.c_sin = di("c_sin", [32, L])
        self.c_psw = di("c_psw", [32, 32])
        self.c_mask = di("c_mask", [2, 128, 128], BF16)
        self.c_tri = di("c_tri", [128, 128])
        self.c_mrs = di("c_mrs", [128, 64])
        self.c_ob = di("c_ob", [128, 2, 128])
        self.c_iota = di("c_iota", [512])
        self.out = self.nc.dram_tensor("out", [S_LEN, D], F32, kind="ExternalOutput").ap()
        ds = self.dram_scr
        self.hT = ds("hT", [8, 128, L], F32)
        self.gT = ds("gT", [32, 128, L], BF16)
        self.convT = ds("convT", [6, 128, L], F32)
        self.mqkT = ds("mqkT", [4, 128, L], BF16)
        self.suT = ds("suT", [2, 128, L], BF16)
        self.cqT = ds("cqT", [3, 128, L], F32)
        self.ckvT = ds("ckvT", [2, 128, L], F32)
        self.krT = ds("krT", [32, L], F32)
        self.mTM = ds("mTM", [L, TM_W], F32)
        self.ysT = ds("ysT", [10, 128, L], BF16)
        self.hn2T = ds("hn2T", [8, 128, L], BF16)
        self.aT = ds("aT", [32, 128, L], BF16)

    def _globals(self, es):
        nc, S = self.nc, self.S
        self.ps = [es.enter_context(nc.psum_tensor("ps%d" % i, [128, 512], F32)) for i in range(8)]
        self.ident = self.sb(es, "ident", [128, 128], F32)
        self.ones_f = self.sb(es, "ones_f", [128, 128], F32)
        self.eps_t = self.sb(es, "eps_t", [128, 1], F32)
        self.one_t = self.sb(es, "one_t", [128, 1], F32)
        self.gains = self.sb(es, "gains", [128, DEPTH * 4, 8], F32)
        S.op("sp", lambda: nc.sync.dma_start(out=self.ident[:], in_=self.c_ident[:]), w=[("ident",)], dma=True)
        S.op("dve", lambda: nc.vector.memset(self.ones_f[:], 1.0), w=[("ones_f",)])
        S.op("dve", lambda: nc.vector.memset(self.eps_t[:], EPS), w=[("eps_t",)])
        S.op("dve", lambda: nc.vector.memset(self.one_t[:], 1.0), w=[("one_t",)])
        with nc.allow_non_contiguous_dma(reason="tiny gain load"):
            S.op("sp", lambda: nc.sync.dma_start(
                out=self.gains[:], in_=self.norm_gains.rearrange("l k (c p) -> p (l k) c", p=128)),
                w=[("gains",)], dma=True)
        S.barrier()

    def gain(self, l, k, c):
        return self.gains[:, l * 4 + k, c:c + 1]

    def _prologue(self):
        nc, S = self.nc, self.S
        with contextlib.ExitStack() as ph:
            xt = [self.sb(ph, "xt%d" % i, [128, D], F32) for i in range(2)]
            ht = [self.sb(ph, "ht%d" % i, [128, 8, 128], F32) for i in range(2)]
            for i in range(NT):
                b = i % 2
                xb, hb = xt[b], ht[b]
                lo = 128 * i - NMETA
                if i == 0:
                    S.op("sp", lambda xb=xb: nc.sync.dma_start(out=xb[0:NMETA, :], in_=self.meta[:, :]),
                         w=[("xt", b)], dma=True)
                    S.op("sp", lambda xb=xb: nc.sync.dma_start(out=xb[NMETA:128, :], in_=self.x[0:128 - NMETA, :]),
                         w=[("xt", b, 1)], dma=True)
                    rk = [("xt", b), ("xt", b, 1)]
                elif i == NT - 1:
                    S.op("dve", lambda xb=xb: nc.vector.memset(xb[:], 0.0), w=[("xt", b)])
                    S.op("sp", lambda xb=xb, lo=lo: nc.sync.dma_start(out=xb[0:NMETA, :], in_=self.x[lo:S_LEN, :]),
                         r=[("xt", b)], w=[("xt", b, 1)], dma=True)
                    rk = [("xt", b), ("xt", b, 1)]
                else:
                    S.op("sp", lambda xb=xb, lo=lo: nc.sync.dma_start(out=xb[:], in_=self.x[lo:lo + 128, :]),
                         w=[("xt", b), ("xt", b, 1)], dma=True)
                    rk = [("xt", b), ("xt", b, 1)]
                for half in range(2):
                    pst = self.ps[(2 * i + half) % 4]
                    pk = ("ps", (2 * i + half) % 4)
                    S.op("pe", [lambda c=c, xb=xb, pst=pst, half=half: nc.tensor.transpose(
                        pst[:, c * 128:(c + 1) * 128], xb[:, (half * 4 + c) * 128:(half * 4 + c + 1) * 128],
                        self.ident[:]) for c in range(4)], r=rk, w=[pk])
                    eng = "act" if half == 0 else "dve"
                    if half == 0:
                        S.op("act", lambda hb=hb, pst=pst: nc.scalar.copy(
                            out=hb[:, 0:4, :], in_=pst[:, :].rearrange("p (c t) -> p c t", c=4)),
                            r=[pk], w=[("ht", b, 0)])
                    else:
                        S.op("dve", lambda hb=hb, pst=pst: nc.vector.tensor_copy(
                            out=hb[:, 4:8, :], in_=pst[:, :].rearrange("p (c t) -> p c t", c=4)),
                            r=[pk], w=[("ht", b, 1)])
                S.op("sp", lambda hb=hb, i=i: nc.sync.dma_start(
                    out=self.hT[:, :, i * 128:(i + 1) * 128].rearrange("c p t -> p c t"), in_=hb[:]),
                    r=[("ht", b, 0), ("ht", b, 1)], w=[("d_hT", i // 4)], dma=True)
            S.barrier()

    def _layer(self, l, stop_after=None):
        import os
        skip = os.environ.get("K_SKIP", "").split(",")
        if "inproj" not in skip:
            self._inproj(l)
        if stop_after == "inproj":
            return
        if "conv" not in skip:
            self._conv(l)
        if "mla" not in skip:
            self._mla(l)
        if stop_after == "mla":
            return
        if "mlstm" not in skip:
            self._mlstm(l)
        if stop_after == "mlstm":
            return
        if "s5" not in skip:
            self._s5(l)
        if stop_after == "s5":
            return
        self._merge(l)
        if stop_after == "merge":
            return
        self._ffn1(l)
        self._ffn2(l)

    def rms_rstd(self, src, nch, nfeat, tn, sq, rstd, pbank, rkeys, key):
        nc, S = self.nc, self.S
        S.op("act", lambda: nc.scalar.activation(out=sq[:, 0:nch, 0:tn], in_=src[:, 0:nch, 0:tn], func=AF.Square),
             r=rkeys, w=[("sq", key)])
        pst = self.ps[pbank]
        S.op("pe", [lambda c=c: nc.tensor.matmul(pst[:, 0:tn], self.ones_f[:], sq[:, c, 0:tn],
                                                  start=(c == 0), stop=(c == nch - 1)) for c in range(nch)],
             r=[("sq", key)], w=[("ps", pbank)])
        S.op("act", lambda: nc.scalar.activation(out=rstd[:, 0:tn], in_=pst[:, 0:tn], func=AF.Sqrt,
                                                 bias=self.eps_t[:], scale=1.0 / nfeat),
             r=[("ps", pbank)], w=[("rstd", key)])
        S.op("dve", lambda: nc.vector.reciprocal(out=rstd[:, 0:tn], in_=rstd[:, 0:tn]),
             r=[("rstd", key)], w=[("rstd", key)])

    def _conv(self, l):
        nc, S = self.nc, self.S
        with contextlib.ExitStack() as ph:
            cw = self.sb(ph, "cw", [128, 3, 2], F32)
            cin = [self.sb(ph, "cin%d" % i, [128, 6, 512], F32) for i in range(2)]
            u = self.sb(ph, "u", [128, 2, 514], F32)
            y = [self.sb(ph, "y%d" % i, [128, 512], F32) for i in range(2)]
            o = [self.sb(ph, "o%d" % i, [128, 512], BF16) for i in range(4)]
            with nc.allow_non_contiguous_dma(reason="tiny conv weight load"):
                for k in range(3):
                    S.op("sp", lambda k=k: nc.sync.dma_start(
                        out=cw[:, k, :], in_=self.conv_w[l, k].rearrange("(c p) -> p c", p=128)),
                        w=[("cw", k)], dma=True)
            S.op("dve", lambda: nc.vector.memset(u[:, :, 0:2], 0.0), w=[("uh",)])
            oc = 0
            for n, (t0, tn) in enumerate(BLOCKS):
                b = n % 2
                S.op("sp", lambda b=b, t0=t0, tn=tn: nc.sync.dma_start(
                    out=cin[b][:, :, 0:tn], in_=self.convT[:, :, t0:t0 + tn].rearrange("c p t -> p c t")),
                    w=[("cin", b)], dma=True)
                for c in range(2):
                    S.op("pool", lambda b=b, c=c, tn=tn: nc.gpsimd.tensor_tensor(
                        out=u[:, c, 2:2 + tn], in0=cin[b][:, 2 + c, 0:tn], in1=cin[b][:, 4 + c, 0:tn], op=ALU.mult),
                        r=[("cin", b)], w=[("u", c)])
                    yb = y[c]
                    S.op("dve", lambda c=c, tn=tn, yb=yb: nc.vector.tensor_scalar(
                        out=yb[:, 0:tn], in0=u[:, c, 2:2 + tn], scalar1=cw[:, 2, c:c + 1], scalar2=None, op0=ALU.mult),
                        r=[("u", c), ("cw", 2)], w=[("y", c)])
                    for k in (1, 0):
                        S.op("dve", lambda c=c, tn=tn, yb=yb, k=k: nc.vector.scalar_tensor_tensor(
                            out=yb[:, 0:tn], in0=u[:, c, k:k + tn], scalar=cw[:, k, c:c + 1], in1=yb[:, 0:tn],
                            op0=ALU.mult, op1=ALU.add), r=[("u", c), ("uh",), ("y", c), ("cw", k)], w=[("y", c)])
                    ob = o[oc % 4]
                    okey = ("o", oc % 4)
                    oc += 1
                    S.op("dve", lambda b=b, c=c, tn=tn, yb=yb, ob=ob: nc.vector.tensor_tensor(
                        out=ob[:, 0:tn], in0=yb[:, 0:tn], in1=cin[b][:, c, 0:tn], op=ALU.mult),
                        r=[("y", c), ("cin", b)], w=[okey])
                    S.op("sp", lambda c=c, t0=t0, tn=tn, ob=ob: nc.sync.dma_start(
                        out=self.ysT[c, :, t0:t0 + tn], in_=ob[:, 0:tn]), r=[okey], w=[("d_ys", c, n)], dma=True)
                S.op("act", lambda tn=tn: nc.scalar.copy(out=u[:, :, 0:2], in_=u[:, :, tn:tn + 2]),
                     r=[("u", 0), ("u", 1)], w=[("uh",)])
            S.barrier()

    def _mlstm(self, l):
        nc, S = self.nc, self.S
        with contextlib.ExitStack() as ph:
            sb = lambda name, shape, dt: self.sb(ph, name, shape, dt)
            tri = sb("tri", [128, 128], F32)
            ob = sb("ob", [128, 2, 128], F32)
            mrs = sb("mrs", [128, 64], F32)
            gb = sb("gb", [128, 8], F32)
            ng = sb("ng", [128, 256], F32)
            ident_b = sb("ident_b2", [128, 128], BF16)
            Cst = sb("Cst", [128, 2, 65], F32)
            Cbf = sb("Cbf", [128, 2, 65], BF16)
            mt = [sb("mt%d" % i, [128, TM_W], F32) for i in range(2)]
            qk = [sb("qk%d" % i, [128, 4, 128], BF16) for i in range(2)]
            qz = [sb("qz%d" % i, [128, 4, 128], BF16) for i in range(2)]
            vaug = [sb("mv%d" % i, [128, 4, 65], BF16) for i in range(2)]
            kwz = [sb("kwz%d" % i, [128, 2, 4, 64], BF16) for i in range(2)]
            sm = [sb("sm%d" % i, [128, 4, 128], BF16) for i in range(2)]
            sig = sb("sig", [128, 256], F32)
            gt = sb("gt", [128, 8, 4], F32)
            bl = sb("bl", [128, 2, 4], F32)
            den = sb("den", [128, 3, 4], F32)
            hv = sb("hv", [128, 4, 64], F32)
            hsq = sb("hsq", [128, 4, 64], F32)
            yb = [sb("yb%d" % i, [128, 256], BF16) for i in range(2)]
            yT = [sb("yT%d" % i, [128, 2, 128], BF16) for i in range(2)]
            ld = lambda dst, src, key: S.op("sp", lambda: nc.sync.dma_start(out=dst, in_=src), w=[key], dma=True)
            ld(tri[:], self.c_tri[:, :], ("tri",))
            ld(ob[:], self.c_ob[:, :, :], ("ob",))
            ld(mrs[:], self.c_mrs[:, :], ("mrs",))
            ld(gb[:], self.mlstm_gate_b[l].partition_broadcast(128), ("gb",))
            ld(ng[:], self.mlstm_norm[l].partition_broadcast(128), ("ng",))
            S.op("dve", lambda: nc.vector.tensor_copy(out=ident_b[:], in_=self.ident[:]), w=[("ident_b",)])
            S.op("dve", lambda: nc.vector.memset(Cst[:], 0.0), w=[("Cst", h) for h in range(4)])
            S.op("dve", lambda: nc.vector.memset(Cbf[:], 0.0), w=[("Cbf", h) for h in range(4)])
            for b in range(2):
                S.op("dve", lambda b=b: nc.vector.memset(vaug[b][:, :, 64:65], 1.0), w=[("mv1", b)])
                S.op("pool", lambda b=b: nc.gpsimd.memset(qz[b][:], 0.0), w=[("qz", b)])
                S.op("pool", lambda b=b: nc.gpsimd.memset(kwz[b][:], 0.0), w=[("kwz", b)])
                S.op("pool", lambda b=b: nc.gpsimd.memset(sm[b][:], 0.0), w=[("sm", b)])
            import os
            mstop = int(os.environ.get("K_MSTOP", "99"))

            def stage(n):
                if n > mstop:
                    raise StopIteration
            ck = 0
            try:
                for i in range(NT):
                    b = i % 2
                    m = mt[b]
                    S.op("sp", lambda m=m, i=i: nc.sync.dma_start(out=m[:], in_=self.mTM[i * 128:(i + 1) * 128, :]),
                         w=[("mt", b)], dma=True)
                    S.op("sp", lambda b=b, i=i: nc.sync.dma_start(
                        out=qk[b][:], in_=self.mqkT[:, :, i * 128:(i + 1) * 128].rearrange("c p t -> p c t")),
                        w=[("qk", b)], dma=True)
                    S.op("dve", lambda m=m: nc.vector.tensor_tensor(out=gt[:, 0, :], in0=m[:, 768:772], in1=gb[:, 0:4], op=ALU.add),
                         r=[("mt", b), ("gb",)], w=[("gt", 0)])
                    S.op("dve", lambda m=m: nc.vector.tensor_tensor(out=gt[:, 5, :], in0=m[:, 772:776], in1=gb[:, 4:8], op=ALU.add),
                         r=[("mt", b), ("gb",)], w=[("gt", 5)])
                    S.op("act", lambda: nc.scalar.activation(out=gt[:, 5, :], in_=gt[:, 5, :], func=AF.Exp, scale=-1.0),
                         r=[("gt", 5)], w=[("gt", 5)])
                    S.op("act", lambda: nc.scalar.activation(out=gt[:, 1, :], in_=gt[:, 5, :], func=AF.Ln, bias=self.one_t[:], scale=1.0),
                         r=[("gt", 5)], w=[("gt", 1)])
                    pg = self.ps[0]
                    S.op("pe", lambda: nc.tensor.matmul(pg[:, 0:4], tri[:], gt[:, 1, :], start=True, stop=True),
                         r=[("gt", 1), ("tri",)], w=[("ps", 0)])
                    S.op("pe", [lambda cc=cc: nc.tensor.matmul(
                        pg[:, 8 + 4 * cc:12 + 4 * cc], ob[:, cc, :], gt[:, 1, :], start=True, stop=True)
                        for cc in range(2)], r=[("gt", 1), ("ob",)], w=[("ps", 0, 1)])
                    for cc in range(2):
                        S.op("dve", lambda cc=cc: nc.vector.tensor_copy(
                            out=gt[64 * cc:64 * cc + 64, 2, :], in_=pg[64 * cc:64 * cc + 64, 8 + 4 * cc:12 + 4 * cc]),
                            r=[("ps", 0, 1)], w=[("gt", 2, cc)])
                    S.op("dve", lambda: nc.vector.tensor_tensor(out=gt[:, 5, :], in0=gt[:, 2, :], in1=pg[:, 0:4], op=ALU.subtract),
                         r=[("gt", 2, 0), ("gt", 2, 1), ("ps", 0)], w=[("gt", 5)])
                    S.op("dve", lambda: nc.vector.tensor_tensor(out=gt[:, 3, :], in0=gt[:, 0, :], in1=gt[:, 5, :], op=ALU.subtract),
                         r=[("gt", 0), ("gt", 5)], w=[("gt", 3)])
                    S.op("act", lambda: nc.scalar.activation(out=gt[:, 3, :], in_=gt[:, 3, :], func=AF.Exp),
                         r=[("gt", 3)], w=[("gt", 3)])
                    S.op("dve", lambda: nc.vector.tensor_scalar(out=gt[:, 3, :], in0=gt[:, 3, :], scalar1=0.125, scalar2=None,
                                                                op0=ALU.mult), r=[("gt", 3)], w=[("gt", 3)])
                    S.op("act", lambda: nc.scalar.activation(out=gt[:, 4, :], in_=gt[:, 5, :], func=AF.Exp, scale=-1.0),
                         r=[("gt", 5)], w=[("gt", 4)])
                    S.op("act", lambda: nc.scalar.activation(
                        out=bl[:, :, :], in_=pg[:, 8:16].rearrange("p (c h) -> p c h", c=2), func=AF.Exp, scale=-1.0),
                        r=[("ps", 0, 1)], w=[("bl",)])
                    stage(2)
                    S.op("act", lambda m=m: nc.scalar.activation(out=sig[:], in_=m[:, 512:768], func=AF.Sigmoid),
                         r=[("mt", b)], w=[("sig",)])
                    S.op("act", lambda m=m, b=b: nc.scalar.copy(
                        out=vaug[b][:, :, 0:64], in_=m[:, 256:512].rearrange("p (h x) -> p h x", x=64)),
                        r=[("mt", b)], w=[("mv", b)])
                    for hp in range(2):
                        S.op("pool", lambda b=b, hp=hp: nc.gpsimd.tensor_copy(
                            out=qz[b][64 * hp:64 * hp + 64, :, :].rearrange("p (c two) t -> p c two t", two=2)[:, :, hp, :],
                            in_=qk[b][64 * hp:64 * hp + 64, 0:2, :]), r=[("qk", b), ("qz", b)], w=[("qz", b, hp)])
                    for cc in range(2):
                        rows = slice(64 * cc, 64 * cc + 64)
                        for h in range(4):
                            S.op("pool", lambda m=m, b=b, h=h, cc=cc, rows=rows: nc.gpsimd.tensor_scalar(
                                out=kwz[b][rows, cc, h, :], in0=m[rows, h * 64:(h + 1) * 64], scalar1=gt[rows, 3, h:h + 1],
                                scalar2=1.0, op0=ALU.mult, op1=ALU.mult), r=[("mt", b), ("gt", 3), ("kwz", b)], w=[("kwz", b, cc, h)])
                    pho = self.ps[3 + b]
                    for cc in range(2):
                        rows = slice(64 * cc, 64 * cc + 64)
                        toks = slice(64 * cc, 64 * cc + 64)
                        pss = self.ps[1 + ck % 2]
                        psk = ("ps", 1 + ck % 2)
                        psu = self.ps[5 + ck % 2]
                        puk = ("ps", 5 + ck % 2)
                        ck += 1
                        stage(3)
                        for h in range(4):
                            hp = slice(64 * (h % 2), 64 * (h % 2) + 64)
                            hc = h // 2
                            S.op("dve", lambda h=h, hp=hp, hc=hc, cc=cc: nc.vector.tensor_scalar(
                                out=Cbf[hp, hc, :], in0=Cst[hp, hc, :], scalar1=bl[hp, cc, h:h + 1], scalar2=None,
                                op0=ALU.mult), r=[("Cst", h), ("bl",)], w=[("Cbf", h)])
                        stage(4)
                        S.op("pe", [lambda h=h, rows=rows, toks=toks, pss=pss, b=b: nc.tensor.matmul(
                            pss[rows, h * 64:(h + 1) * 64], qk[b][:, 2 + h // 2, toks], qz[b][:, h, toks],
                            start=True, stop=True) for h in range(4)],
                            r=[("qk", b), ("qz", b), ("qz", b, 0), ("qz", b, 1)], w=[psk])
                        for h in range(4):
                            S.op("dve", lambda h=h, rows=rows, toks=toks, pss=pss, b=b: nc.vector.scalar_tensor_tensor(
                                out=sm[b][rows, h, toks], in0=pss[rows, h * 64:(h + 1) * 64], scalar=gt[rows, 3, h:h + 1],
                                in1=mrs[rows, :], op0=ALU.mult, op1=ALU.mult),
                                r=[psk, ("gt", 3), ("mrs",), ("sm", b)], w=[("sm", b, cc, h)])
                        stage(5)
                        mm = []
                        for h in range(4):
                            mm.append(lambda h=h, rows=rows, toks=toks, b=b: nc.tensor.matmul(
                                pho[rows, h * 65:(h + 1) * 65], sm[b][:, h, toks], vaug[b][:, h, :],
                                start=True, stop=False))
                            mm.append(lambda h=h, rows=rows, toks=toks, b=b: nc.tensor.matmul(
                                pho[rows, h * 65:(h + 1) * 65], qz[b][:, h, toks], Cbf[:, h // 2, :],
                                start=False, stop=True))
                        S.op("pe", mm, r=[("sm", b, cc, h) for h in range(4)] + [("sm", b), ("mv", b), ("mv1", b), ("qz", b),
                                          ("qz", b, 0), ("qz", b, 1)] + [("Cbf", h) for h in range(4)],
                             w=[("ps", 3 + b, cc)])
                        stage(6)
                        S.op("pe", [lambda h=h, psu=psu, b=b, cc=cc: nc.tensor.matmul(
                            psu[64 * (h % 2):64 * (h % 2) + 64, (h // 2) * 65:(h // 2) * 65 + 65],
                            kwz[b][:, cc, h, :], vaug[b][:, h, :], start=True, stop=True) for h in range(4)],
                            r=[("kwz", b, cc, h) for h in range(4)] + [("kwz", b), ("mv", b), ("mv1", b)], w=[puk])
                        for h in range(4):
                            hp = slice(64 * (h % 2), 64 * (h % 2) + 64)
                            hc = h // 2
                            S.op("dve", lambda h=h, hp=hp, hc=hc, psu=psu, cc=cc: nc.vector.scalar_tensor_tensor(
                                out=Cst[hp, hc, :], in0=Cst[hp, hc, :], scalar=bl[hp, cc, h:h + 1],
                                in1=psu[hp, hc * 65:hc * 65 + 65], op0=ALU.mult, op1=ALU.add),
                                r=[("Cst", h), ("bl",), puk], w=[("Cst", h)])
                    stage(7)
                    pv = pho[:, 0:260].rearrange("p (h x) -> p h x", x=65)
                    S.op("act", lambda pv=pv: nc.scalar.activation(out=den[:, 0, :], in_=pv[:, :, 64], func=AF.Abs),
                         r=[("ps", 3 + b, 0), ("ps", 3 + b, 1)], w=[("den", 0)])
                    S.op("dve", lambda: nc.vector.tensor_tensor(out=den[:, 1, :], in0=den[:, 0, :], in1=gt[:, 4, :], op=ALU.max),
                         r=[("den", 0), ("gt", 4)], w=[("den", 1)])
                    S.op("dve", lambda: nc.vector.reciprocal(out=den[:, 1, :], in_=den[:, 1, :]), r=[("den", 1)], w=[("den", 1)])
                    S.op("dve", lambda pv=pv: nc.vector.tensor_tensor(
                        out=hv[:], in0=pv[:, :, 0:64], in1=den[:, 1, :].unsqueeze(2).broadcast_to([128, 4, 64]), op=ALU.mult),
                        r=[("ps", 3 + b, 0), ("ps", 3 + b, 1), ("den", 1)], w=[("hv",)])
                    S.op("pool", lambda: nc.gpsimd.tensor_tensor(
                        out=hv[:], in0=hv[:], in1=sig[:, :].rearrange("p (h x) -> p h x", x=64), op=ALU.mult),
                        r=[("hv",), ("sig",)], w=[("hv",)])
                    S.op("act", lambda: nc.scalar.activation(out=hsq[:], in_=hv[:], func=AF.Square), r=[("hv",)], w=[("hsq",)])
                    S.op("dve", lambda: nc.vector.tensor_reduce(out=den[:, 2, :], in_=hsq[:], axis=AX.X, op=ALU.add),
                         r=[("hsq",)], w=[("den", 2)])
                    S.op("act", lambda: nc.scalar.activation(out=den[:, 2, :], in_=den[:, 2, :], func=AF.Sqrt,
                                                             bias=self.eps_t[:], scale=1.0 / 64),
                         r=[("den", 2)], w=[("den", 2)])
                    S.op("dve", lambda: nc.vector.reciprocal(out=den[:, 2, :], in_=den[:, 2, :]), r=[("den", 2)], w=[("den", 2)])
                    S.op("dve", lambda: nc.vector.tensor_tensor(
                        out=hv[:], in0=hv[:], in1=den[:, 2, :].unsqueeze(2).broadcast_to([128, 4, 64]), op=ALU.mult),
                        r=[("hv",), ("den", 2)], w=[("hv",)])
                    S.op("pool", lambda b=b: nc.gpsimd.tensor_tensor(
                        out=yb[b][:], in0=hv[:, :, :].rearrange("p h x -> p (h x)"), in1=ng[:], op=ALU.mult),
                        r=[("hv",), ("ng",)], w=[("yb", b)])
                    pt = self.ps[7]
                    ptv = pt[:, 0:128].bitcast(BF16)
                    S.op("pe", [lambda c=c, b=b: nc.tensor.transpose(
                        ptv[:, c * 128:(c + 1) * 128], yb[b][:, c * 128:(c + 1) * 128], ident_b[:]) for c in range(2)],
                        r=[("yb", b), ("ident_b",)], w=[("ps", 7)])
                    S.op("act", lambda b=b: nc.scalar.copy(out=yT[b][:], in_=ptv[:, 0:256].rearrange("p (c t) -> p c t", c=2)),
                         r=[("ps", 7)], w=[("yT", b)])
                    S.op("sp", lambda b=b, i=i: nc.sync.dma_start(
                        out=self.ysT[2:4, :, i * 128:(i + 1) * 128].rearrange("c p t -> p c t"), in_=yT[b][:]),
                        r=[("yT", b)], w=[("d_ys", 2, i)], dma=True)
            except StopIteration:
                pass
            S.barrier()

    def _s5(self, l):
        nc, S = self.nc, self.S
        TC = 512
        TWO_PI = 2.0 * math.pi

        class T:
            def __init__(self, ap, key):
                self.ap, self.key = ap, key

            def __getitem__(self, idx):
                return T(self.ap[idx], self.key)
        E = {"dve": nc.vector, "pool": nc.gpsimd}

        def TT(eng, o, a, b, op):
            S.op(eng, lambda: E[eng].tensor_tensor(out=o.ap, in0=a.ap, in1=b.ap, op=op), r=[a.key, b.key], w=[o.key])

        def TS(o, a, s1, op0, s2=None, op1=None):
            kw = {} if op1 is None else {"op1": op1}
            sv = s1.ap if isinstance(s1, T) else s1
            rk = [a.key] + ([s1.key] if isinstance(s1, T) else [])
            S.op("dve", lambda: nc.vector.tensor_scalar(out=o.ap, in0=a.ap, scalar1=sv, scalar2=s2, op0=op0, **kw),
                 r=rk, w=[o.key])

        def STT(o, a, sc, b, op0, op1):
            sv = sc.ap if isinstance(sc, T) else sc
            rk = [a.key, b.key] + ([sc.key] if isinstance(sc, T) else [])
            S.op("dve", lambda: nc.vector.scalar_tensor_tensor(out=o.ap, in0=a.ap, scalar=sv, in1=b.ap, op0=op0, op1=op1),
                 r=rk, w=[o.key])

        def ACT(o, a, func, scale=1.0, bias=None):
            kw = {} if bias is None else {"bias": bias}
            S.op("act", lambda: nc.scalar.activation(out=o.ap, in_=a.ap, func=func, scale=scale, **kw), r=[a.key], w=[o.key])

        def CP(eng, o, a):
            if eng == "act":
                S.op("act", lambda: nc.scalar.copy(out=o.ap, in_=a.ap), r=[a.key], w=[o.key])
            else:
                S.op(eng, lambda: E[eng].tensor_copy(out=o.ap, in_=a.ap), r=[a.key], w=[o.key])

        def LD(o, src):
            S.op("sp", lambda: nc.sync.dma_start(out=o.ap, in_=src), w=[o.key], dma=True)

        with contextlib.ExitStack() as ph:
            def mk(name, shape, dt=F32):
                return T(self.sb(ph, "s5_" + name, shape, dt)[:], ("s5", name))
            cosT = mk("cosT", [128, 8, TC])
            sinT = mk("sinT", [128, 8, TC])
            rho = mk("rho", [128, 8])
            crot = mk("crot", [128, 8])
            srot = mk("srot", [128, 8])
            Wre = mk("Wre", [128, 8, 128], BF16)
            Wim = mk("Wim", [128, 8, 128], BF16)
            Cre = mk("Cre", [128, 8, 128], BF16)
            Cim = mk("Cim", [128, 8, 128], BF16)
            dg = mk("dg", [128, 2, 128], BF16)
            wg = mk("wg", [128, 2, 512], BF16)
            with contextlib.ExitStack() as pp:
                def mp(name, shape, dt=F32):
                    return T(self.sb(pp, "s5p_" + name, shape, dt)[:], ("s5p", name))
                are, aim, ls, dtt, th, thn = [mp(n, [128, 8]) for n in ("are", "aim", "ls", "dt", "th", "thn")]
                cth, sth, lre, lim, den, xr, zre, zim, t8a, t8b = [mp(n, [128, 8]) for n in
                                                                  ("cth", "sth", "lre", "lim", "den", "xr", "zre", "zim", "t8a", "t8b")]
                w8 = mp("w8", [128, 8])
                i8 = mp("i8", [128, 8], mybir.dt.int32)
                f8 = mp("f8", [128, 8])
                g8 = mp("g8", [128, 8])
                jrow = mp("jrow", [128, TC])
                Wb = mp("Wb", [128, 8 * TC])
                Ib = mp("Ib", [128, 8 * TC], mybir.dt.int32)
                Fb = mp("Fb", [128, 8 * TC])
                bre = mp("bre", [128, 8, 16])
                bim = mp("bim", [128, 8, 16])
                bbre = mp("bbre", [128, 8, 16])
                bbim = mp("bbim", [128, 8, 16])
                tb = mp("tb", [128, 8, 16])
                Ex = mp("Ex", [128, 128])
                dcol = mp("dcol", [128, 2])
                wg_st = mp("wg_st", [128, 2, 512])
                with nc.allow_non_contiguous_dma(reason="tiny S5 parameter loads"):
                    LD(are, self.s5_a_re[l].rearrange("(k q) p -> (q p) k", q=2))
                    LD(aim, self.s5_a_im[l].rearrange("(k q) p -> (q p) k", q=2))
                    for q in range(2):
                        S.op("sp", lambda q=q: nc.sync.dma_start(
                            out=ls.ap[64 * q:64 * q + 64, :],
                            in_=self.s5_log_step[l].rearrange("(k q) -> q k", q=2)[q].partition_broadcast(64)),
                            w=[("s5p", "ls", q)], dma=True)
                    LD(bre, self.s5_b_re[l].rearrange("(k q) p h -> (q p) k h", q=2))
                    LD(bim, self.s5_b_im[l].rearrange("(k q) p h -> (q p) k h", q=2))
                    LD(dcol, self.s5_d[l].rearrange("(c p) -> p c", p=128))
                LD(jrow, self.c_iota.partition_broadcast(128))
                LD(wg_st, self.s5_glu[l].rearrange("(c p) n -> p c n", p=128))
                CP("pool", wg, wg_st)
                ls.key2 = [("s5p", "ls", 0), ("s5p", "ls", 1)]
                S.op("act", lambda: nc.scalar.activation(out=dtt.ap, in_=ls.ap, func=AF.Exp),
                     r=ls.key2, w=[dtt.key])
                TT("dve", t8a, are, dtt, ALU.mult)
                ACT(rho, t8a, AF.Exp)
                TT("dve", th, aim, dtt, ALU.mult)
                TS(thn, th, 1.0 / TWO_PI, ALU.mult)

                def sin_turns(out, w, ib, fb, gb):
                    CP("dve", ib, w)
                    CP("dve", fb, ib)
                    TT("dve", fb, w, fb, ALU.subtract)
                    STT(gb, fb, 0.5, fb, ALU.is_gt, ALU.subtract)
                    STT(fb, gb, 0.5, gb, ALU.is_gt, ALU.subtract)
                    ACT(out, fb, AF.Sin, scale=TWO_PI * (1.0 - 1e-6))

                sin_turns(sth, thn, i8, f8, g8)
                TS(w8, thn, 0.25, ALU.add)
                sin_turns(cth, w8, i8, f8, g8)
                TS(w8, thn, float(TC), ALU.mult)
                sin_turns(srot, w8, i8, f8, g8)
                TS(w8, w8, 0.25, ALU.add)
                sin_turns(crot, w8, i8, f8, g8)
                for k in range(8):
                    TS(T(Wb.ap[:, k * TC:(k + 1) * TC], Wb.key), jrow, thn[:, k:k + 1], ALU.mult)
                sin_turns(T(sinT.ap[:, :, :].rearrange("p k t -> p (k t)"), sinT.key), Wb, Ib, Fb, T(cosT.ap[:, :, :].rearrange("p k t -> p (k t)"), cosT.key))
                TS(Wb, Wb, 0.25, ALU.add)
                Gb = mp("Gb", [128, 8 * TC])
                sin_turns(T(cosT.ap[:, :, :].rearrange("p k t -> p (k t)"), cosT.key), Wb, Ib, Fb, Gb)
                TT("dve", lre, rho, cth, ALU.mult)
                TT("dve", lim, rho, sth, ALU.mult)
                TT("dve", t8a, are, are, ALU.mult)
                TT("dve", t8b, aim, aim, ALU.mult)
                TT("dve", den, t8a, t8b, ALU.add)
                S.op("dve", lambda: nc.vector.reciprocal(out=den.ap, in_=den.ap), r=[den.key], w=[den.key])
                TS(xr, lre, -1.0, ALU.add)
                TT("dve", t8a, xr, are, ALU.mult)
                TT("dve", t8b, lim, aim, ALU.mult)
                TT("dve", t8a, t8a, t8b, ALU.add)
                TT("dve", zre, t8a, den, ALU.mult)
                TT("dve", t8a, lim, are, ALU.mult)
                TT("dve", t8b, xr, aim, ALU.mult)
                TT("dve", t8a, t8a, t8b, ALU.subtract)
                TT("dve", zim, t8a, den, ALU.mult)
                bc = lambda v: T(v.ap[:, :].unsqueeze(2).broadcast_to([128, 8, 16]), v.key)
                TT("dve", bbre, bre, bc(zre), ALU.mult)
                TT("dve", tb, bim, bc(zim), ALU.mult)
                TT("dve", bbre, bbre, tb, ALU.subtract)
                TT("dve", bbim, bim, bc(zre), ALU.mult)
                TT("dve", tb, bre, bc(zim), ALU.mult)
                TT("dve", bbim, bbim, tb, ALU.add)
                pcount = 0
                for (src, dst) in ((bbre, Wre), (bbim, Wim)):
                    for k in range(8):
                        S.op("dve", lambda: nc.vector.memset(Ex.ap, 0.0), w=[Ex.key])
                        for q in range(2):
                            gl = (2 * k + q) % 8
                            CP("dve", T(Ex.ap[64 * q:64 * q + 64, gl * 16:gl * 16 + 16], Ex.key),
                               T(src.ap[64 * q:64 * q + 64, k, :], src.key))
                        pst = self.ps[pcount % 4]
                        pkey = ("ps", pcount % 4)
                        pcount += 1
                        S.op("pe", lambda pst=pst: nc.tensor.transpose(pst[:, 0:128], Ex.ap, self.ident[:]),
                             r=[Ex.key], w=[pkey])
                        S.op("act", lambda pst=pst, dst=dst, k=k: nc.scalar.copy(out=dst.ap[:, k, :], in_=pst[:, 0:128]),
                             r=[pkey], w=[dst.key])
                for (srcd, dst, sgn, nm) in ((self.s5_c_re, Cre, 1.0, "r"), (self.s5_c_im, Cim, -1.0, "i")):
                    ExC = mp("ExC" + nm, [128, 8, 128])
                    S.op("pool", lambda ExC=ExC: nc.gpsimd.memset(ExC.ap, 0.0), w=[ExC.key])
                    for k in range(8):
                        for q in range(2):
                            gl = (2 * k + q) % 8
                            S.op("sp", lambda ExC=ExC, k=k, q=q, gl=gl, srcd=srcd: nc.sync.dma_start(
                                out=ExC.ap[gl * 16:gl * 16 + 16, k, 64 * q:64 * q + 64], in_=srcd[l, 2 * k + q, :, :]),
                                r=[ExC.key], w=[(ExC.key, k, q)], dma=True)
                    for k in range(8):
                        pst = self.ps[pcount % 4]
                        pkey = ("ps", pcount % 4)
                        pcount += 1
                        S.op("pe", lambda pst=pst, ExC=ExC, k=k: nc.tensor.transpose(pst[:, 0:128], ExC.ap[:, k, :], self.ident[:]),
                             r=[ExC.key, (ExC.key, k, 0), (ExC.key, k, 1)], w=[pkey])
                        S.op("act", lambda pst=pst, dst=dst, k=k, sgn=sgn: nc.scalar.activation(
                            out=dst.ap[:, k, :], in_=pst[:, 0:128], func=AF.Copy, scale=sgn), r=[pkey], w=[dst.key])
                for c in range(2):
                    TS(T(dg.ap[:, c, :], dg.key), T(self.ident[:], ("ident",)), dcol[:, c:c + 1], ALU.mult)
                S.barrier()
            with contextlib.ExitStack() as mn:
                def mm_(name, shape, dt=F32):
                    return [T(self.sb(mn, "s5m_%s%d" % (name, i), shape, dt)[:], ("s5m", name, i)) for i in range(2)]
                su = mm_("su", [128, 2, TC], BF16)
                t1, t2, t3, t4 = mm_("t1", [128, TC]), mm_("t2", [128, TC]), mm_("t3", [128, TC]), mm_("t4", [128, TC])
                btr, bti, sr, si = mm_("btr", [128, TC]), mm_("bti", [128, TC]), mm_("sr", [128, TC]), mm_("si", [128, TC])
                u1, u2 = mm_("u1", [128, TC]), mm_("u2", [128, TC])
                sre = mm_("sre", [128, 4, TC], BF16)
                sim_ = mm_("sim", [128, 4, TC], BF16)
                yT = mm_("yT", [128, 2, TC], BF16)
                sg = mm_("sg", [128, TC])
                oo = mm_("oo", [128, TC], BF16)
                init = T(self.sb(mn, "s5m_init", [128, 8, 2], F32)[:], ("s5m", "init"))
                i2 = T(self.sb(mn, "s5m_i2", [128, 8, 2], F32)[:], ("s5m", "i2"))
                S.op("dve", lambda: nc.vector.memset(init.ap, 0.0), w=[init.key])
                it = 0
                for n, (t0, tn) in enumerate(BLOCKS):
                    sb_ = su[n % 2]
                    S.op("sp", lambda sb_=sb_, t0=t0, tn=tn: nc.sync.dma_start(
                        out=sb_.ap[:, :, 0:tn], in_=self.suT[:, :, t0:t0 + tn].rearrange("c p t -> p c t")),
                        w=[sb_.key], dma=True)
                    for jc in range(2):
                        SR, SI = sre[jc], sim_[jc]
                        for kk in range(4):
                            k = 4 * jc + kk
                            b = it % 2
                            it += 1
                            pA, pB = self.ps[2 * b], self.ps[2 * b + 1]
                            kA, kB = ("ps", 2 * b), ("ps", 2 * b + 1)
                            S.op("pe", lambda pA=pA, k=k, jc=jc, sb_=sb_, tn=tn: nc.tensor.matmul(
                                pA[:, 0:tn], Wre.ap[:, k, :], sb_.ap[:, jc, 0:tn], start=True, stop=True),
                                r=[Wre.key, sb_.key], w=[kA])
                            S.op("pe", lambda pB=pB, k=k, jc=jc, sb_=sb_, tn=tn: nc.tensor.matmul(
                                pB[:, 0:tn], Wim.ap[:, k, :], sb_.ap[:, jc, 0:tn], start=True, stop=True),
                                r=[Wim.key, sb_.key], w=[kB])
                            A, Bp = T(pA[:, 0:tn], kA), T(pB[:, 0:tn], kB)
                            ck, sk = cosT[:, k, 0:tn], sinT[:, k, 0:tn]
                            w_ = lambda lst: lst[b][:, 0:tn]
                            TT("dve", w_(t1), A, ck, ALU.mult)
                            TT("dve", w_(t2), Bp, sk, ALU.mult)
                            TT("dve", w_(t3), Bp, ck, ALU.mult)
                            TT("dve", w_(t4), A, sk, ALU.mult)
                            TT("pool", w_(btr), w_(t1), w_(t2), ALU.add)
                            TT("pool", w_(bti), w_(t3), w_(t4), ALU.subtract)
                            for (bt, st, c01) in ((btr, sr, 0), (bti, si, 1)):
                                S.op("dve", lambda bt=bt, st=st, c01=c01, k=k, b=b, tn=tn: nc.vector.tensor_tensor_scan(
                                    out=st[b].ap[:, 0:tn], data0=rho.ap[:, k:k + 1].broadcast_to([128, tn]),
                                    data1=bt[b].ap[:, 0:tn], initial=init.ap[:, k, c01:c01 + 1],
                                    op0=ALU.mult, op1=ALU.add), r=[bt[b].key, init.key, rho.key], w=[st[b].key])
                            TT("pool", w_(u1), w_(sr), ck, ALU.mult)
                            TT("pool", w_(u2), w_(si), sk, ALU.mult)
                            TT("pool", T(SR.ap[:, kk, 0:tn], SR.key), w_(u1), w_(u2), ALU.subtract)
                            TT("pool", w_(u1), w_(sr), sk, ALU.mult)
                            TT("pool", w_(u2), w_(si), ck, ALU.mult)
                            TT("pool", T(SI.ap[:, kk, 0:tn], SI.key), w_(u1), w_(u2), ALU.add)
                            if tn == TC:
                                lr, li = sr[b][:, tn - 1:tn], si[b][:, tn - 1:tn]
                                TS(T(i2.ap[:, k, 0:1], i2.key), lr, crot[:, k:k + 1], ALU.mult)
                                TS(T(i2.ap[:, k, 1:2], i2.key), lr, srot[:, k:k + 1], ALU.mult)
                                TS(T(init.ap[:, k, 0:1], init.key), li, srot[:, k:k + 1], ALU.mult)
                                TT("dve", T(init.ap[:, k, 0:1], init.key), T(i2.ap[:, k, 0:1], i2.key),
                                   T(init.ap[:, k, 0:1], init.key), ALU.subtract)
                                STT(T(init.ap[:, k, 1:2], init.key), li, crot[:, k:k + 1], T(i2.ap[:, k, 1:2], i2.key),
                                    ALU.mult, ALU.add)
                        py = self.ps[4 + jc]
                        pyk = ("ps", 4 + jc)
                        mms = []
                        for kk in range(4):
                            k = 4 * jc + kk
                            mms.append(lambda k=k, kk=kk, tn=tn, py=py, SR=SR: nc.tensor.matmul(
                                py[:, 0:tn], Cre.ap[:, k, :], SR.ap[:, kk, 0:tn], start=(kk == 0), stop=False))
                            mms.append(lambda k=k, kk=kk, tn=tn, py=py, SI=SI: nc.tensor.matmul(
                                py[:, 0:tn], Cim.ap[:, k, :], SI.ap[:, kk, 0:tn], start=False, stop=False))
                        mms.append(lambda jc=jc, tn=tn, py=py, sb_=sb_: nc.tensor.matmul(
                            py[:, 0:tn], dg.ap[:, jc, :], sb_.ap[:, jc, 0:tn], start=False, stop=True))
                        S.op("pe", mms, r=[Cre.key, Cim.key, SR.key, SI.key, dg.key, sb_.key], w=[pyk])
                        yb_ = yT[n % 2]
                        S.op("act", lambda jc=jc, tn=tn, py=py, yb_=yb_: nc.scalar.copy(
                            out=yb_.ap[:, jc, 0:tn], in_=py[:, 0:tn]), r=[pyk], w=[(yb_.key, jc)])
                    yb_ = yT[n % 2]
                    for c in range(2):
                        pa, pg_ = self.ps[6], self.ps[7]
                        S.op("pe", [lambda jc=jc, c=c, tn=tn: nc.tensor.matmul(
                            pa[:, 0:tn], wg.ap[:, jc, c * 128:(c + 1) * 128], yb_.ap[:, jc, 0:tn],
                            start=(jc == 0), stop=(jc == 1)) for jc in range(2)],
                            r=[wg.key, (yb_.key, 0), (yb_.key, 1)], w=[("ps", 6)])
                        S.op("pe", [lambda jc=jc, c=c, tn=tn: nc.tensor.matmul(
                            pg_[:, 0:tn], wg.ap[:, jc, 256 + c * 128:256 + (c + 1) * 128], yb_.ap[:, jc, 0:tn],
                            start=(jc == 0), stop=(jc == 1)) for jc in range(2)],
                            r=[wg.key, (yb_.key, 0), (yb_.key, 1)], w=[("ps", 7)])
                        sgb, oob = sg[c], oo[c]
                        S.op("act", lambda tn=tn, sgb=sgb: nc.scalar.activation(
                            out=sgb.ap[:, 0:tn], in_=pg_[:, 0:tn], func=AF.Sigmoid), r=[("ps", 7)], w=[sgb.key])
                        S.op("dve", lambda tn=tn, sgb=sgb, oob=oob: nc.vector.tensor_tensor(
                            out=oob.ap[:, 0:tn], in0=pa[:, 0:tn], in1=sgb.ap[:, 0:tn], op=ALU.mult),
                            r=[("ps", 6), sgb.key], w=[oob.key])
                        S.op("sp", lambda c=c, t0=t0, tn=tn, oob=oob: nc.sync.dma_start(
                            out=self.ysT[4 + c, :, t0:t0 + tn], in_=oob.ap[:, 0:tn]), r=[oob.key], w=[("d_ys", 4 + c, n)],
                            dma=True)
                S.barrier()

    def _mla(self, l):
        nc, S = self.nc, self.S
        SCALE = 96.0 ** -0.5
        with contextlib.ExitStack() as ph:
            sb = lambda name, shape, dt: self.sb(ph, name, shape, dt)
            cqn = sb("cqn", [128, 3, L], BF16)
            ckvn = sb("ckvn", [128, 2, L], BF16)
            krr = sb("krr", [96, L], BF16)
            vaug = sb("vaug", [128, NT, 8, 65], BF16)
            otm = sb("otm_a", [128, NT, 512], BF16)
            qg = sb("qg", [128, 3], F32)
            kvg = sb("kvg", [128, 2], F32)
            psw = sb("psw", [96, 32], F32)
            msk = sb("msk", [128, 2, 128], BF16)
            ident_b = sb("ident_b", [128, 128], BF16)
            wq_st = sb("wq_st", [128, 3, 768], F32)
            wq = sb("wq", [128, 3, 8, 96], BF16)
            wqs = sb("wqs", [128, 3, 8, 96], BF16)
            wkv_st = sb("wkv_st", [128, 2, 1024], F32)
            wkv = sb("wkv", [128, 2, 1024], BF16)
            with nc.allow_non_contiguous_dma(reason="tiny norm gain loads"):
                S.op("sp", lambda: nc.sync.dma_start(out=qg[:], in_=self.mla_q_norm[l].rearrange("(c p) -> p c", p=128)),
                     w=[("qg",)], dma=True)
                S.op("sp", lambda: nc.sync.dma_start(out=kvg[:], in_=self.mla_kv_norm[l].rearrange("(c p) -> p c", p=128)),
                     w=[("kvg",)], dma=True)
            S.op("sp", lambda: nc.sync.dma_start(out=psw[64:96, :], in_=self.c_psw[:, :]), w=[("psw",)], dma=True)
            S.op("sp", lambda: nc.sync.dma_start(out=msk[:], in_=self.c_mask.rearrange("m p q -> p m q")),
                 w=[("msk",)], dma=True)
            S.op("dve", lambda: nc.vector.tensor_copy(out=ident_b[:], in_=self.ident[:]), w=[("ident_b",)])
            S.op("sp", lambda: nc.sync.dma_start(out=wq_st[:], in_=self.mla_w_uq[l].rearrange("(c p) n -> p c n", p=128)),
                 w=[("wq_st",)], dma=True)
            S.op("sp", lambda: nc.sync.dma_start(out=wkv_st[:], in_=self.mla_w_ukv[l].rearrange("(c p) n -> p c n", p=128)),
                 w=[("wkv_st",)], dma=True)
            S.op("pool", lambda: nc.gpsimd.tensor_copy(out=wkv[:], in_=wkv_st[:]), r=[("wkv_st",)], w=[("wkv",)])
            wq_v = wq_st[:, :, :].rearrange("p c (h x) -> p c h x", x=96)
            S.op("pool", lambda: nc.gpsimd.tensor_copy(out=wq[:], in_=wq_v), r=[("wq_st",)], w=[("wq",)])
            S.op("pool", lambda: nc.gpsimd.memset(wqs[:], 0.0), w=[("wqs",)])
            S.op("pool", lambda: nc.gpsimd.tensor_scalar(out=wqs[:, :, :, 64:80], in0=wq_v[:, :, :, 80:96],
                                                         scalar1=-1.0, scalar2=1.0, op0=ALU.mult, op1=ALU.mult),
                 r=[("wq_st",)], w=[("wqs",)])
            S.op("pool", lambda: nc.gpsimd.tensor_copy(out=wqs[:, :, :, 80:96], in_=wq_v[:, :, :, 64:80]),
                 r=[("wq_st",)], w=[("wqs",)])
            S.op("dve", lambda: nc.vector.memset(vaug[:, :, :, 64:65], 1.0), w=[("vaug1",)])

            with contextlib.ExitStack() as st1:
                lat = [self.sb(st1, "lat%d" % i, [128, 5, 512], F32) for i in range(2)]
                sq = self.sb(st1, "sq_a", [128, 5, 512], F32)
                rs = [self.sb(st1, "rs%d" % i, [128, 512], F32) for i in range(2)]
                krb = [self.sb(st1, "krb%d" % i, [96, 512], F32) for i in range(2)]
                cs = [self.sb(st1, "cs%d" % i, [96, 2, 512], F32) for i in range(2)]
                t1 = self.sb(st1, "t1", [96, 512], F32)
                t2 = self.sb(st1, "t2", [96, 512], F32)
                for n, (t0, tn) in enumerate(BLOCKS):
                    b = n % 2
                    S.op("sp", lambda b=b, t0=t0, tn=tn: nc.sync.dma_start(
                        out=lat[b][:, 0:3, 0:tn], in_=self.cqT[:, :, t0:t0 + tn].rearrange("c p t -> p c t")),
                        w=[("lat", b, 0)], dma=True)
                    S.op("sp", lambda b=b, t0=t0, tn=tn: nc.sync.dma_start(
                        out=lat[b][:, 3:5, 0:tn], in_=self.ckvT[:, :, t0:t0 + tn].rearrange("c p t -> p c t")),
                        w=[("lat", b, 1)], dma=True)
                    S.op("sp", lambda b=b, t0=t0, tn=tn: nc.sync.dma_start(
                        out=krb[b][64:96, 0:tn], in_=self.krT[:, t0:t0 + tn]), w=[("krb", b)], dma=True)
                    S.op("sp", lambda b=b, t0=t0, tn=tn: nc.sync.dma_start(
                        out=cs[b][64:96, 0, 0:tn], in_=self.c_cos[:, t0:t0 + tn]), w=[("cs", b, 0)], dma=True)
                    S.op("sp", lambda b=b, t0=t0, tn=tn: nc.sync.dma_start(
                        out=cs[b][64:96, 1, 0:tn], in_=self.c_sin[:, t0:t0 + tn]), w=[("cs", b, 1)], dma=True)
                    self.rms_rstd(lat[b][:, 0:3, :], 3, 384, tn, sq[:, 0:3, :], rs[0], 6, [("lat", b, 0)], "q")
                    for c in range(3):
                        S.op("dve", lambda b=b, c=c, t0=t0, tn=tn: nc.vector.scalar_tensor_tensor(
                            out=cqn[:, c, t0:t0 + tn], in0=lat[b][:, c, 0:tn], scalar=qg[:, c:c + 1],
                            in1=rs[0][:, 0:tn], op0=ALU.mult, op1=ALU.mult),
                            r=[("lat", b, 0), ("rstd", "q"), ("qg",)], w=[("cqn", n)])
                    self.rms_rstd(lat[b][:, 3:5, :], 2, 256, tn, sq[:, 3:5, :], rs[1], 7, [("lat", b, 1)], "kv")
                    for c in range(2):
                        S.op("dve", lambda b=b, c=c, t0=t0, tn=tn: nc.vector.scalar_tensor_tensor(
                            out=ckvn[:, c, t0:t0 + tn], in0=lat[b][:, 3 + c, 0:tn], scalar=kvg[:, c:c + 1],
                            in1=rs[1][:, 0:tn], op0=ALU.mult, op1=ALU.mult),
                            r=[("lat", b, 1), ("rstd", "kv"), ("kvg",)], w=[("ckvn", n)])
                    pst = self.ps[4 + b]
                    S.op("pe", lambda b=b, tn=tn, pst=pst: nc.tensor.matmul(
                        pst[64:96, 0:tn], psw[64:96, :], krb[b][64:96, 0:tn], start=True, stop=True),
                        r=[("krb", b), ("psw",)], w=[("ps", 4 + b)])
                    S.op("dve", lambda b=b, tn=tn, pst=pst: nc.vector.tensor_tensor(
                        out=t1[64:96, 0:tn], in0=pst[64:96, 0:tn], in1=cs[b][64:96, 1, 0:tn], op=ALU.mult),
                        r=[("ps", 4 + b), ("cs", b, 1)], w=[("t1",)])
                    S.op("pool", lambda b=b, tn=tn: nc.gpsimd.tensor_tensor(
                        out=t2[64:96, 0:tn], in0=krb[b][64:96, 0:tn], in1=cs[b][64:96, 0, 0:tn], op=ALU.mult),
                        r=[("krb", b), ("cs", b, 0)], w=[("t2",)])
                    S.op("dve", lambda t0=t0, tn=tn: nc.vector.tensor_tensor(
                        out=krr[64:96, t0:t0 + tn], in0=t1[64:96, 0:tn], in1=t2[64:96, 0:tn], op=ALU.add),
                        r=[("t1",), ("t2",)], w=[("krr", n)])
                    for i in range(t0 // 128, (t0 + tn) // 128):
                        pi = i % 4
                        pst = self.ps[pi]
                        S.op("pe", [lambda c=c, i=i, pst=pst: nc.tensor.matmul(
                            pst[:, 0:512], ckvn[:, c, i * 128:(i + 1) * 128],
                            wkv[:, c, :].rearrange("p (h x) -> p h x", x=128)[:, :, 64:128],
                            start=(c == 0), stop=(c == 1)) for c in range(2)],
                            r=[("ckvn", n), ("wkv",)], w=[("ps", pi)])
                        S.op("act", lambda i=i, pst=pst: nc.scalar.copy(
                            out=vaug[:, i, :, 0:64], in_=pst[:, 0:512].rearrange("p (h x) -> p h x", x=64)),
                            r=[("ps", pi)], w=[("vaug", i)])
                S.barrier()

            with contextlib.ExitStack() as st2:
                QT = [self.sb(st2, "QT%d" % i, [96, L], BF16) for i in range(2)]
                KT = [self.sb(st2, "KT%d" % i, [96, L], BF16) for i in range(2)]
                pT = [self.sb(st2, "pT%d" % i, [128, 512], BF16) for i in range(4)]
                cs = [self.sb(st2, "cs2_%d" % i, [96, 2, 512], F32) for i in range(2)]
                t1 = self.sb(st2, "t1b", [96, 512], F32)
                rec = self.sb(st2, "rec", [128, 8], F32)
                pcnt = 0
                rcnt = 0
                for h in range(8):
                    hb = h % 2
                    for n, (t0, tn) in enumerate(BLOCKS):
                        b = n % 2
                        S.op("sp", lambda b=b, t0=t0, tn=tn: nc.sync.dma_start(
                            out=cs[b][64:96, 0, 0:tn], in_=self.c_cos[:, t0:t0 + tn]), w=[("cs", b, 0)], dma=True)
                        S.op("sp", lambda b=b, t0=t0, tn=tn: nc.sync.dma_start(
                            out=cs[b][64:96, 1, 0:tn], in_=self.c_sin[:, t0:t0 + tn]), w=[("cs", b, 1)], dma=True)
                        pq, pqs, pk = self.ps[6], self.ps[7], self.ps[4 + b]
                        S.op("pe", [lambda c=c, t0=t0, tn=tn: nc.tensor.matmul(
                            pq[0:96, 0:tn], wq[:, c, h, :], cqn[:, c, t0:t0 + tn], start=(c == 0), stop=(c == 2))
                            for c in range(3)], r=[("wq",)], w=[("ps", 6)])
                        S.op("pe", [lambda c=c, t0=t0, tn=tn: nc.tensor.matmul(
                            pqs[0:96, 0:tn], wqs[:, c, h, :], cqn[:, c, t0:t0 + tn], start=(c == 0), stop=(c == 2))
                            for c in range(3)], r=[("wqs",)], w=[("ps", 7)])
                        S.op("pe", [lambda c=c, t0=t0, tn=tn: nc.tensor.matmul(
                            pk[0:64, 0:tn], wkv[:, c, h * 128:h * 128 + 64], ckvn[:, c, t0:t0 + tn],
                            start=(c == 0), stop=(c == 1)) for c in range(2)], r=[("wkv",)], w=[("ps", 4 + b)])
                        S.op("act", lambda t0=t0, tn=tn: nc.scalar.copy(
                            out=QT[hb][0:64, t0:t0 + tn], in_=pq[0:64, 0:tn]), r=[("ps", 6)], w=[("QT", hb, n)])
                        S.op("dve", lambda b=b, tn=tn: nc.vector.tensor_tensor(
                            out=t1[64:96, 0:tn], in0=pqs[64:96, 0:tn], in1=cs[b][64:96, 1, 0:tn], op=ALU.mult),
                            r=[("ps", 7), ("cs", b, 1)], w=[("t1",)])
                        S.op("dve", lambda b=b, tn=tn: nc.vector.tensor_tensor(
                            out=cs[b][64:96, 0, 0:tn], in0=pq[64:96, 0:tn], in1=cs[b][64:96, 0, 0:tn], op=ALU.mult),
                            r=[("ps", 6), ("cs", b, 0)], w=[("cs", b, 0)])
                        S.op("dve", lambda b=b, t0=t0, tn=tn: nc.vector.tensor_tensor(
                            out=QT[hb][64:96, t0:t0 + tn], in0=t1[64:96, 0:tn], in1=cs[b][64:96, 0, 0:tn], op=ALU.add),
                            r=[("t1",), ("cs", b, 0)], w=[("QT", hb, n, 1)])
                        S.op("act", lambda b=b, t0=t0, tn=tn: nc.scalar.copy(
                            out=KT[hb][0:64, t0:t0 + tn], in_=pk[0:64, 0:tn]), r=[("ps", 4 + b)], w=[("KT", hb, n)])
                        S.op("pool", lambda t0=t0, tn=tn: nc.gpsimd.tensor_copy(
                            out=KT[hb][64:96, t0:t0 + tn], in_=krr[64:96, t0:t0 + tn]), w=[("KT", hb, n, 1)])
                    for n, (t0, tn) in enumerate(BLOCKS):
                        qts = list(range(t0 // 128, (t0 + tn) // 128))
                        kt_max = min(qts[-1] + 1, NT - 1)
                        for kt in range(kt_max + 1):
                            pi = pcnt % 2
                            pb = pT[pcnt % 4]
                            pkey = ("pT", pcnt % 4)
                            pcnt += 1
                            pss = self.ps[4 + pi]
                            n_k = kt // 4
                            S.op("pe", lambda kt=kt, t0=t0, tn=tn, pss=pss: nc.tensor.matmul(
                                pss[:, 0:tn], KT[hb][:, kt * 128:(kt + 1) * 128], QT[hb][:, t0:t0 + tn],
                                start=True, stop=True),
                                r=[("KT", hb, n_k), ("KT", hb, n_k, 1), ("QT", hb, n), ("QT", hb, n, 1)],
                                w=[("ps", 4 + pi)])
                            S.op("act", lambda tn=tn, pss=pss, pb=pb: nc.scalar.activation(
                                out=pb[:, 0:tn], in_=pss[:, 0:tn], func=AF.Exp, scale=SCALE),
                                r=[("ps", 4 + pi)], w=[pkey])
                            for j, qt in enumerate(qts):
                                if kt == qt or kt == qt + 1:
                                    m = 0 if kt == qt else 1
                                    S.op("pool", lambda j=j, m=m, pb=pb: nc.gpsimd.tensor_tensor(
                                        out=pb[:, j * 128:(j + 1) * 128], in0=pb[:, j * 128:(j + 1) * 128],
                                        in1=msk[:, m, :], op=ALU.mult), r=[pkey, ("msk",)], w=[pkey])
                            for j, qt in enumerate(qts):
                                if kt > qt + 1:
                                    continue
                                last = (kt == min(qt + 1, NT - 1))
                                S.op("pe", lambda j=j, kt=kt, pb=pb, last=last: nc.tensor.matmul(
                                    self.ps[j][:, 0:65], pb[:, j * 128:(j + 1) * 128], vaug[:, kt, h, :],
                                    start=(kt == 0), stop=last),
                                    r=[pkey, ("vaug", kt), ("vaug1",)], w=[("ps", j)])
                                if last:
                                    rc = rec[:, rcnt % 8:rcnt % 8 + 1]
                                    rkey = ("rec", rcnt % 8)
                                    rcnt += 1
                                    S.op("dve", lambda j=j, rc=rc: nc.vector.reciprocal(out=rc, in_=self.ps[j][:, 64:65]),
                                         r=[("ps", j)], w=[rkey])
                                    S.op("dve", lambda j=j, rc=rc, qt=qt: nc.vector.tensor_scalar(
                                        out=otm[:, qt, h * 64:(h + 1) * 64], in0=self.ps[j][:, 0:64], scalar1=rc,
                                        scalar2=None, op0=ALU.mult), r=[("ps", j), rkey], w=[("otm", qt, h)])
                ob = [self.sb(st2, "oT%d" % i, [128, 4, 128], BF16) for i in range(2)]
                for i in range(NT):
                    pst = self.ps[4 + i % 2]
                    pv = pst[:, 0:256].bitcast(BF16) if hasattr(pst[:, 0:256], "bitcast") else None
                    S.op("pe", [lambda c=c, i=i, pv=pv: nc.tensor.transpose(
                        pv[:, c * 128:(c + 1) * 128], otm[:, i, c * 128:(c + 1) * 128], ident_b[:]) for c in range(4)],
                        r=[("otm", i, hh) for hh in range(8)] + [("ident_b",)], w=[("ps", 4 + i % 2)])
                    S.op("act", lambda i=i, pv=pv: nc.scalar.copy(
                        out=ob[i % 2][:, :, :], in_=pv[:, 0:512].rearrange("p (c t) -> p c t", c=4)),
                        r=[("ps", 4 + i % 2)], w=[("oT", i % 2)])
                    S.op("sp", lambda i=i: nc.sync.dma_start(
                        out=self.ysT[6:10, :, i * 128:(i + 1) * 128].rearrange("c p t -> p c t"), in_=ob[i % 2][:]),
                        r=[("oT", i % 2)], w=[("d_ys", 6, i)], dma=True)
                S.barrier()

    def _fm_tiles(self):
        def dest(col):
            if col < C_CB:
                return (self.gT, col // 128, "sig")
            if col < C_MQ:
                return (self.convT, (col - C_CB) // 128, "f32")
            if col < C_MV:
                return (self.mqkT, (col - C_MQ) // 128, "bf16")
            if col < C_CQ:
                return (self.suT, (col - C_SU) // 128, "bf16")
            if col < C_CKV:
                return (self.cqT, (col - C_CQ) // 128, "f32")
            if col < C_KR:
                return (self.ckvT, (col - C_CKV) // 128, "f32")
            return (self.krT, 0, "f32")
        groups = []
        for (a, b) in [(0, C_MV), (C_SU, IN_W)]:
            c0 = a
            while c0 < b:
                n = min(512, b - c0)
                tiles = []
                c = c0
                while c < c0 + n:
                    wdt = min(128, c0 + n - c)
                    tiles.append((c, wdt) + dest(c))
                    c += wdt
                groups.append((c0, n, tiles))
                c0 += n
        return groups

    def _inproj(self, l):
        nc, S = self.nc, self.S
        with contextlib.ExitStack() as ph:
            hn = self.sb(ph, "hn", [128, 8, L], BF16)
            hb = [self.sb(ph, "hb%d" % i, [128, 8, 512], F32) for i in range(2)]
            sq = [self.sb(ph, "sq0", [128, 8, 512], F32)] * 2
            rstd = [self.sb(ph, "rstd%d" % i, [128, 512], F32) for i in range(2)]
            wst = [self.sb(ph, "wst%d" % i, [128, 8, 512], F32) for i in range(2)]
            wbf = [self.sb(ph, "wbf%d" % i, [128, 8, 512], BF16) for i in range(2)]
            wtm = self.sb(ph, "wtm", [128, 8, TM_W], BF16)
            ob16 = [self.sb(ph, "ob16_%d" % i, [128, 512], BF16) for i in range(4)]
            of32 = [self.sb(ph, "of32_%d" % i, [128, 512], F32) for i in range(4)]
            otm = [self.sb(ph, "otm%d" % i, [128, TM_W], F32) for i in range(2)]
            w_l = self.w_in[l].rearrange("(c p) n -> p c n", p=128)
            groups = self._fm_tiles()

            def load_w(gi):
                c0, n, _ = groups[gi]
                b = gi % 2
                S.op("sp", lambda: nc.sync.dma_start(out=wst[b][:, :, 0:n], in_=w_l[:, :, c0:c0 + n]),
                     w=[("wst", b)], dma=True)
                S.op("pool", lambda: nc.gpsimd.tensor_copy(out=wbf[b][:, :, 0:n], in_=wst[b][:, :, 0:n]),
                     r=[("wst", b)], w=[("wbf", b)])

            for b, (ca, cw) in enumerate([(0, 512), (512, TM_W - 512)]):
                S.op("sp", lambda b=b, ca=ca, cw=cw: nc.sync.dma_start(
                    out=wst[b][:, :, 0:cw], in_=w_l[:, :, TM_C0 + ca:TM_C0 + ca + cw]), w=[("wst", b)], dma=True)
                S.op("pool", lambda b=b, ca=ca, cw=cw: nc.gpsimd.tensor_copy(
                    out=wtm[:, :, ca:ca + cw], in_=wst[b][:, :, 0:cw]), r=[("wst", b)], w=[("wtm", b)])
            load_w(0)

            for n, (t0, tn) in enumerate(BLOCKS):
                b = n % 2
                S.op("sp", lambda b=b, t0=t0, tn=tn: nc.sync.dma_start(
                    out=hb[b][:, :, 0:tn], in_=self.hT[:, :, t0:t0 + tn].rearrange("c p t -> p c t")),
                    r=[("d_hT", n)], w=[("hb", b)], dma=True)
                S.op("act", lambda b=b, tn=tn: nc.scalar.activation(
                    out=sq[b][:, :, 0:tn], in_=hb[b][:, :, 0:tn], func=AF.Square), r=[("hb", b)], w=[("sq", 0)])
                pst = self.ps[6 + b]
                S.op("pe", [lambda c=c, b=b, tn=tn, pst=pst: nc.tensor.matmul(
                    pst[:, 0:tn], self.ones_f[:], sq[b][:, c, 0:tn], start=(c == 0), stop=(c == 7))
                    for c in range(8)], r=[("sq", 0)], w=[("ps", 6 + b)])
                S.op("act", lambda b=b, tn=tn, pst=pst: nc.scalar.activation(
                    out=rstd[b][:, 0:tn], in_=pst[:, 0:tn], func=AF.Sqrt, bias=self.eps_t[:], scale=1.0 / D),
                    r=[("ps", 6 + b)], w=[("rstd", b)])
                S.op("dve", lambda b=b, tn=tn: nc.vector.reciprocal(out=rstd[b][:, 0:tn], in_=rstd[b][:, 0:tn]),
                     r=[("rstd", b)], w=[("rstd", b)])
                for c in range(8):
                    S.op("dve", lambda b=b, c=c, t0=t0, tn=tn: nc.vector.scalar_tensor_tensor(
                        out=hn[:, c, t0:t0 + tn], in0=hb[b][:, c, 0:tn], scalar=self.gain(l, 0, c),
                        in1=rstd[b][:, 0:tn], op0=ALU.mult, op1=ALU.mult),
                        r=[("hb", b), ("rstd", b)], w=[("hn", n)])

            pcount = 0
            for i in range(NT):
                ob = otm[i % 2]
                for (ca, cw) in [(0, 512), (512, TM_W - 512)]:
                    pi = pcount % 6
                    pcount += 1
                    pst = self.ps[pi]
                    S.op("pe", [lambda c=c, i=i, ca=ca, cw=cw, pst=pst: nc.tensor.matmul(
                        pst[:, 0:cw], hn[:, c, i * 128:(i + 1) * 128], wtm[:, c, ca:ca + cw],
                        start=(c == 0), stop=(c == 7)) for c in range(8)],
                        r=[("hn", i // 4), ("wtm", 0), ("wtm", 1)], w=[("ps", pi)])
                    if ca == 0:
                        S.op("act", lambda ob=ob, pst=pst, ca=ca, cw=cw: nc.scalar.copy(
                            out=ob[:, ca:ca + cw], in_=pst[:, 0:cw]), r=[("ps", pi)], w=[("otm", i % 2, 0)])
                    else:
                        S.op("dve", lambda ob=ob, pst=pst, ca=ca, cw=cw: nc.vector.tensor_copy(
                            out=ob[:, ca:ca + cw], in_=pst[:, 0:cw]), r=[("ps", pi)], w=[("otm", i % 2, 1)])
                S.op("sp", lambda ob=ob, i=i: nc.sync.dma_start(out=self.mTM[i * 128:(i + 1) * 128, :], in_=ob[:]),
                     r=[("otm", i % 2, 0), ("otm", i % 2, 1)], w=[("d_mTM", i)], dma=True)

            o16c = 0
            o32c = 0
            for gi, (c0, ncols, tiles) in enumerate(groups):
                if gi + 1 < len(groups):
                    load_w(gi + 1)
                wb = wbf[gi % 2]
                for (col, cw, dst, chunk, kind) in tiles:
                    off = col - c0
                    for n, (t0, tn) in enumerate(BLOCKS):
                        pi = pcount % 6
                        pcount += 1
                        pst = self.ps[pi]
                        S.op("pe", [lambda c=c, off=off, cw=cw, t0=t0, tn=tn, pst=pst, wb=wb: nc.tensor.matmul(
                            pst[0:cw, 0:tn], wb[:, c, off:off + cw], hn[:, c, t0:t0 + tn],
                            start=(c == 0), stop=(c == 7)) for c in range(8)],
                            r=[("hn", n), ("wbf", gi % 2)], w=[("ps", pi)])
                        if kind == "f32":
                            bi = o32c % 4
                            o32c += 1
                            ob = of32[bi]
                            okey = ("of32", bi)
                            S.op("dve", lambda ob=ob, pst=pst, cw=cw, tn=tn: nc.vector.tensor_copy(
                                out=ob[0:cw, 0:tn], in_=pst[0:cw, 0:tn]), r=[("ps", pi)], w=[okey])
                        else:
                            bi = o16c % 4
                            o16c += 1
                            ob = ob16[bi]
                            okey = ("ob16", bi)
                            fn = AF.Sigmoid if kind == "sig" else AF.Copy
                            S.op("act", lambda ob=ob, pst=pst, cw=cw, tn=tn, fn=fn: nc.scalar.activation(
                                out=ob[0:cw, 0:tn], in_=pst[0:cw, 0:tn], func=fn), r=[("ps", pi)], w=[okey])
                        if dst is self.krT:
                            dap = dst[:, t0:t0 + tn]
                        else:
                            dap = dst[chunk, :, t0:t0 + tn]
                        S.op("sp", lambda ob=ob, dap=dap, cw=cw, tn=tn: nc.sync.dma_start(
                            out=dap, in_=ob[0:cw, 0:tn]), r=[okey], w=[("d_fm", col, n)], dma=True)
            S.barrier()

    def _merge(self, l):
        nc, S = self.nc, self.S
        with contextlib.ExitStack() as ph:
            sb = lambda name, shape, dt: self.sb(ph, name, shape, dt)
            wbr = sb("wbr", [128, 10, D], BF16)
            wo = sb("wo", [128, 8, D], BF16)
            wst = [sb("mwst%d" % i, [128, D], F32) for i in range(2)]
            ys = [sb("mys%d" % i, [128, 10, 512], BF16) for i in range(2)]
            gt = [sb("mgt%d" % i, [128, 4, 512], BF16) for i in range(3)]
            tb = [sb("mtb%d" % i, [128, 512], F32) for i in range(4)]
            mg = sb("mmg", [128, 8, 512], BF16)
            hb = [sb("mhb%d" % i, [128, 8, 512], F32) for i in range(2)]
            o2 = sb("mo2", [128, 8, 512], F32)
            sq = sb("msq", [128, 8, 512], F32)
            rs = sb("mrs_", [128, 512], F32)
            hn = [sb("mhn%d" % i, [128, 8, 512], BF16) for i in range(2)]
            wc = 0
            for (src, dst, nchunk) in ((self.w_branch[l], wbr, 10), (self.w_out[l], wo, 8)):
                for c in range(nchunk):
                    b = wc % 2
                    wc += 1
                    S.op("sp", lambda b=b, c=c, src=src: nc.sync.dma_start(out=wst[b][:], in_=src[c * 128:(c + 1) * 128, :]),
                         w=[("wst", b)], dma=True)
                    S.op("pool", lambda b=b, c=c, dst=dst: nc.gpsimd.tensor_copy(out=dst[:, c, :], in_=wst[b][:]),
                         r=[("wst", b)], w=[("w", id(dst), c)])
            wbr_keys = [("w", id(wbr), c) for c in range(10)]
            wo_keys = [("w", id(wo), c) for c in range(8)]
            branch_chunks = [(0, 2), (2, 4), (4, 6), (6, 10)]
            gcount = 0
            pcount = 0
            for n, (t0, tn) in enumerate(BLOCKS):
                b = n % 2
                S.op("sp", lambda b=b, t0=t0, tn=tn: nc.sync.dma_start(
                    out=ys[b][:, :, 0:tn], in_=self.ysT[:, :, t0:t0 + tn].rearrange("c p t -> p c t")),
                    w=[("ys", b)], dma=True)
                S.op("sp", lambda b=b, t0=t0, tn=tn: nc.sync.dma_start(
                    out=hb[b][:, :, 0:tn], in_=self.hT[:, :, t0:t0 + tn].rearrange("c p t -> p c t")),
                    w=[("hb", b, c) for c in range(8)], dma=True)
                for dc in range(8):
                    gi = gcount % 3
                    gcount += 1
                    gtb = gt[gi]
                    S.op("sp", lambda gtb=gtb, dc=dc, t0=t0, tn=tn: nc.sync.dma_start(
                        out=gtb[:, :, 0:tn], in_=self.gT[dc:32:8, :, t0:t0 + tn].rearrange("c p t -> p c t")),
                        w=[("gt", gi)], dma=True)
                    for bi, (ca, cb_) in enumerate(branch_chunks):
                        pi = pcount % 8
                        pcount += 1
                        pst = self.ps[pi]
                        S.op("pe", [lambda kc=kc, pst=pst, dc=dc, b=b, tn=tn, ca=ca, cb_=cb_: nc.tensor.matmul(
                            pst[:, 0:tn], wbr[:, kc, dc * 128:(dc + 1) * 128], ys[b][:, kc, 0:tn],
                            start=(kc == ca), stop=(kc == cb_ - 1)) for kc in range(ca, cb_)],
                            r=[("ys", b)] + wbr_keys, w=[("ps", pi)])
                        S.op("dve", lambda pst=pst, bi=bi, gtb=gtb, tn=tn: nc.vector.tensor_tensor(
                            out=tb[bi][:, 0:tn], in0=pst[:, 0:tn], in1=gtb[:, bi, 0:tn], op=ALU.mult),
                            r=[("ps", pi), ("gt", gi)], w=[("tb", bi)])
                    S.op("pool", lambda tn=tn: nc.gpsimd.tensor_tensor(
                        out=tb[0][:, 0:tn], in0=tb[0][:, 0:tn], in1=tb[1][:, 0:tn], op=ALU.add),
                        r=[("tb", 0), ("tb", 1)], w=[("tb", 0)])
                    S.op("pool", lambda tn=tn: nc.gpsimd.tensor_tensor(
                        out=tb[2][:, 0:tn], in0=tb[2][:, 0:tn], in1=tb[3][:, 0:tn], op=ALU.add),
                        r=[("tb", 2), ("tb", 3)], w=[("tb", 2)])
                    S.op("pool", lambda tn=tn, dc=dc: nc.gpsimd.tensor_tensor(
                        out=mg[:, dc, 0:tn], in0=tb[0][:, 0:tn], in1=tb[2][:, 0:tn], op=ALU.add),
                        r=[("tb", 0), ("tb", 2)], w=[("mg", dc)])
                for dc2 in range(8):
                    pi = pcount % 8
                    pcount += 1
                    pst = self.ps[pi]
                    S.op("pe", [lambda dc=dc, pst=pst, dc2=dc2, tn=tn: nc.tensor.matmul(
                        pst[:, 0:tn], wo[:, dc, dc2 * 128:(dc2 + 1) * 128], mg[:, dc, 0:tn],
                        start=(dc == 0), stop=(dc == 7)) for dc in range(8)],
                        r=[("mg", dc) for dc in range(8)] + wo_keys, w=[("ps", pi)])
                    S.op("act", lambda pst=pst, dc2=dc2, tn=tn: nc.scalar.copy(out=o2[:, dc2, 0:tn], in_=pst[:, 0:tn]),
                         r=[("ps", pi)], w=[("o2", dc2)])
                self.rms_rstd(o2, 8, D, tn, sq, rs, 0, [("o2", c) for c in range(8)], "m1")
                pcount = 1
                for c in range(8):
                    S.op("dve", lambda c=c, tn=tn: nc.vector.scalar_tensor_tensor(
                        out=o2[:, c, 0:tn], in0=o2[:, c, 0:tn], scalar=self.gain(l, 1, c), in1=rs[:, 0:tn],
                        op0=ALU.mult, op1=ALU.mult), r=[("o2", c), ("rstd", "m1")], w=[("o2", c)])
                    S.op("pool", lambda c=c, b=b, tn=tn: nc.gpsimd.tensor_tensor(
                        out=hb[b][:, c, 0:tn], in0=hb[b][:, c, 0:tn], in1=o2[:, c, 0:tn], op=ALU.add),
                        r=[("hb", b, c), ("o2", c)], w=[("hb", b, c)])
                hkeys = [("hb", b, c) for c in range(8)]
                S.op("sp", lambda b=b, t0=t0, tn=tn: nc.sync.dma_start(
                    out=self.hT[:, :, t0:t0 + tn].rearrange("c p t -> p c t"), in_=hb[b][:, :, 0:tn]),
                    r=hkeys, w=[("d_hT", n)], dma=True)
                self.rms_rstd(hb[b], 8, D, tn, sq, rs, 0, hkeys, "m2")
                pcount = 1
                for c in range(8):
                    S.op("dve", lambda c=c, b=b, tn=tn: nc.vector.scalar_tensor_tensor(
                        out=hn[b][:, c, 0:tn], in0=hb[b][:, c, 0:tn], scalar=self.gain(l, 2, c), in1=rs[:, 0:tn],
                        op0=ALU.mult, op1=ALU.mult), r=[("hb", b, c), ("rstd", "m2")], w=[("hn", b, c)])
                S.op("sp", lambda b=b, t0=t0, tn=tn: nc.sync.dma_start(
                    out=self.hn2T[:, :, t0:t0 + tn].rearrange("c p t -> p c t"), in_=hn[b][:, :, 0:tn]),
                    r=[("hn", b, c) for c in range(8)], w=[("d_hn2", n)], dma=True)
            S.barrier()

    def _ffn1(self, l):
        nc, S = self.nc, self.S
        with contextlib.ExitStack() as ph:
            sb = lambda name, shape, dt: self.sb(ph, name, shape, dt)
            hn = sb("fhn", [128, 8, L], BF16)
            wst = [sb("fwst%d" % i, [128, 8, 512], F32) for i in range(2)]
            wbf = [sb("fwbf%d" % i, [128, 8, 512], BF16) for i in range(2)]
            rl = [sb("frl%d" % i, [128, 512], F32) for i in range(3)]
            ab = [sb("fab%d" % i, [128, 512], BF16) for i in range(4)]
            w_l = self.mlp_w1[l].rearrange("(c p) n -> p c n", p=128)
            for n, (t0, tn) in enumerate(BLOCKS):
                S.op("sp", lambda t0=t0, tn=tn: nc.sync.dma_start(
                    out=hn[:, :, t0:t0 + tn], in_=self.hn2T[:, :, t0:t0 + tn].rearrange("c p t -> p c t")),
                    w=[("hn", n)], dma=True)

            def load_w(gi):
                b = gi % 2
                S.op("sp", lambda: nc.sync.dma_start(out=wst[b][:], in_=w_l[:, :, gi * 512:(gi + 1) * 512]),
                     w=[("wst", b)], dma=True)
                S.op("pool", lambda: nc.gpsimd.tensor_copy(out=wbf[b][:], in_=wst[b][:]), r=[("wst", b)], w=[("wbf", b)])
            load_w(0)
            pcount = 0
            cnt = 0
            for gi in range(8):
                if gi + 1 < 8:
                    load_w(gi + 1)
                wb = wbf[gi % 2]
                for ti in range(4):
                    fc = gi * 4 + ti
                    for n, (t0, tn) in enumerate(BLOCKS):
                        pi = pcount % 8
                        pcount += 1
                        pst = self.ps[pi]
                        S.op("pe", [lambda c=c, ti=ti, t0=t0, tn=tn, pst=pst, wb=wb: nc.tensor.matmul(
                            pst[:, 0:tn], wb[:, c, ti * 128:(ti + 1) * 128], hn[:, c, t0:t0 + tn],
                            start=(c == 0), stop=(c == 7)) for c in range(8)],
                            r=[("hn", n), ("wbf", gi % 2)], w=[("ps", pi)])
                        ri, ai = cnt % 3, cnt % 4
                        cnt += 1
                        S.op("act", lambda pst=pst, ri=ri, tn=tn: nc.scalar.activation(
                            out=rl[ri][:, 0:tn], in_=pst[:, 0:tn], func=AF.Relu), r=[("ps", pi)], w=[("rl", ri)])
                        eng = "dve" if cnt % 2 == 0 else "pool"
                        E = nc.vector if eng == "dve" else nc.gpsimd
                        S.op(eng, lambda E=E, ri=ri, ai=ai, tn=tn: E.tensor_tensor(
                            out=ab[ai][:, 0:tn], in0=rl[ri][:, 0:tn], in1=rl[ri][:, 0:tn], op=ALU.mult),
                            r=[("rl", ri)], w=[("ab", ai)])
                        S.op("sp", lambda fc=fc, ai=ai, t0=t0, tn=tn: nc.sync.dma_start(
                            out=self.aT[fc, :, t0:t0 + tn], in_=ab[ai][:, 0:tn]), r=[("ab", ai)], w=[("d_aT", fc, n)],
                            dma=True)
            S.barrier()

    def _ffn2(self, l):
        nc, S = self.nc, self.S
        with contextlib.ExitStack() as ph:
            sb = lambda name, shape, dt: self.sb(ph, name, shape, dt)
            w2 = sb("gw2", [128, 32, D], BF16)
            wst = [sb("gwst%d" % i, [128, D], F32) for i in range(2)]
            ab = [sb("gab%d" % i, [128, 32, 512], BF16) for i in range(2)]
            hb = [sb("ghb%d" % i, [128, 8, 512], F32) for i in range(2)]
            ff = sb("gff", [128, 8, 512], F32)
            sq = sb("gsq", [128, 8, 512], F32)
            rs = sb("grs", [128, 512], F32)
            for c in range(32):
                b = c % 2
                S.op("sp", lambda b=b, c=c: nc.sync.dma_start(out=wst[b][:], in_=self.mlp_w2[l, c * 128:(c + 1) * 128, :]),
                     w=[("wst", b)], dma=True)
                S.op("pool", lambda b=b, c=c: nc.gpsimd.tensor_copy(out=w2[:, c, :], in_=wst[b][:]),
                     r=[("wst", b)], w=[("w2", c)])
            w2keys = [("w2", c) for c in range(32)]
            pcount = 0
            for n, (t0, tn) in enumerate(BLOCKS):
                b = n % 2
                S.op("sp", lambda b=b, t0=t0, tn=tn: nc.sync.dma_start(
                    out=ab[b][:, :, 0:tn], in_=self.aT[:, :, t0:t0 + tn].rearrange("c p t -> p c t")),
                    w=[("ab", b)], dma=True)
                S.op("sp", lambda b=b, t0=t0, tn=tn: nc.sync.dma_start(
                    out=hb[b][:, :, 0:tn], in_=self.hT[:, :, t0:t0 + tn].rearrange("c p t -> p c t")),
                    w=[("hb", b, c) for c in range(8)], dma=True)
                for dc2 in range(8):
                    pi = 1 + pcount % 7
                    pcount += 1
                    pst = self.ps[pi]
                    S.op("pe", [lambda fc=fc, pst=pst, dc2=dc2, b=b, tn=tn: nc.tensor.matmul(
                        pst[:, 0:tn], w2[:, fc, dc2 * 128:(dc2 + 1) * 128], ab[b][:, fc, 0:tn],
                        start=(fc == 0), stop=(fc == 31)) for fc in range(32)],
                        r=[("ab", b)] + w2keys, w=[("ps", pi)])
                    S.op("act", lambda pst=pst, dc2=dc2, tn=tn: nc.scalar.copy(out=ff[:, dc2, 0:tn], in_=pst[:, 0:tn]),
                         r=[("ps", pi)], w=[("ff", dc2)])
                self.rms_rstd(ff, 8, D, tn, sq, rs, 0, [("ff", c) for c in range(8)], "f")
                for c in range(8):
                    S.op("dve", lambda c=c, tn=tn: nc.vector.scalar_tensor_tensor(
                        out=ff[:, c, 0:tn], in0=ff[:, c, 0:tn], scalar=self.gain(l, 3, c), in1=rs[:, 0:tn],
                        op0=ALU.mult, op1=ALU.mult), r=[("ff", c), ("rstd", "f")], w=[("ff", c)])
                    S.op("pool", lambda c=c, b=b, tn=tn: nc.gpsimd.tensor_tensor(
                        out=hb[b][:, c, 0:tn], in0=hb[b][:, c, 0:tn], in1=ff[:, c, 0:tn], op=ALU.add),
                        r=[("hb", b, c), ("ff", c)], w=[("hb", b, c)])
                S.op("sp", lambda b=b, t0=t0, tn=tn: nc.sync.dma_start(
                    out=self.hT[:, :, t0:t0 + tn].rearrange("c p t -> p c t"), in_=hb[b][:, :, 0:tn]),
                    r=[("hb", b, c) for c in range(8)], w=[("d_hT", n)], dma=True)
            S.barrier()

    def _epilogue(self):
        nc, S = self.nc, self.S
        with contextlib.ExitStack() as ph:
            hb = [self.sb(ph, "ehb%d" % i, [128, 8, 128], F32) for i in range(2)]
            ot = [self.sb(ph, "eot%d" % i, [128, D], F32) for i in range(2)]
            for i in range(NT):
                b = i % 2
                S.op("sp", lambda b=b, i=i: nc.sync.dma_start(
                    out=hb[b][:], in_=self.hT[:, :, i * 128:(i + 1) * 128].rearrange("c p t -> p c t")),
                    w=[("hb", b)], dma=True)
                for half in range(2):
                    pi = (2 * i + half) % 4
                    pst = self.ps[pi]
                    S.op("pe", [lambda c=c, half=half, pst=pst, b=b: nc.tensor.transpose(
                        pst[:, c * 128:(c + 1) * 128], hb[b][:, half * 4 + c, :], self.ident[:]) for c in range(4)],
                        r=[("hb", b)], w=[("ps", pi)])
                    if half == 0:
                        S.op("act", lambda pst=pst, b=b: nc.scalar.copy(out=ot[b][:, 0:512], in_=pst[:, :]),
                             r=[("ps", pi)], w=[("ot", b, 0)])
                    else:
                        S.op("dve", lambda pst=pst, b=b: nc.vector.tensor_copy(out=ot[b][:, 512:1024], in_=pst[:, :]),
                             r=[("ps", pi)], w=[("ot", b, 1)])
                lo = 128 * i - NMETA
                if i == 0:
                    src, dst = ot[b][NMETA:128, :], self.out[0:128 - NMETA, :]
                elif i == L // 128 - 1:
                    src, dst = ot[b][0:NMETA, :], self.out[lo:S_LEN, :]
                else:
                    src, dst = ot[b][:, :], self.out[lo:lo + 128, :]
                S.op("sp", lambda src=src, dst=dst: nc.sync.dma_start(out=dst, in_=src),
                     r=[("ot", b, 0), ("ot", b, 1)], w=[("d_out", i)], dma=True)
            S.barrier()


def _consts():
    pos = np.arange(L, dtype=np.float32)
    inv_freq = (10000.0 ** (-np.arange(0, 32, 2, dtype=np.float32) / 32.0)).astype(np.float32)
    ang = (pos[:, None] * inv_freq[None, :]).astype(np.float32)
    cos = np.cos(ang).astype(np.float32).T
    sin = np.sin(ang).astype(np.float32).T
    psw = np.zeros((32, 32), np.float32)
    for m in range(16):
        psw[16 + m, m] = -1.0
        psw[m, 16 + m] = 1.0
    a = np.arange(128)
    lim = np.where(a < 16, 16, np.where(a < 80, 80, 128))
    kb = np.arange(128)
    m_diag = (kb[:, None] < lim[None, :]).astype(np.float32)
    m_next = ((a[None, :] >= 80) & (kb[:, None] < 16)).astype(np.float32)
    mask = np.stack([m_diag, m_next], 0).astype(ml_dtypes.bfloat16)
    r_ = np.arange(128)
    tri = ((r_[:, None] <= r_[None, :]) & ((r_[:, None] // 64) == (r_[None, :] // 64))).astype(np.float32)
    mrs = ((r_[:, None] % 64) <= np.arange(64)[None, :]).astype(np.float32)
    obk = np.zeros((128, 2, 128), np.float32)
    obk[0:64, 0, :] = 1.0
    obk[64:128, 1, :] = 1.0
    return {"c_tri": tri, "c_mrs": mrs, "c_ob": obk, "c_iota": np.arange(512, dtype=np.float32), "c_ident": np.eye(128, dtype=np.float32),
            "c_cos": np.ascontiguousarray(np.concatenate([cos, cos], 0)),
            "c_sin": np.ascontiguousarray(np.concatenate([sin, sin], 0)),
            "c_psw": psw, "c_mask": mask}


_WEIGHT_KEYS = ["meta", "norm_gains", "w_in", "conv_w", "mlstm_gate_b", "mlstm_norm", "s5_a_re", "s5_a_im",
                "s5_log_step", "s5_b_re", "s5_b_im", "s5_c_re", "s5_c_im", "s5_d", "s5_glu", "mla_q_norm",
                "mla_kv_norm", "mla_w_uq", "mla_w_ukv", "w_branch", "w_out", "mlp_w1", "mlp_w2"]


def make_in_maps(inputs, n_cores=8):
    shared = {k: np.ascontiguousarray(np.asarray(inputs[k], dtype=np.float32)) for k in _WEIGHT_KEYS}
    shared.update(_consts())
    x = np.asarray(inputs["x"], dtype=np.float32)
    maps = []
    for b in range(n_cores):
        m = dict(shared)
        m["x"] = np.ascontiguousarray(x[b])
        maps.append(m)
    return maps


def kernel(**inputs):
    nc = Builder().build()
    in_maps = make_in_maps(inputs, 8)
    res = run_bass_kernel_spmd(nc, in_maps, core_ids=list(range(8)))
    return np.stack([np.asarray(r["out"], dtype=np.float32) for r in res.results], axis=0)
```

```python
import contextlib
import math
import numpy as np
import ml_dtypes
import concourse.bass as bass
import concourse.mybir as mybir
from concourse.bass_utils import run_bass_kernel_spmd
from concourse.alu_op_type import AluOpType as ALU

F32 = mybir.dt.float32
BF16 = mybir.dt.bfloat16
AF = mybir.ActivationFunctionType
AX = mybir.AxisListType

D = 1024
S_LEN = 4096
NMETA = 16
L = 4224
NT = L // 128
DEPTH = 2
EPS = 1e-6
IN_W = 6824
DFF = 4096
BLOCKS = [(i * 512, 512) for i in range(8)] + [(4096, 128)]

C_GATE, C_CB, C_CC, C_CV = 0, 4096, 4352, 4608
C_MQ, C_MK, C_MV, C_MO, C_MI, C_MF = 4864, 5120, 5376, 5632, 5888, 5892
C_SU, C_CQ, C_CKV, C_KR = 5896, 6152, 6536, 6792
TM_C0, TM_W = C_MK, 776


class Sched:
    EPOCH = 16000
    R = 8

    def __init__(self, nc, es):
        self.nc = nc
        self.es = es
        self.eng = {"pe": nc.tensor, "act": nc.scalar, "dve": nc.vector, "pool": nc.gpsimd, "sp": nc.sync}
        self.cnt = {e: 0 for e in self.eng}
        self.csem = {}
        self.dcnt = {e: 0 for e in self.eng}
        self.dsem = {}
        self.waited = {e: {} for e in self.eng}
        self.last_w = {}
        self.readers = {}
        self.open_dmas = []
        self.last_tok = {e: None for e in self.eng}
        self.nsem = 0

    def _sem(self, key):
        tab = self.csem if key[0] == "c" else self.dsem
        if key not in tab:
            tab[key] = self.es.enter_context(self.nc.semaphore("s_%s_%s_%s" % key))
            self.nsem += 1
        return tab[key]

    def _wait(self, eng, tok):
        key, val = tok
        if self.waited[eng].get(key, 0) >= val:
            return
        self.eng[eng].wait_ge(self._sem(key), val)
        self.waited[eng][key] = val

    def op(self, eng, fns, r=(), w=(), dma=False):
        deps = []
        for k in r:
            t = self.last_w.get(k)
            if t is not None:
                deps.append((t, True))
        for k in w:
            t = self.last_w.get(k)
            if t is not None:
                deps.append((t, False))
            for t in self.readers.get(k, ()):
                deps.append((t, False))
        for (t, raw) in deps:
            teng, tdma, tok = t
            if (not tdma) and teng == eng and eng == "pe" and not dma:
                continue
            self._wait(eng, tok)
        if not isinstance(fns, (list, tuple)):
            fns = [fns]
        if dma:
            k = self.dcnt[eng]
            slot = k % self.R
            key = ("d", eng, slot)
            if k >= self.R:
                self._wait(eng, (key, 16 * (k // self.R)))
            ins = None
            for f in fns:
                ins = f()
            ins.then_inc(self._sem(key), 16)
            self.dcnt[eng] = k + 1
            tok = (key, 16 * (k // self.R + 1))
            me = (eng, True, tok)
            self.open_dmas.append(me)
        else:
            ins = None
            for f in fns:
                ins = f()
            n = self.cnt[eng]
            key = ("c", eng, n // self.EPOCH)
            ins.then_inc(self._sem(key), 1)
            self.cnt[eng] = n + 1
            tok = (key, n % self.EPOCH + 1)
            me = (eng, False, tok)
            self.last_tok[eng] = me
        for k in r:
            self.readers.setdefault(k, []).append(me)
        for k in w:
            self.last_w[k] = me
            self.readers[k] = []
        return me

    def barrier(self, engines=None):
        toks = [t for t in self.last_tok.values() if t is not None] + self.open_dmas
        for e in (engines or self.eng):
            for (_, _, tok) in toks:
                self._wait(e, tok)
        self.open_dmas = []
        self.last_w = {}
        self.readers = {}


class Builder:
    def __init__(self, debug_taps=()):
        self.debug_taps = set(debug_taps)
        self.nc = bass.Bass("TRN2", target_bir_lowering=False)
        self.es = contextlib.ExitStack()

    def dram_in(self, name, shape, dt=F32):
        return self.nc.dram_tensor(name, list(shape), dt, kind="ExternalInput").ap()

    def dram_scr(self, name, shape, dt):
        kind = "ExternalOutput" if name in self.debug_taps else "Internal"
        return self.nc.dram_tensor(name, list(shape), dt, kind=kind).ap()

    def sb(self, stack, name, shape, dt):
        self._uid = getattr(self, "_uid", 0) + 1
        return stack.enter_context(self.nc.sbuf_tensor("%s_%d" % (name, self._uid), list(shape), dt))

    def build(self, n_layers=DEPTH, stop_after=None):
        nc = self.nc
        with self.es as es:
            self.S = Sched(nc, es)
            self._declare_io()
            self._globals(es)
            import os
            if "prologue" not in os.environ.get("K_SKIP", "").split(","):
                self._prologue()
            for l in range(n_layers):
                self._layer(l, stop_after)
                if stop_after is not None:
                    break
            if stop_after is None:
                self._epilogue()
            self.S.barrier()
        return nc

    def _declare_io(self):
        di = self.dram_in
        self.x = di("x", [S_LEN, D])
        self.meta = di("meta", [NMETA, D])
        self.norm_gains = di("norm_gains", [DEPTH, 4, D])
        self.w_in = di("w_in", [DEPTH, D, IN_W])
        self.conv_w = di("conv_w", [DEPTH, 3, 256])
        self.mlstm_gate_b = di("mlstm_gate_b", [DEPTH, 8])
        self.mlstm_norm = di("mlstm_norm", [DEPTH, 256])
        self.s5_a_re = di("s5_a_re", [DEPTH, 16, 64])
        self.s5_a_im = di("s5_a_im", [DEPTH, 16, 64])
        self.s5_log_step = di("s5_log_step", [DEPTH, 16])
        self.s5_b_re = di("s5_b_re", [DEPTH, 16, 64, 16])
        self.s5_b_im = di("s5_b_im", [DEPTH, 16, 64, 16])
        self.s5_c_re = di("s5_c_re", [DEPTH, 16, 16, 64])
        self.s5_c_im = di("s5_c_im", [DEPTH, 16, 16, 64])
        self.s5_d = di("s5_d", [DEPTH, 256])
        self.s5_glu = di("s5_glu", [DEPTH, 256, 512])
        self.mla_q_norm = di("mla_q_norm", [DEPTH, 384])
        self.mla_kv_norm = di("mla_kv_norm", [DEPTH, 256])
        self.mla_w_uq = di("mla_w_uq", [DEPTH, 384, 768])
        self.mla_w_ukv = di("mla_w_ukv", [DEPTH, 256, 1024])
        self.w_branch = di("w_branch", [DEPTH, 1280, D])
        self.w_out = di("w_out", [DEPTH, D, D])
        self.mlp_w1 = di("mlp_w1", [DEPTH, D, DFF])
        self.mlp_w2 = di("mlp_w2", [DEPTH, DFF, D])
        self.c_ident = di("c_ident", [128, 128])
        self.c_cos = di("c_cos", [32, L])
        self.c_sin = di("c_sin", [32, L])
        self.c_psw = di("c_psw", [32, 32])
        self.c_mask = di("c_mask", [2, 128, 128], BF16)
        self.c_tri = di("c_tri", [128, 128])
        self.c_mrs = di("c_mrs", [128, 64])
        self.c_ob = di("c_ob", [128, 2, 128])
        self.c_iota = di("c_iota", [512])
        self.out = self.nc.dram_tensor("out", [S_LEN, D], F32, kind="ExternalOutput").ap()
        ds = self.dram_scr
        self.hT = ds("hT", [8, 128, L], F32)
        self.gT = ds("gT", [32, 128, L], BF16)
        self.convT = ds("convT", [6, 128, L], F32)
        self.mqkT = ds("mqkT", [4, 128, L], BF16)
        self.suT = ds("suT", [2, 128, L], BF16)
        self.cqT = ds("cqT", [3, 128, L], F32)
        self.ckvT = ds("ckvT", [2, 128, L], F32)
        self.krT = ds("krT", [32, L], F32)
        self.mTM = ds("mTM", [L, TM_W], F32)
        self.ysT = ds("ysT", [10, 128, L], BF16)
        self.hn2T = ds("hn2T", [8, 128, L], BF16)
        self.aT = ds("aT", [32, 128, L], BF16)

    def _globals(self, es):
        nc, S = self.nc, self.S
        self.ps = [es.enter_context(nc.psum_tensor("ps%d" % i, [128, 512], F32)) for i in range(8)]
        self.ident = self.sb(es, "ident", [128, 128], F32)
        self.ones_f = self.sb(es, "ones_f", [128, 128], F32)
        self.eps_t = self.sb(es, "eps_t", [128, 1], F32)
        self.one_t = self.sb(es, "one_t", [128, 1], F32)
        self.gains = self.sb(es, "gains", [128, DEPTH * 4, 8], F32)
        S.op("sp", lambda: nc.sync.dma_start(out=self.ident[:], in_=self.c_ident[:]), w=[("ident",)], dma=True)
        S.op("dve", lambda: nc.vector.memset(self.ones_f[:], 1.0), w=[("ones_f",)])
        S.op("dve", lambda: nc.vector.memset(self.eps_t[:], EPS), w=[("eps_t",)])
        S.op("dve", lambda: nc.vector.memset(self.one_t[:], 1.0), w=[("one_t",)])
        with nc.allow_non_contiguous_dma(reason="tiny gain load"):
            S.op("sp", lambda: nc.sync.dma_start(
                out=self.gains[:], in_=self.norm_gains.rearrange("l k (c p) -> p (l k) c", p=128)),
                w=[("gains",)], dma=True)
        S.barrier()

    def gain(self, l, k, c):
        return self.gains[:, l * 4 + k, c:c + 1]

    def _prologue(self):
        nc, S = self.nc, self.S
        with contextlib.ExitStack() as ph:
            xt = [self.sb(ph, "xt%d" % i, [128, D], F32) for i in range(2)]
            ht = [self.sb(ph, "ht%d" % i, [128, 8, 128], F32) for i in range(2)]
            for i in range(NT):
                b = i % 2
                xb, hb = xt[b], ht[b]
                lo = 128 * i - NMETA
                if i == 0:
                    S.op("sp", lambda xb=xb: nc.sync.dma_start(out=xb[0:NMETA, :], in_=self.meta[:, :]),
                         w=[("xt", b)], dma=True)
                    S.op("sp", lambda xb=xb: nc.sync.dma_start(out=xb[NMETA:128, :], in_=self.x[0:128 - NMETA, :]),
                         w=[("xt", b, 1)], dma=True)
                    rk = [("xt", b), ("xt", b, 1)]
                elif i == NT - 1:
                    S.op("dve", lambda xb=xb: nc.vector.memset(xb[:], 0.0), w=[("xt", b)])
                    S.op("sp", lambda xb=xb, lo=lo: nc.sync.dma_start(out=xb[0:NMETA, :], in_=self.x[lo:S_LEN, :]),
                         r=[("xt", b)], w=[("xt", b, 1)], dma=True)
                    rk = [("xt", b), ("xt", b, 1)]
                else:
                    S.op("sp", lambda xb=xb, lo=lo: nc.sync.dma_start(out=xb[:], in_=self.x[lo:lo + 128, :]),
                         w=[("xt", b), ("xt", b, 1)], dma=True)
                    rk = [("xt", b), ("xt", b, 1)]
                for half in range(2):
                    pst = self.ps[(2 * i + half) % 4]
                    pk = ("ps", (2 * i + half) % 4)
                    S.op("pe", [lambda c=c, xb=xb, pst=pst, half=half: nc.tensor.transpose(
                        pst[:, c * 128:(c + 1) * 128], xb[:, (half * 4 + c) * 128:(half * 4 + c + 1) * 128],
                        self.ident[:]) for c in range(4)], r=rk, w=[pk])
                    eng = "act" if half == 0 else "dve"
                    if half == 0:
                        S.op("act", lambda hb=hb, pst=pst: nc.scalar.copy(
                            out=hb[:, 0:4, :], in_=pst[:, :].rearrange("p (c t) -> p c t", c=4)),
                            r=[pk], w=[("ht", b, 0)])
                    else:
                        S.op("dve", lambda hb=hb, pst=pst: nc.vector.tensor_copy(
                            out=hb[:, 4:8, :], in_=pst[:, :].rearrange("p (c t) -> p c t", c=4)),
                            r=[pk], w=[("ht", b, 1)])
                S.op("sp", lambda hb=hb, i=i: nc.sync.dma_start(
                    out=self.hT[:, :, i * 128:(i + 1) * 128].rearrange("c p t -> p c t"), in_=hb[:]),
                    r=[("ht", b, 0), ("ht", b, 1)], w=[("d_hT", i // 4)], dma=True)
            S.barrier()

    def _layer(self, l, stop_after=None):
        import os
        skip = os.environ.get("K_SKIP", "").split(",")
        if "inproj" not in skip:
            self._inproj(l)
        if stop_after == "inproj":
            return
        if "conv" not in skip:
            self._conv(l)
        if "mla" not in skip:
            self._mla(l)
        if stop_after == "mla":
            return
        if "mlstm" not in skip:
            self._mlstm(l)
        if stop_after == "mlstm":
            return
        if "s5" not in skip:
            self._s5(l)
        if stop_after == "s5":
            return
        self._merge(l)
        if stop_after == "merge":
            return
        self._ffn1(l)
        self._ffn2(l)

    def rms_rstd(self, src, nch, nfeat, tn, sq, rstd, pbank, rkeys, key):
        nc, S = self.nc, self.S
        S.op("act", lambda: nc.scalar.activation(out=sq[:, 0:nch, 0:tn], in_=src[:, 0:nch, 0:tn], func=AF.Square),
             r=rkeys, w=[("sq", key)])
        pst = self.ps[pbank]
        S.op("pe", [lambda c=c: nc.tensor.matmul(pst[:, 0:tn], self.ones_f[:], sq[:, c, 0:tn],
                                                  start=(c == 0), stop=(c == nch - 1)) for c in range(nch)],
             r=[("sq", key)], w=[("ps", pbank)])
        S.op("act", lambda: nc.scalar.activation(out=rstd[:, 0:tn], in_=pst[:, 0:tn], func=AF.Sqrt,
                                                 bias=self.eps_t[:], scale=1.0 / nfeat),
             r=[("ps", pbank)], w=[("rstd", key)])
        S.op("dve", lambda: nc.vector.reciprocal(out=rstd[:, 0:tn], in_=rstd[:, 0:tn]),
             r=[("rstd", key)], w=[("rstd", key)])

    def _conv(self, l):
        nc, S = self.nc, self.S
        with contextlib.ExitStack() as ph:
            cw = self.sb(ph, "cw", [128, 3, 2], F32)
            cin = [self.sb(ph, "cin%d" % i, [128, 6, 512], F32) for i in range(2)]
            u = self.sb(ph, "u", [128, 2, 514], F32)
            y = [self.sb(ph, "y%d" % i, [128, 512], F32) for i in range(2)]
            o = [self.sb(ph, "o%d" % i, [128, 512], BF16) for i in range(4)]
            with nc.allow_non_contiguous_dma(reason="tiny conv weight load"):
                for k in range(3):
                    S.op("sp", lambda k=k: nc.sync.dma_start(
                        out=cw[:, k, :], in_=self.conv_w[l, k].rearrange("(c p) -> p c", p=128)),
                        w=[("cw", k)], dma=True)
            S.op("dve", lambda: nc.vector.memset(u[:, :, 0:2], 0.0), w=[("uh",)])
            oc = 0
            for n, (t0, tn) in enumerate(BLOCKS):
                b = n % 2
                S.op("sp", lambda b=b, t0=t0, tn=tn: nc.sync.dma_start(
                    out=cin[b][:, :, 0:tn], in_=self.convT[:, :, t0:t0 + tn].rearrange("c p t -> p c t")),
                    w=[("cin", b)], dma=True)
                for c in range(2):
                    S.op("pool", lambda b=b, c=c, tn=tn: nc.gpsimd.tensor_tensor(
                        out=u[:, c, 2:2 + tn], in0=cin[b][:, 2 + c, 0:tn], in1=cin[b][:, 4 + c, 0:tn], op=ALU.mult),
                        r=[("cin", b)], w=[("u", c)])
                    yb = y[c]
                    S.op("dve", lambda c=c, tn=tn, yb=yb: nc.vector.tensor_scalar(
                        out=yb[:, 0:tn], in0=u[:, c, 2:2 + tn], scalar1=cw[:, 2, c:c + 1], scalar2=None, op0=ALU.mult),
                        r=[("u", c), ("cw", 2)], w=[("y", c)])
                    for k in (1, 0):
                        S.op("dve", lambda c=c, tn=tn, yb=yb, k=k: nc.vector.scalar_tensor_tensor(
                            out=yb[:, 0:tn], in0=u[:, c, k:k + tn], scalar=cw[:, k, c:c + 1], in1=yb[:, 0:tn],
                            op0=ALU.mult, op1=ALU.add), r=[("u", c), ("uh",), ("y", c), ("cw", k)], w=[("y", c)])
                    ob = o[oc % 4]
                    okey = ("o", oc % 4)
                    oc += 1
                    S.op("dve", lambda b=b, c=c, tn=tn, yb=yb, ob=ob: nc.vector.tensor_tensor(
                        out=ob[:, 0:tn], in0=yb[:, 0:tn], in1=cin[b][:, c, 0:tn], op=ALU.mult),
                        r=[("y", c), ("cin", b)], w=[okey])
                    S.op("sp", lambda c=c, t0=t0, tn=tn, ob=ob: nc.sync.dma_start(
                        out=self.ysT[c, :, t0:t0 + tn], in_=ob[:, 0:tn]), r=[okey], w=[("d_ys", c, n)], dma=True)
                S.op("act", lambda tn=tn: nc.scalar.copy(out=u[:, :, 0:2], in_=u[:, :, tn:tn + 2]),
                     r=[("u", 0), ("u", 1)], w=[("uh",)])
            S.barrier()

    def _mlstm(self, l):
        nc, S = self.nc, self.S
        with contextlib.ExitStack() as ph:
            sb = lambda name, shape, dt: self.sb(ph, name, shape, dt)
            tri = sb("tri", [128, 128], F32)
            ob = sb("ob", [128, 2, 128], F32)
            mrs = sb("mrs", [128, 64], F32)
            gb = sb("gb", [128, 8], F32)
            ng = sb("ng", [128, 256], F32)
            ident_b = sb("ident_b2", [128, 128], BF16)
            Cst = sb("Cst", [128, 2, 65], F32)
            Cbf = sb("Cbf", [128, 2, 65], BF16)
            mt = [sb("mt%d" % i, [128, TM_W], F32) for i in range(2)]
            qk = [sb("qk%d" % i, [128, 4, 128], BF16) for i in range(2)]
            qz = [sb("qz%d" % i, [128, 4, 128], BF16) for i in range(2)]
            vaug = [sb("mv%d" % i, [128, 4, 65], BF16) for i in range(2)]
            kwz = [sb("kwz%d" % i, [128, 2, 4, 64], BF16) for i in range(2)]
            sm = [sb("sm%d" % i, [128, 4, 128], BF16) for i in range(2)]
            sig = sb("sig", [128, 256], F32)
            gt = sb("gt", [128, 8, 4], F32)
            bl = sb("bl", [128, 2, 4], F32)
            den = sb("den", [128, 3, 4], F32)
            hv = sb("hv", [128, 4, 64], F32)
            hsq = sb("hsq", [128, 4, 64], F32)
            yb = [sb("yb%d" % i, [128, 256], BF16) for i in range(2)]
            yT = [sb("yT%d" % i, [128, 2, 128], BF16) for i in range(2)]
            ld = lambda dst, src, key: S.op("sp", lambda: nc.sync.dma_start(out=dst, in_=src), w=[key], dma=True)
            ld(tri[:], self.c_tri[:, :], ("tri",))
            ld(ob[:], self.c_ob[:, :, :], ("ob",))
            ld(mrs[:], self.c_mrs[:, :], ("mrs",))
            ld(gb[:], self.mlstm_gate_b[l].partition_broadcast(128), ("gb",))
            ld(ng[:], self.mlstm_norm[l].partition_broadcast(128), ("ng",))
            S.op("dve", lambda: nc.vector.tensor_copy(out=ident_b[:], in_=self.ident[:]), w=[("ident_b",)])
            S.op("dve", lambda: nc.vector.memset(Cst[:], 0.0), w=[("Cst", h) for h in range(4)])
            S.op("dve", lambda: nc.vector.memset(Cbf[:], 0.0), w=[("Cbf", h) for h in range(4)])
            for b in range(2):
                S.op("dve", lambda b=b: nc.vector.memset(vaug[b][:, :, 64:65], 1.0), w=[("mv1", b)])
                S.op("pool", lambda b=b: nc.gpsimd.memset(qz[b][:], 0.0), w=[("qz", b)])
                S.op("pool", lambda b=b: nc.gpsimd.memset(kwz[b][:], 0.0), w=[("kwz", b)])
                S.op("pool", lambda b=b: nc.gpsimd.memset(sm[b][:], 0.0), w=[("sm", b)])
            import os
            mstop = int(os.environ.get("K_MSTOP", "99"))

            def stage(n):
                if n > mstop:
                    raise StopIteration
            ck = 0
            try:
                for i in range(NT):
                    b = i % 2
                    m = mt[b]
                    S.op("sp", lambda m=m, i=i: nc.sync.dma_start(out=m[:], in_=self.mTM[i * 128:(i + 1) * 128, :]),
                         w=[("mt", b)], dma=True)
                    S.op("sp", lambda b=b, i=i: nc.sync.dma_start(
                        out=qk[b][:], in_=self.mqkT[:, :, i * 128:(i + 1) * 128].rearrange("c p t -> p c t")),
                        w=[("qk", b)], dma=True)
                    S.op("dve", lambda m=m: nc.vector.tensor_tensor(out=gt[:, 0, :], in0=m[:, 768:772], in1=gb[:, 0:4], op=ALU.add),
                         r=[("mt", b), ("gb",)], w=[("gt", 0)])
                    S.op("dve", lambda m=m: nc.vector.tensor_tensor(out=gt[:, 5, :], in0=m[:, 772:776], in1=gb[:, 4:8], op=ALU.add),
                         r=[("mt", b), ("gb",)], w=[("gt", 5)])
                    S.op("act", lambda: nc.scalar.activation(out=gt[:, 5, :], in_=gt[:, 5, :], func=AF.Exp, scale=-1.0),
                         r=[("gt", 5)], w=[("gt", 5)])
                    S.op("act", lambda: nc.scalar.activation(out=gt[:, 1, :], in_=gt[:, 5, :], func=AF.Ln, bias=self.one_t[:], scale=1.0),
                         r=[("gt", 5)], w=[("gt", 1)])
                    pg = self.ps[0]
                    S.op("pe", lambda: nc.tensor.matmul(pg[:, 0:4], tri[:], gt[:, 1, :], start=True, stop=True),
                         r=[("gt", 1), ("tri",)], w=[("ps", 0)])
                    S.op("pe", [lambda cc=cc: nc.tensor.matmul(
                        pg[:, 8 + 4 * cc:12 + 4 * cc], ob[:, cc, :], gt[:, 1, :], start=True, stop=True)
                        for cc in range(2)], r=[("gt", 1), ("ob",)], w=[("ps", 0, 1)])
                    for cc in range(2):
                        S.op("dve", lambda cc=cc: nc.vector.tensor_copy(
                            out=gt[64 * cc:64 * cc + 64, 2, :], in_=pg[64 * cc:64 * cc + 64, 8 + 4 * cc:12 + 4 * cc]),
                            r=[("ps", 0, 1)], w=[("gt", 2, cc)])
                    S.op("dve", lambda: nc.vector.tensor_tensor(out=gt[:, 5, :], in0=gt[:, 2, :], in1=pg[:, 0:4], op=ALU.subtract),
                         r=[("gt", 2, 0), ("gt", 2, 1), ("ps", 0)], w=[("gt", 5)])
                    S.op("dve", lambda: nc.vector.tensor_tensor(out=gt[:, 3, :], in0=gt[:, 0, :], in1=gt[:, 5, :], op=ALU.subtract),
                         r=[("gt", 0), ("gt", 5)], w=[("gt", 3)])
                    S.op("act", lambda: nc.scalar.activation(out=gt[:, 3, :], in_=gt[:, 3, :], func=AF.Exp),
                         r=[("gt", 3)], w=[("gt", 3)])
                    S.op("dve", lambda: nc.vector.tensor_scalar(out=gt[:, 3, :], in0=gt[:, 3, :], scalar1=0.125, scalar2=None,
                                                                op0=ALU.mult), r=[("gt", 3)], w=[("gt", 3)])
                    S.op("act", lambda: nc.scalar.activation(out=gt[:, 4, :], in_=gt[:, 5, :], func=AF.Exp, scale=-1.0),
                         r=[("gt", 5)], w=[("gt", 4)])
                    S.op("act", lambda: nc.scalar.activation(
                        out=bl[:, :, :], in_=pg[:, 8:16].rearrange("p (c h) -> p c h", c=2), func=AF.Exp, scale=-1.0),
                        r=[("ps", 0, 1)], w=[("bl",)])
                    stage(2)
                    S.op("act", lambda m=m: nc.scalar.activation(out=sig[:], in_=m[:, 512:768], func=AF.Sigmoid),
                         r=[("mt", b)], w=[("sig",)])
                    S.op("act", lambda m=m, b=b: nc.scalar.copy(
                        out=vaug[b][:, :, 0:64], in_=m[:, 256:512].rearrange("p (h x) -> p h x", x=64)),
                        r=[("mt", b)], w=[("mv", b)])
                    for hp in range(2):
                        S.op("pool", lambda b=b, hp=hp: nc.gpsimd.tensor_copy(
                            out=qz[b][64 * hp:64 * hp + 64, :, :].rearrange("p (c two) t -> p c two t", two=2)[:, :, hp, :],
                            in_=qk[b][64 * hp:64 * hp + 64, 0:2, :]), r=[("qk", b), ("qz", b)], w=[("qz", b, hp)])
                    for cc in range(2):
                        rows = slice(64 * cc, 64 * cc + 64)
                        for h in range(4):
                            S.op("pool", lambda m=m, b=b, h=h, cc=cc, rows=rows: nc.gpsimd.tensor_scalar(
                                out=kwz[b][rows, cc, h, :], in0=m[rows, h * 64:(h + 1) * 64], scalar1=gt[rows, 3, h:h + 1],
                                scalar2=1.0, op0=ALU.mult, op1=ALU.mult), r=[("mt", b), ("gt", 3), ("kwz", b)], w=[("kwz", b, cc, h)])
                    pho = self.ps[3 + b]
                    for cc in range(2):
                        rows = slice(64 * cc, 64 * cc + 64)
                        toks = slice(64 * cc, 64 * cc + 64)
                        pss = self.ps[1 + ck % 2]
                        psk = ("ps", 1 + ck % 2)
                        psu = self.ps[5 + ck % 2]
                        puk = ("ps", 5 + ck % 2)
                        ck += 1
                        stage(3)
                        for h in range(4):
                            hp = slice(64 * (h % 2), 64 * (h % 2) + 64)
                            hc = h // 2
                            S.op("dve", lambda h=h, hp=hp, hc=hc, cc=cc: nc.vector.tensor_scalar(
                                out=Cbf[hp, hc, :], in0=Cst[hp, hc, :], scalar1=bl[hp, cc, h:h + 1], scalar2=None,
                                op0=ALU.mult), r=[("Cst", h), ("bl",)], w=[("Cbf", h)])
                        stage(4)
                        S.op("pe", [lambda h=h, rows=rows, toks=toks, pss=pss, b=b: nc.tensor.matmul(
                            pss[rows, h * 64:(h + 1) * 64], qk[b][:, 2 + h // 2, toks], qz[b][:, h, toks],
                            start=True, stop=True) for h in range(4)],
                            r=[("qk", b), ("qz", b), ("qz", b, 0), ("qz", b, 1)], w=[psk])
                        for h in range(4):
                            S.op("dve", lambda h=h, rows=rows, toks=toks, pss=pss, b=b: nc.vector.scalar_tensor_tensor(
                                out=sm[b][rows, h, toks], in0=pss[rows, h * 64:(h + 1) * 64], scalar=gt[rows, 3, h:h + 1],
                                in1=mrs[rows, :], op0=ALU.mult, op1=ALU.mult),
                                r=[psk, ("gt", 3), ("mrs",), ("sm", b)], w=[("sm", b, cc, h)])
                        stage(5)
                        mm = []
                        for h in range(4):
                            mm.append(lambda h=h, rows=rows, toks=toks, b=b: nc.tensor.matmul(
                                pho[rows, h * 65:(h + 1) * 65], sm[b][:, h, toks], vaug[b][:, h, :],
                                start=True, stop=False))
                            mm.append(lambda h=h, rows=rows, toks=toks, b=b: nc.tensor.matmul(
                                pho[rows, h * 65:(h + 1) * 65], qz[b][:, h, toks], Cbf[:, h // 2, :],
                                start=False, stop=True))
                        S.op("pe", mm, r=[("sm", b, cc, h) for h in range(4)] + [("sm", b), ("mv", b), ("mv1", b), ("qz", b),
                                          ("qz", b, 0), ("qz", b, 1)] + [("Cbf", h) for h in range(4)],
                             w=[("ps", 3 + b, cc)])
                        stage(6)
                        S.op("pe", [lambda h=h, psu=psu, b=b, cc=cc: nc.tensor.matmul(
                            psu[64 * (h % 2):64 * (h % 2) + 64, (h // 2) * 65:(h // 2) * 65 + 65],
                            kwz[b][:, cc, h, :], vaug[b][:, h, :], start=True, stop=True) for h in range(4)],
                            r=[("kwz", b, cc, h) for h in range(4)] + [("kwz", b), ("mv", b), ("mv1", b)], w=[puk])
                        for h in range(4):
                            hp = slice(64 * (h % 2), 64 * (h % 2) + 64)
                            hc = h // 2
                            S.op("dve", lambda h=h, hp=hp, hc=hc, psu=psu, cc=cc: nc.vector.scalar_tensor_tensor(
                                out=Cst[hp, hc, :], in0=Cst[hp, hc, :], scalar=bl[hp, cc, h:h + 1],
                                in1=psu[hp, hc * 65:hc * 65 + 65], op0=ALU.mult, op1=ALU.add),
                                r=[("Cst", h), ("bl",), puk], w=[("Cst", h)])
                    stage(7)
                    pv = pho[:, 0:260].rearrange("p (h x) -> p h x", x=65)
                    S.op("act", lambda pv=pv: nc.scalar.activation(out=den[:, 0, :], in_=pv[:, :, 64], func=AF.Abs),
                         r=[("ps", 3 + b, 0), ("ps", 3 + b, 1)], w=[("den", 0)])
                    S.op("dve", lambda: nc.vector.tensor_tensor(out=den[:, 1, :], in0=den[:, 0, :], in1=gt[:, 4, :], op=ALU.max),
                         r=[("den", 0), ("gt", 4)], w=[("den", 1)])
                    S.op("dve", lambda: nc.vector.reciprocal(out=den[:, 1, :], in_=den[:, 1, :]), r=[("den", 1)], w=[("den", 1)])
                    S.op("dve", lambda pv=pv: nc.vector.tensor_tensor(
                        out=hv[:], in0=pv[:, :, 0:64], in1=den[:, 1, :].unsqueeze(2).broadcast_to([128, 4, 64]), op=ALU.mult),
                        r=[("ps", 3 + b, 0), ("ps", 3 + b, 1), ("den", 1)], w=[("hv",)])
                    S.op("pool", lambda: nc.gpsimd.tensor_tensor(
                        out=hv[:], in0=hv[:], in1=sig[:, :].rearrange("p (h x) -> p h x", x=64), op=ALU.mult),
                        r=[("hv",), ("sig",)], w=[("hv",)])
                    S.op("act", lambda: nc.scalar.activation(out=hsq[:], in_=hv[:], func=AF.Square), r=[("hv",)], w=[("hsq",)])
                    S.op("dve", lambda: nc.vector.tensor_reduce(out=den[:, 2, :], in_=hsq[:], axis=AX.X, op=ALU.add),
                         r=[("hsq",)], w=[("den", 2)])
                    S.op("act", lambda: nc.scalar.activation(out=den[:, 2, :], in_=den[:, 2, :], func=AF.Sqrt,
                                                             bias=self.eps_t[:], scale=1.0 / 64),
                         r=[("den", 2)], w=[("den", 2)])
                    S.op("dve", lambda: nc.vector.reciprocal(out=den[:, 2, :], in_=den[:, 2, :]), r=[("den", 2)], w=[("den", 2)])
                    S.op("dve", lambda: nc.vector.tensor_tensor(
                        out=hv[:], in0=hv[:], in1=den[:, 2, :].unsqueeze(2).broadcast_to([128, 4, 64]), op=ALU.mult),
                        r=[("hv",), ("den", 2)], w=[("hv",)])
                    S.op("pool", lambda b=b: nc.gpsimd.tensor_tensor(
                        out=yb[b][:], in0=hv[:, :, :].rearrange("p h x -> p (h x)"), in1=ng[:], op=ALU.mult),
                        r=[("hv",), ("ng",)], w=[("yb", b)])
                    pt = self.ps[7]
                    ptv = pt[:, 0:128].bitcast(BF16)
                    S.op("pe", [lambda c=c, b=b: nc.tensor.transpose(
                        ptv[:, c * 128:(c + 1) * 128], yb[b][:, c * 128:(c + 1) * 128], ident_b[:]) for c in range(2)],
                        r=[("yb", b), ("ident_b",)], w=[("ps", 7)])
                    S.op("act", lambda b=b: nc.scalar.copy(out=yT[b][:], in_=ptv[:, 0:256].rearrange("p (c t) -> p c t", c=2)),
                         r=[("ps", 7)], w=[("yT", b)])
                    S.op("sp", lambda b=b, i=i: nc.sync.dma_start(
                        out=self.ysT[2:4, :, i * 128:(i + 1) * 128].rearrange("c p t -> p c t"), in_=yT[b][:]),
                        r=[("yT", b)], w=[("d_ys", 2, i)], dma=True)
            except StopIteration:
                pass
            S.barrier()

    def _s5(self, l):
        nc, S = self.nc, self.S
        TC = 512
        TWO_PI = 2.0 * math.pi

        class T:
            def __init__(self, ap, key):
                self.ap, self.key = ap, key

            def __getitem__(self, idx):
                return T(self.ap[idx], self.key)
        E = {"dve": nc.vector, "pool": nc.gpsimd}

        def TT(eng, o, a, b, op):
            S.op(eng, lambda: E[eng].tensor_tensor(out=o.ap, in0=a.ap, in1=b.ap, op=op), r=[a.key, b.key], w=[o.key])

        def TS(o, a, s1, op0, s2=None, op1=None):
            kw = {} if op1 is None else {"op1": op1}
            sv = s1.ap if isinstance(s1, T) else s1
            rk = [a.key] + ([s1.key] if isinstance(s1, T) else [])
            S.op("dve", lambda: nc.vector.tensor_scalar(out=o.ap, in0=a.ap, scalar1=sv, scalar2=s2, op0=op0, **kw),
                 r=rk, w=[o.key])

        def STT(o, a, sc, b, op0, op1):
            sv = sc.ap if isinstance(sc, T) else sc
            rk = [a.key, b.key] + ([sc.key] if isinstance(sc, T) else [])
            S.op("dve", lambda: nc.vector.scalar_tensor_tensor(out=o.ap, in0=a.ap, scalar=sv, in1=b.ap, op0=op0, op1=op1),
                 r=rk, w=[o.key])

        def ACT(o, a, func, scale=1.0, bias=None):
            kw = {} if bias is None else {"bias": bias}
            S.op("act", lambda: nc.scalar.activation(out=o.ap, in_=a.ap, func=func, scale=scale, **kw), r=[a.key], w=[o.key])

        def CP(eng, o, a):
            if eng == "act":
                S.op("act", lambda: nc.scalar.copy(out=o.ap, in_=a.ap), r=[a.key], w=[o.key])
            else:
                S.op(eng, lambda: E[eng].tensor_copy(out=o.ap, in_=a.ap), r=[a.key], w=[o.key])

        def LD(o, src):
            S.op("sp", lambda: nc.sync.dma_start(out=o.ap, in_=src), w=[o.key], dma=True)

        with contextlib.ExitStack() as ph:
            def mk(name, shape, dt=F32):
                return T(self.sb(ph, "s5_" + name, shape, dt)[:], ("s5", name))
            cosT = mk("cosT", [128, 8, TC])
            sinT = mk("sinT", [128, 8, TC])
            rho = mk("rho", [128, 8])
            crot = mk("crot", [128, 8])
            srot = mk("srot", [128, 8])
            Wre = mk("Wre", [128, 8, 128], BF16)
            Wim = mk("Wim", [128, 8, 128], BF16)
            Cre = mk("Cre", [128, 8, 128], BF16)
            Cim = mk("Cim", [128, 8, 128], BF16)
            dg = mk("dg", [128, 2, 128], BF16)
            wg = mk("wg", [128, 2, 512], BF16)
            with contextlib.ExitStack() as pp:
                def mp(name, shape, dt=F32):
                    return T(self.sb(pp, "s5p_" + name, shape, dt)[:], ("s5p", name))
                are, aim, ls, dtt, th, thn = [mp(n, [128, 8]) for n in ("are", "aim", "ls", "dt", "th", "thn")]
                cth, sth, lre, lim, den, xr, zre, zim, t8a, t8b = [mp(n, [128, 8]) for n in
                                                                  ("cth", "sth", "lre", "lim", "den", "xr", "zre", "zim", "t8a", "t8b")]
                w8 = mp("w8", [128, 8])
                i8 = mp("i8", [128, 8], mybir.dt.int32)
                f8 = mp("f8", [128, 8])
                g8 = mp("g8", [128, 8])
                jrow = mp("jrow", [128, TC])
                Wb = mp("Wb", [128, 8 * TC])
                Ib = mp("Ib", [128, 8 * TC], mybir.dt.int32)
                Fb = mp("Fb", [128, 8 * TC])
                bre = mp("bre", [128, 8, 16])
                bim = mp("bim", [128, 8, 16])
                bbre = mp("bbre", [128, 8, 16])
                bbim = mp("bbim", [128, 8, 16])
                tb = mp("tb", [128, 8, 16])
                Ex = mp("Ex", [128, 128])
                dcol = mp("dcol", [128, 2])
                wg_st = mp("wg_st", [128, 2, 512])
                with nc.allow_non_contiguous_dma(reason="tiny S5 parameter loads"):
                    LD(are, self.s5_a_re[l].rearrange("(k q) p -> (q p) k", q=2))
                    LD(aim, self.s5_a_im[l].rearrange("(k q) p -> (q p) k", q=2))
                    for q in range(2):
                        S.op("sp", lambda q=q: nc.sync.dma_start(
                            out=ls.ap[64 * q:64 * q + 64, :],
                            in_=self.s5_log_step[l].rearrange("(k q) -> q k", q=2)[q].partition_broadcast(64)),
                            w=[("s5p", "ls", q)], dma=True)
                    LD(bre, self.s5_b_re[l].rearrange("(k q) p h -> (q p) k h", q=2))
                    LD(bim, self.s5_b_im[l].rearrange("(k q) p h -> (q p) k h", q=2))
                    LD(dcol, self.s5_d[l].rearrange("(c p) -> p c", p=128))
                LD(jrow, self.c_iota.partition_broadcast(128))
                LD(wg_st, self.s5_glu[l].rearrange("(c p) n -> p c n", p=128))
                CP("pool", wg, wg_st)
                ls.key2 = [("s5p", "ls", 0), ("s5p", "ls", 1)]
                S.op("act", lambda: nc.scalar.activation(out=dtt.ap, in_=ls.ap, func=AF.Exp),
                     r=ls.key2, w=[dtt.key])
                TT("dve", t8a, are, dtt, ALU.mult)
                ACT(rho, t8a, AF.Exp)
                TT("dve", th, aim, dtt, ALU.mult)
                TS(thn, th, 1.0 / TWO_PI, ALU.mult)

                def sin_turns(out, w, ib, fb, gb):
                    CP("dve", ib, w)
                    CP("dve", fb, ib)
                    TT("dve", fb, w, fb, ALU.subtract)
                    STT(gb, fb, 0.5, fb, ALU.is_gt, ALU.subtract)
                    STT(fb, gb, 0.5, gb, ALU.is_gt, ALU.subtract)
                    ACT(out, fb, AF.Sin, scale=TWO_PI * (1.0 - 1e-6))

                sin_turns(sth, thn, i8, f8, g8)
                TS(w8, thn, 0.25, ALU.add)
                sin_turns(cth, w8, i8, f8, g8)
                TS(w8, thn, float(TC), ALU.mult)
                sin_turns(srot, w8, i8, f8, g8)
                TS(w8, w8, 0.25, ALU.add)
                sin_turns(crot, w8, i8, f8, g8)
                for k in range(8):
                    TS(T(Wb.ap[:, k * TC:(k + 1) * TC], Wb.key), jrow, thn[:, k:k + 1], ALU.mult)
                sin_turns(T(sinT.ap[:, :, :].rearrange("p k t -> p (k t)"), sinT.key), Wb, Ib, Fb, T(cosT.ap[:, :, :].rearrange("p k t -> p (k t)"), cosT.key))
                TS(Wb, Wb, 0.25, ALU.add)
                Gb = mp("Gb", [128, 8 * TC])
                sin_turns(T(cosT.ap[:, :, :].rearrange("p k t -> p (k t)"), cosT.key), Wb, Ib, Fb, Gb)
                TT("dve", lre, rho, cth, ALU.mult)
                TT("dve", lim, rho, sth, ALU.mult)
                TT("dve", t8a, are, are, ALU.mult)
                TT("dve", t8b, aim, aim, ALU.mult)
                TT("dve", den, t8a, t8b, ALU.add)
                S.op("dve", lambda: nc.vector.reciprocal(out=den.ap, in_=den.ap), r=[den.key], w=[den.key])
                TS(xr, lre, -1.0, ALU.add)
                TT("dve", t8a, xr, are, ALU.mult)
                TT("dve", t8b, lim, aim, ALU.mult)
                TT("dve", t8a, t8a, t8b, ALU.add)
                TT("dve", zre, t8a, den, ALU.mult)
                TT("dve", t8a, lim, are, ALU.mult)
                TT("dve", t8b, xr, aim, ALU.mult)
                TT("dve", t8a, t8a, t8b, ALU.subtract)
                TT("dve", zim, t8a, den, ALU.mult)
                bc = lambda v: T(v.ap[:, :].unsqueeze(2).broadcast_to([128, 8, 16]), v.key)
                TT("dve", bbre, bre, bc(zre), ALU.mult)
                TT("dve", tb, bim, bc(zim), ALU.mult)
                TT("dve", bbre, bbre, tb, ALU.subtract)
                TT("dve", bbim, bim, bc(zre), ALU.mult)
                TT("dve", tb, bre, bc(zim), ALU.mult)
                TT("dve", bbim, bbim, tb, ALU.add)
                pcount = 0
                for (src, dst) in ((bbre, Wre), (bbim, Wim)):
                    for k in range(8):
                        S.op("dve", lambda: nc.vector.memset(Ex.ap, 0.0), w=[Ex.key])
                        for q in range(2):
                            gl = (2 * k + q) % 8
                            CP("dve", T(Ex.ap[64 * q:64 * q + 64, gl * 16:gl * 16 + 16], Ex.key),
                               T(src.ap[64 * q:64 * q + 64, k, :], src.key))
                        pst = self.ps[pcount % 4]
                        pkey = ("ps", pcount % 4)
                        pcount += 1
                        S.op("pe", lambda pst=pst: nc.tensor.transpose(pst[:, 0:128], Ex.ap, self.ident[:]),
                             r=[Ex.key], w=[pkey])
                        S.op("act", lambda pst=pst, dst=dst, k=k: nc.scalar.copy(out=dst.ap[:, k, :], in_=pst[:, 0:128]),
                             r=[pkey], w=[dst.key])
                for (srcd, dst, sgn, nm) in ((self.s5_c_re, Cre, 1.0, "r"), (self.s5_c_im, Cim, -1.0, "i")):
                    ExC = mp("ExC" + nm, [128, 8, 128])
                    S.op("pool", lambda ExC=ExC: nc.gpsimd.memset(ExC.ap, 0.0), w=[ExC.key])
                    for k in range(8):
                        for q in range(2):
                            gl = (2 * k + q) % 8
                            S.op("sp", lambda ExC=ExC, k=k, q=q, gl=gl, srcd=srcd: nc.sync.dma_start(
                                out=ExC.ap[gl * 16:gl * 16 + 16, k, 64 * q:64 * q + 64], in_=srcd[l, 2 * k + q, :, :]),
                                r=[ExC.key], w=[(ExC.key, k, q)], dma=True)
                    for k in range(8):
                        pst = self.ps[pcount % 4]
                        pkey = ("ps", pcount % 4)
                        pcount += 1
                        S.op("pe", lambda pst=pst, ExC=ExC, k=k: nc.tensor.transpose(pst[:, 0:128], ExC.ap[:, k, :], self.ident[:]),
                             r=[ExC.key, (ExC.key, k, 0), (ExC.key, k, 1)], w=[pkey])
                        S.op("act", lambda pst=pst, dst=dst, k=k, sgn=sgn: nc.scalar.activation(
                            out=dst.ap[:, k, :], in_=pst[:, 0:128], func=AF.Copy, scale=sgn), r=[pkey], w=[dst.key])
                for c in range(2):
                    TS(T(dg.ap[:, c, :], dg.key), T(self.ident[:], ("ident",)), dcol[:, c:c + 1], ALU.mult)
                S.barrier()
            with contextlib.ExitStack() as mn:
                def mm_(name, shape, dt=F32):
                    return [T(self.sb(mn, "s5m_%s%d" % (name, i), shape, dt)[:], ("s5m", name, i)) for i in range(2)]
                su = mm_("su", [128, 2, TC], BF16)
                t1, t2, t3, t4 = mm_("t1", [128, TC]), mm_("t2", [128, TC]), mm_("t3", [128, TC]), mm_("t4", [128, TC])
                btr, bti, sr, si = mm_("btr", [128, TC]), mm_("bti", [128, TC]), mm_("sr", [128, TC]), mm_("si", [128, TC])
                u1, u2 = mm_("u1", [128, TC]), mm_("u2", [128, TC])
                sre = mm_("sre", [128, 4, TC], BF16)
                sim_ = mm_("sim", [128, 4, TC], BF16)
                yT = mm_("yT", [128, 2, TC], BF16)
                sg = mm_("sg", [128, TC])
                oo = mm_("oo", [128, TC], BF16)
                init = T(self.sb(mn, "s5m_init", [128, 8, 2], F32)[:], ("s5m", "init"))
                i2 = T(self.sb(mn, "s5m_i2", [128, 8, 2], F32)[:], ("s5m", "i2"))
                S.op("dve", lambda: nc.vector.memset(init.ap, 0.0), w=[init.key])
                it = 0
                for n, (t0, tn) in enumerate(BLOCKS):
                    sb_ = su[n % 2]
                    S.op("sp", lambda sb_=sb_, t0=t0, tn=tn: nc.sync.dma_start(
                        out=sb_.ap[:, :, 0:tn], in_=self.suT[:, :, t0:t0 + tn].rearrange("c p t -> p c t")),
                        w=[sb_.key], dma=True)
                    for jc in range(2):
                        SR, SI = sre[jc], sim_[jc]
                        for kk in range(4):
                            k = 4 * jc + kk
                            b = it % 2
                            it += 1
                            pA, pB = self.ps[2 * b], self.ps[2 * b + 1]
                            kA, kB = ("ps", 2 * b), ("ps", 2 * b + 1)
                            S.op("pe", lambda pA=pA, k=k, jc=jc, sb_=sb_, tn=tn: nc.tensor.matmul(
                                pA[:, 0:tn], Wre.ap[:, k, :], sb_.ap[:, jc, 0:tn], start=True, stop=True),
                                r=[Wre.key, sb_.key], w=[kA])
                            S.op("pe", lambda pB=pB, k=k, jc=jc, sb_=sb_, tn=tn: nc.tensor.matmul(
                                pB[:, 0:tn], Wim.ap[:, k, :], sb_.ap[:, jc, 0:tn], start=True, stop=True),
                                r=[Wim.key, sb_.key], w=[kB])
                            A, Bp = T(pA[:, 0:tn], kA), T(pB[:, 0:tn], kB)
                            ck, sk = cosT[:, k, 0:tn], sinT[:, k, 0:tn]
                            w_ = lambda lst: lst[b][:, 0:tn]
                            TT("dve", w_(t1), A, ck, ALU.mult)
                            TT("dve", w_(t2), Bp, sk, ALU.mult)
                            TT("dve", w_(t3), Bp, ck, ALU.mult)
                            TT("dve", w_(t4), A, sk, ALU.mult)
                            TT("pool", w_(btr), w_(t1), w_(t2), ALU.add)
                            TT("pool", w_(bti), w_(t3), w_(t4), ALU.subtract)
                            for (bt, st, c01) in ((btr, sr, 0), (bti, si, 1)):
                                S.op("dve", lambda bt=bt, st=st, c01=c01, k=k, b=b, tn=tn: nc.vector.tensor_tensor_scan(
                                    out=st[b].ap[:, 0:tn], data0=rho.ap[:, k:k + 1].broadcast_to([128, tn]),
                                    data1=bt[b].ap[:, 0:tn], initial=init.ap[:, k, c01:c01 + 1],
                                    op0=ALU.mult, op1=ALU.add), r=[bt[b].key, init.key, rho.key], w=[st[b].key])
                            TT("pool", w_(u1), w_(sr), ck, ALU.mult)
                            TT("pool", w_(u2), w_(si), sk, ALU.mult)
                            TT("pool", T(SR.ap[:, kk, 0:tn], SR.key), w_(u1), w_(u2), ALU.subtract)
                            TT("pool", w_(u1), w_(sr), sk, ALU.mult)
                            TT("pool", w_(u2), w_(si), ck, ALU.mult)
                            TT("pool", T(SI.ap[:, kk, 0:tn], SI.key), w_(u1), w_(u2), ALU.add)
                            if tn == TC:
                                lr, li = sr[b][:, tn - 1:tn], si[b][:, tn - 1:tn]
                                TS(T(i2.ap[:, k, 0:1], i2.key), lr, crot[:, k:k + 1], ALU.mult)
                                TS(T(i2.ap[:, k, 1:2], i2.key), lr, srot[:, k:k + 1], ALU.mult)
                                TS(T(init.ap[:, k, 0:1], init.key), li, srot[:, k:k + 1], ALU.mult)
                                TT("dve", T(init.ap[:, k, 0:1], init.key), T(i2.ap[:, k, 0:1], i2.key),
                                   T(init.ap[:, k, 0:1], init.key), ALU.subtract)
                                STT(T(init.ap[:, k, 1:2], init.key), li, crot[:, k:k + 1], T(i2.ap[:, k, 1:2], i2.key),
                                    ALU.mult, ALU.add)
                        py = self.ps[4 + jc]
                        pyk = ("ps", 4 + jc)
                        mms = []
                        for kk in range(4):
                            k = 4 * jc + kk
                            mms.append(lambda k=k, kk=kk, tn=tn, py=py, SR=SR: nc.tensor.matmul(
                                py[:, 0:tn], Cre.ap[:, k, :], SR.ap[:, kk, 0:tn], start=(kk == 0), stop=False))
                            mms.append(lambda k=k, kk=kk, tn=tn, py=py, SI=SI: nc.tensor.matmul(
                                py[:, 0:tn], Cim.ap[:, k, :], SI.ap[:, kk, 0:tn], start=False, stop=False))
                        mms.append(lambda jc=jc, tn=tn, py=py, sb_=sb_: nc.tensor.matmul(
                            py[:, 0:tn], dg.ap[:, jc, :], sb_.ap[:, jc, 0:tn], start=False, stop=True))
                        S.op("pe", mms, r=[Cre.key, Cim.key, SR.key, SI.key, dg.key, sb_.key], w=[pyk])
                        yb_ = yT[n % 2]
                        S.op("act", lambda jc=jc, tn=tn, py=py, yb_=yb_: nc.scalar.copy(
                            out=yb_.ap[:, jc, 0:tn], in_=py[:, 0:tn]), r=[pyk], w=[(yb_.key, jc)])
                    yb_ = yT[n % 2]
                    for c in range(2):
                        pa, pg_ = self.ps[6], self.ps[7]
                        S.op("pe", [lambda jc=jc, c=c, tn=tn: nc.tensor.matmul(
                            pa[:, 0:tn], wg.ap[:, jc, c * 128:(c + 1) * 128], yb_.ap[:, jc, 0:tn],
                            start=(jc == 0), stop=(jc == 1)) for jc in range(2)],
                            r=[wg.key, (yb_.key, 0), (yb_.key, 1)], w=[("ps", 6)])
                        S.op("pe", [lambda jc=jc, c=c, tn=tn: nc.tensor.matmul(
                            pg_[:, 0:tn], wg.ap[:, jc, 256 + c * 128:256 + (c + 1) * 128], yb_.ap[:, jc, 0:tn],
                            start=(jc == 0), stop=(jc == 1)) for jc in range(2)],
                            r=[wg.key, (yb_.key, 0), (yb_.key, 1)], w=[("ps", 7)])
                        sgb, oob = sg[c], oo[c]
                        S.op("act", lambda tn=tn, sgb=sgb: nc.scalar.activation(
                            out=sgb.ap[:, 0:tn], in_=pg_[:, 0:tn], func=AF.Sigmoid), r=[("ps", 7)], w=[sgb.key])
                        S.op("dve", lambda tn=tn, sgb=sgb, oob=oob: nc.vector.tensor_tensor(
                            out=oob.ap[:, 0:tn], in0=pa[:, 0:tn], in1=sgb.ap[:, 0:tn], op=ALU.mult),
                            r=[("ps", 6), sgb.key], w=[oob.key])
                        S.op("sp", lambda c=c, t0=t0, tn=tn, oob=oob: nc.sync.dma_start(
                            out=self.ysT[4 + c, :, t0:t0 + tn], in_=oob.ap[:, 0:tn]), r=[oob.key], w=[("d_ys", 4 + c, n)],
                            dma=True)
                S.barrier()

    def _mla(self, l):
        nc, S = self.nc, self.S
        SCALE = 96.0 ** -0.5
        with contextlib.ExitStack() as ph:
            sb = lambda name, shape, dt: self.sb(ph, name, shape, dt)
            cqn = sb("cqn", [128, 3, L], BF16)
            ckvn = sb("ckvn", [128, 2, L], BF16)
            krr = sb("krr", [96, L], BF16)
            vaug = sb("vaug", [128, NT, 8, 65], BF16)
            otm = sb("otm_a", [128, NT, 512], BF16)
            qg = sb("qg", [128, 3], F32)
            kvg = sb("kvg", [128, 2], F32)
            psw = sb("psw", [96, 32], F32)
            msk = sb("msk", [128, 2, 128], BF16)
            ident_b = sb("ident_b", [128, 128], BF16)
            wq_st = sb("wq_st", [128, 3, 768], F32)
            wq = sb("wq", [128, 3, 8, 96], BF16)
            wqs = sb("wqs", [128, 3, 8, 96], BF16)
            wkv_st = sb("wkv_st", [128, 2, 1024], F32)
            wkv = sb("wkv", [128, 2, 1024], BF16)
            with nc.allow_non_contiguous_dma(reason="tiny norm gain loads"):
                S.op("sp", lambda: nc.sync.dma_start(out=qg[:], in_=self.mla_q_norm[l].rearrange("(c p) -> p c", p=128)),
                     w=[("qg",)], dma=True)
                S.op("sp", lambda: nc.sync.dma_start(out=kvg[:], in_=self.mla_kv_norm[l].rearrange("(c p) -> p c", p=128)),
                     w=[("kvg",)], dma=True)
            S.op("sp", lambda: nc.sync.dma_start(out=psw[64:96, :], in_=self.c_psw[:, :]), w=[("psw",)], dma=True)
            S.op("sp", lambda: nc.sync.dma_start(out=msk[:], in_=self.c_mask.rearrange("m p q -> p m q")),
                 w=[("msk",)], dma=True)
            S.op("dve", lambda: nc.vector.tensor_copy(out=ident_b[:], in_=self.ident[:]), w=[("ident_b",)])
            S.op("sp", lambda: nc.sync.dma_start(out=wq_st[:], in_=self.mla_w_uq[l].rearrange("(c p) n -> p c n", p=128)),
                 w=[("wq_st",)], dma=True)
            S.op("sp", lambda: nc.sync.dma_start(out=wkv_st[:], in_=self.mla_w_ukv[l].rearrange("(c p) n -> p c n", p=128)),
                 w=[("wkv_st",)], dma=True)
            S.op("pool", lambda: nc.gpsimd.tensor_copy(out=wkv[:], in_=wkv_st[:]), r=[("wkv_st",)], w=[("wkv",)])
            wq_v = wq_st[:, :, :].rearrange("p c (h x) -> p c h x", x=96)
            S.op("pool", lambda: nc.gpsimd.tensor_copy(out=wq[:], in_=wq_v), r=[("wq_st",)], w=[("wq",)])
            S.op("pool", lambda: nc.gpsimd.memset(wqs[:], 0.0), w=[("wqs",)])
            S.op("pool", lambda: nc.gpsimd.tensor_scalar(out=wqs[:, :, :, 64:80], in0=wq_v[:, :, :, 80:96],
                                                         scalar1=-1.0, scalar2=1.0, op0=ALU.mult, op1=ALU.mult),
                 r=[("wq_st",)], w=[("wqs",)])
            S.op("pool", lambda: nc.gpsimd.tensor_copy(out=wqs[:, :, :, 80:96], in_=wq_v[:, :, :, 64:80]),
                 r=[("wq_st",)], w=[("wqs",)])
            S.op("dve", lambda: nc.vector.memset(vaug[:, :, :, 64:65], 1.0), w=[("vaug1",)])

            with contextlib.ExitStack() as st1:
                lat = [self.sb(st1, "lat%d" % i, [128, 5, 512], F32) for i in range(2)]
                sq = self.sb(st1, "sq_a", [128, 5, 512], F32)
                rs = [self.sb(st1, "rs%d" % i, [128, 512], F32) for i in range(2)]
                krb = [self.sb(st1, "krb%d" % i, [96, 512], F32) for i in range(2)]
                cs = [self.sb(st1, "cs%d" % i, [96, 2, 512], F32) for i in range(2)]
                t1 = self.sb(st1, "t1", [96, 512], F32)
                t2 = self.sb(st1, "t2", [96, 512], F32)
                for n, (t0, tn) in enumerate(BLOCKS):
                    b = n % 2
                    S.op("sp", lambda b=b, t0=t0, tn=tn: nc.sync.dma_start(
                        out=lat[b][:, 0:3, 0:tn], in_=self.cqT[:, :, t0:t0 + tn].rearrange("c p t -> p c t")),
                        w=[("lat", b, 0)], dma=True)
                    S.op("sp", lambda b=b, t0=t0, tn=tn: nc.sync.dma_start(
                        out=lat[b][:, 3:5, 0:tn], in_=self.ckvT[:, :, t0:t0 + tn].rearrange("c p t -> p c t")),
                        w=[("lat", b, 1)], dma=True)
                    S.op("sp", lambda b=b, t0=t0, tn=tn: nc.sync.dma_start(
                        out=krb[b][64:96, 0:tn], in_=self.krT[:, t0:t0 + tn]), w=[("krb", b)], dma=True)
                    S.op("sp", lambda b=b, t0=t0, tn=tn: nc.sync.dma_start(
                        out=cs[b][64:96, 0, 0:tn], in_=self.c_cos[:, t0:t0 + tn]), w=[("cs", b, 0)], dma=True)
                    S.op("sp", lambda b=b, t0=t0, tn=tn: nc.sync.dma_start(
                        out=cs[b][64:96, 1, 0:tn], in_=self.c_sin[:, t0:t0 + tn]), w=[("cs", b, 1)], dma=True)
                    self.rms_rstd(lat[b][:, 0:3, :], 3, 384, tn, sq[:, 0:3, :], rs[0], 6, [("lat", b, 0)], "q")
                    for c in range(3):
                        S.op("dve", lambda b=b, c=c, t0=t0, tn=tn: nc.vector.scalar_tensor_tensor(
                            out=cqn[:, c, t0:t0 + tn], in0=lat[b][:, c, 0:tn], scalar=qg[:, c:c + 1],
                            in1=rs[0][:, 0:tn], op0=ALU.mult, op1=ALU.mult),
                            r=[("lat", b, 0), ("rstd", "q"), ("qg",)], w=[("cqn", n)])
                    self.rms_rstd(lat[b][:, 3:5, :], 2, 256, tn, sq[:, 3:5, :], rs[1], 7, [("lat", b, 1)], "kv")
                    for c in range(2):
                        S.op("dve", lambda b=b, c=c, t0=t0, tn=tn: nc.vector.scalar_tensor_tensor(
                            out=ckvn[:, c, t0:t0 + tn], in0=lat[b][:, 3 + c, 0:tn], scalar=kvg[:, c:c + 1],
                            in1=rs[1][:, 0:tn], op0=ALU.mult, op1=ALU.mult),
                            r=[("lat", b, 1), ("rstd", "kv"), ("kvg",)], w=[("ckvn", n)])
                    pst = self.ps[4 + b]
                    S.op("pe", lambda b=b, tn=tn, pst=pst: nc.tensor.matmul(
                        pst[64:96, 0:tn], psw[64:96, :], krb[b][64:96, 0:tn], start=True, stop=True),
                        r=[("krb", b), ("psw",)], w=[("ps", 4 + b)])
                    S.op("dve", lambda b=b, tn=tn, pst=pst: nc.vector.tensor_tensor(
                        out=t1[64:96, 0:tn], in0=pst[64:96, 0:tn], in1=cs[b][64:96, 1, 0:tn], op=ALU.mult),
                        r=[("ps", 4 + b), ("cs", b, 1)], w=[("t1",)])
                    S.op("pool", lambda b=b, tn=tn: nc.gpsimd.tensor_tensor(
                        out=t2[64:96, 0:tn], in0=krb[b][64:96, 0:tn], in1=cs[b][64:96, 0, 0:tn], op=ALU.mult),
                        r=[("krb", b), ("cs", b, 0)], w=[("t2",)])
                    S.op("dve", lambda t0=t0, tn=tn: nc.vector.tensor_tensor(
                        out=krr[64:96, t0:t0 + tn], in0=t1[64:96, 0:tn], in1=t2[64:96, 0:tn], op=ALU.add),
                        r=[("t1",), ("t2",)], w=[("krr", n)])
                    for i in range(t0 // 128, (t0 + tn) // 128):
                        pi = i % 4
                        pst = self.ps[pi]
                        S.op("pe", [lambda c=c, i=i, pst=pst: nc.tensor.matmul(
                            pst[:, 0:512], ckvn[:, c, i * 128:(i + 1) * 128],
                            wkv[:, c, :].rearrange("p (h x) -> p h x", x=128)[:, :, 64:128],
                            start=(c == 0), stop=(c == 1)) for c in range(2)],
                            r=[("ckvn", n), ("wkv",)], w=[("ps", pi)])
                        S.op("act", lambda i=i, pst=pst: nc.scalar.copy(
                            out=vaug[:, i, :, 0:64], in_=pst[:, 0:512].rearrange("p (h x) -> p h x", x=64)),
                            r=[("ps", pi)], w=[("vaug", i)])
                S.barrier()

            with contextlib.ExitStack() as st2:
                QT = [self.sb(st2, "QT%d" % i, [96, L], BF16) for i in range(2)]
                KT = [self.sb(st2, "KT%d" % i, [96, L], BF16) for i in range(2)]
                pT = [self.sb(st2, "pT%d" % i, [128, 512], BF16) for i in range(4)]
                cs = [self.sb(st2, "cs2_%d" % i, [96, 2, 512], F32) for i in range(2)]
                t1 = self.sb(st2, "t1b", [96, 512], F32)
                rec = self.sb(st2, "rec", [128, 8], F32)
                pcnt = [0]
                rcnt = [0]

                def build_qk(h, n):
                    hb = h % 2
                    t0, tn = BLOCKS[n]
                    b = n % 2
                    S.op("sp", lambda: nc.sync.dma_start(
                        out=cs[b][64:96, 0, 0:tn], in_=self.c_cos[:, t0:t0 + tn]), w=[("cs", b, 0)], dma=True)
                    S.op("sp", lambda: nc.sync.dma_start(
                        out=cs[b][64:96, 1, 0:tn], in_=self.c_sin[:, t0:t0 + tn]), w=[("cs", b, 1)], dma=True)
                    pq, pqs = self.ps[6], self.ps[7]
                    S.op("pe", [lambda c=c: nc.tensor.matmul(
                        pq[0:96, 0:tn], wq[:, c, h, :], cqn[:, c, t0:t0 + tn], start=(c == 0), stop=(c == 2))
                        for c in range(3)], r=[("wq",)], w=[("ps", 6)])
                    S.op("pe", [lambda c=c: nc.tensor.matmul(
                        pqs[0:96, 0:tn], wqs[:, c, h, :], cqn[:, c, t0:t0 + tn], start=(c == 0), stop=(c == 2))
                        for c in range(3)], r=[("wqs",)], w=[("ps", 7)])
                    S.op("act", lambda: nc.scalar.copy(
                        out=QT[hb][0:64, t0:t0 + tn], in_=pq[0:64, 0:tn]), r=[("ps", 6)], w=[("QT", hb, n)])
                    S.op("dve", lambda: nc.vector.tensor_tensor(
                        out=t1[64:96, 0:tn], in0=pqs[64:96, 0:tn], in1=cs[b][64:96, 1, 0:tn], op=ALU.mult),
                        r=[("ps", 7), ("cs", b, 1)], w=[("t1",)])
                    S.op("dve", lambda: nc.vector.tensor_tensor(
                        out=cs[b][64:96, 0, 0:tn], in0=pq[64:96, 0:tn], in1=cs[b][64:96, 0, 0:tn], op=ALU.mult),
                        r=[("ps", 6), ("cs", b, 0)], w=[("cs", b, 0)])
                    S.op("dve", lambda: nc.vector.tensor_tensor(
                        out=QT[hb][64:96, t0:t0 + tn], in0=t1[64:96, 0:tn], in1=cs[b][64:96, 0, 0:tn], op=ALU.add),
                        r=[("t1",), ("cs", b, 0)], w=[("QT", hb, n, 1)])
                    pk = self.ps[6]
                    S.op("pe", [lambda c=c: nc.tensor.matmul(
                        pk[0:64, 0:tn], wkv[:, c, h * 128:h * 128 + 64], ckvn[:, c, t0:t0 + tn],
                        start=(c == 0), stop=(c == 1)) for c in range(2)], r=[("wkv",)], w=[("ps", 6)])
                    S.op("act", lambda: nc.scalar.copy(
                        out=KT[hb][0:64, t0:t0 + tn], in_=pk[0:64, 0:tn]), r=[("ps", 6)], w=[("KT", hb, n)])
                    S.op("pool", lambda: nc.gpsimd.tensor_copy(
                        out=KT[hb][64:96, t0:t0 + tn], in_=krr[64:96, t0:t0 + tn]), w=[("KT", hb, n, 1)])

                def sweep(h, n):
                    hb = h % 2
                    t0, tn = BLOCKS[n]
                    qts = list(range(t0 // 128, (t0 + tn) // 128))
                    kt_max = min(qts[-1] + 1, NT - 1)

                    def score(kt):
                        pi = pcnt[0] % 2
                        pbi = pcnt[0] % 4
                        pcnt[0] += 1
                        pss = self.ps[4 + pi]
                        n_k = kt // 4
                        S.op("pe", lambda: nc.tensor.matmul(
                            pss[:, 0:tn], KT[hb][:, kt * 128:(kt + 1) * 128], QT[hb][:, t0:t0 + tn],
                            start=True, stop=True),
                            r=[("KT", hb, n_k), ("KT", hb, n_k, 1), ("QT", hb, n), ("QT", hb, n, 1)],
                            w=[("ps", 4 + pi)])
                        return (pi, pbi)
                    nxt = score(0)
                    for kt in range(kt_max + 1):
                        pi, pbi = nxt
                        if kt + 1 <= kt_max:
                            nxt = score(kt + 1)
                        pss = self.ps[4 + pi]
                        pb = pT[pbi]
                        pkey = ("pT", pbi)
                        S.op("act", lambda pss=pss, pb=pb: nc.scalar.activation(
                            out=pb[:, 0:tn], in_=pss[:, 0:tn], func=AF.Exp, scale=SCALE),
                            r=[("ps", 4 + pi)], w=[pkey])
                        for j, qt in enumerate(qts):
                            if kt == qt or kt == qt + 1:
                                m = 0 if kt == qt else 1
                                S.op("pool", lambda j=j, m=m, pb=pb: nc.gpsimd.tensor_tensor(
                                    out=pb[:, j * 128:(j + 1) * 128], in0=pb[:, j * 128:(j + 1) * 128],
                                    in1=msk[:, m, :], op=ALU.mult), r=[pkey, ("msk",)], w=[pkey])
                        for j, qt in enumerate(qts):
                            if kt > qt + 1:
                                continue
                            last = (kt == min(qt + 1, NT - 1))
                            S.op("pe", lambda j=j, kt=kt, pb=pb, last=last: nc.tensor.matmul(
                                self.ps[j][:, 0:65], pb[:, j * 128:(j + 1) * 128], vaug[:, kt, h, :],
                                start=(kt == 0), stop=last),
                                r=[pkey, ("vaug", kt), ("vaug1",)], w=[("ps", j)])
                            if last:
                                rc = rec[:, rcnt[0] % 8:rcnt[0] % 8 + 1]
                                rkey = ("rec", rcnt[0] % 8)
                                rcnt[0] += 1
                                S.op("dve", lambda j=j, rc=rc: nc.vector.reciprocal(out=rc, in_=self.ps[j][:, 64:65]),
                                     r=[("ps", j)], w=[rkey])
                                S.op("dve", lambda j=j, rc=rc, qt=qt: nc.vector.tensor_scalar(
                                    out=otm[:, qt, h * 64:(h + 1) * 64], in0=self.ps[j][:, 0:64], scalar1=rc,
                                    scalar2=None, op0=ALU.mult), r=[("ps", j), rkey], w=[("otm", qt, h)])

                for n in range(len(BLOCKS)):
                    build_qk(0, n)
                for h in range(8):
                    for n in range(len(BLOCKS)):
                        sweep(h, n)
                        if h + 1 < 8:
                            build_qk(h + 1, n)
                ob = [self.sb(st2, "oT%d" % i, [128, 4, 128], BF16) for i in range(2)]
                for i in range(NT):
                    pst = self.ps[4 + i % 2]
                    pv = pst[:, 0:256].bitcast(BF16) if hasattr(pst[:, 0:256], "bitcast") else None
                    S.op("pe", [lambda c=c, i=i, pv=pv: nc.tensor.transpose(
                        pv[:, c * 128:(c + 1) * 128], otm[:, i, c * 128:(c + 1) * 128], ident_b[:]) for c in range(4)],
                        r=[("otm", i, hh) for hh in range(8)] + [("ident_b",)], w=[("ps", 4 + i % 2)])
                    S.op("act", lambda i=i, pv=pv: nc.scalar.copy(
                        out=ob[i % 2][:, :, :], in_=pv[:, 0:512].rearrange("p (c t) -> p c t", c=4)),
                        r=[("ps", 4 + i % 2)], w=[("oT", i % 2)])
                    S.op("sp", lambda i=i: nc.sync.dma_start(
                        out=self.ysT[6:10, :, i * 128:(i + 1) * 128].rearrange("c p t -> p c t"), in_=ob[i % 2][:]),
                        r=[("oT", i % 2)], w=[("d_ys", 6, i)], dma=True)
                S.barrier()

    def _fm_tiles(self):
        def dest(col):
            if col < C_CB:
                return (self.gT, col // 128, "sig")
            if col < C_MQ:
                return (self.convT, (col - C_CB) // 128, "f32")
            if col < C_MV:
                return (self.mqkT, (col - C_MQ) // 128, "bf16")
            if col < C_CQ:
                return (self.suT, (col - C_SU) // 128, "bf16")
            if col < C_CKV:
                return (self.cqT, (col - C_CQ) // 128, "f32")
            if col < C_KR:
                return (self.ckvT, (col - C_CKV) // 128, "f32")
            return (self.krT, 0, "f32")
        groups = []
        for (a, b) in [(0, C_MV), (C_SU, IN_W)]:
            c0 = a
            while c0 < b:
                n = min(512, b - c0)
                tiles = []
                c = c0
                while c < c0 + n:
                    wdt = min(128, c0 + n - c)
                    tiles.append((c, wdt) + dest(c))
                    c += wdt
                groups.append((c0, n, tiles))
                c0 += n
        return groups

    def _inproj(self, l):
        nc, S = self.nc, self.S
        with contextlib.ExitStack() as ph:
            hn = self.sb(ph, "hn", [128, 8, L], BF16)
            hb = [self.sb(ph, "hb%d" % i, [128, 8, 512], F32) for i in range(2)]
            sq = [self.sb(ph, "sq0", [128, 8, 512], F32)] * 2
            rstd = [self.sb(ph, "rstd%d" % i, [128, 512], F32) for i in range(2)]
            wst = [self.sb(ph, "wst%d" % i, [128, 8, 512], F32) for i in range(2)]
            wbf = [self.sb(ph, "wbf%d" % i, [128, 8, 512], BF16) for i in range(2)]
            wtm = self.sb(ph, "wtm", [128, 8, TM_W], BF16)
            ob16 = [self.sb(ph, "ob16_%d" % i, [128, 512], BF16) for i in range(4)]
            of32 = [self.sb(ph, "of32_%d" % i, [128, 512], F32) for i in range(4)]
            otm = [self.sb(ph, "otm%d" % i, [128, TM_W], F32) for i in range(2)]
            w_l = self.w_in[l].rearrange("(c p) n -> p c n", p=128)
            groups = self._fm_tiles()

            def load_w(gi):
                c0, n, _ = groups[gi]
                b = gi % 2
                S.op("sp", lambda: nc.sync.dma_start(out=wst[b][:, :, 0:n], in_=w_l[:, :, c0:c0 + n]),
                     w=[("wst", b)], dma=True)
                S.op("pool", lambda: nc.gpsimd.tensor_copy(out=wbf[b][:, :, 0:n], in_=wst[b][:, :, 0:n]),
                     r=[("wst", b)], w=[("wbf", b)])

            for b, (ca, cw) in enumerate([(0, 512), (512, TM_W - 512)]):
                S.op("sp", lambda b=b, ca=ca, cw=cw: nc.sync.dma_start(
                    out=wst[b][:, :, 0:cw], in_=w_l[:, :, TM_C0 + ca:TM_C0 + ca + cw]), w=[("wst", b)], dma=True)
                S.op("pool", lambda b=b, ca=ca, cw=cw: nc.gpsimd.tensor_copy(
                    out=wtm[:, :, ca:ca + cw], in_=wst[b][:, :, 0:cw]), r=[("wst", b)], w=[("wtm", b)])
            load_w(0)

            for n, (t0, tn) in enumerate(BLOCKS):
                b = n % 2
                S.op("sp", lambda b=b, t0=t0, tn=tn: nc.sync.dma_start(
                    out=hb[b][:, :, 0:tn], in_=self.hT[:, :, t0:t0 + tn].rearrange("c p t -> p c t")),
                    r=[("d_hT", n)], w=[("hb", b)], dma=True)
                S.op("act", lambda b=b, tn=tn: nc.scalar.activation(
                    out=sq[b][:, :, 0:tn], in_=hb[b][:, :, 0:tn], func=AF.Square), r=[("hb", b)], w=[("sq", 0)])
                pst = self.ps[6 + b]
                S.op("pe", [lambda c=c, b=b, tn=tn, pst=pst: nc.tensor.matmul(
                    pst[:, 0:tn], self.ones_f[:], sq[b][:, c, 0:tn], start=(c == 0), stop=(c == 7))
                    for c in range(8)], r=[("sq", 0)], w=[("ps", 6 + b)])
                S.op("act", lambda b=b, tn=tn, pst=pst: nc.scalar.activation(
                    out=rstd[b][:, 0:tn], in_=pst[:, 0:tn], func=AF.Sqrt, bias=self.eps_t[:], scale=1.0 / D),
                    r=[("ps", 6 + b)], w=[("rstd", b)])
                S.op("dve", lambda b=b, tn=tn: nc.vector.reciprocal(out=rstd[b][:, 0:tn], in_=rstd[b][:, 0:tn]),
                     r=[("rstd", b)], w=[("rstd", b)])
                for c in range(8):
                    S.op("dve", lambda b=b, c=c, t0=t0, tn=tn: nc.vector.scalar_tensor_tensor(
                        out=hn[:, c, t0:t0 + tn], in0=hb[b][:, c, 0:tn], scalar=self.gain(l, 0, c),
                        in1=rstd[b][:, 0:tn], op0=ALU.mult, op1=ALU.mult),
                        r=[("hb", b), ("rstd", b)], w=[("hn", n)])

            pcount = 0
            for i in range(NT):
                ob = otm[i % 2]
                for (ca, cw) in [(0, 512), (512, TM_W - 512)]:
                    pi = pcount % 6
                    pcount += 1
                    pst = self.ps[pi]
                    S.op("pe", [lambda c=c, i=i, ca=ca, cw=cw, pst=pst: nc.tensor.matmul(
                        pst[:, 0:cw], hn[:, c, i * 128:(i + 1) * 128], wtm[:, c, ca:ca + cw],
                        start=(c == 0), stop=(c == 7)) for c in range(8)],
                        r=[("hn", i // 4), ("wtm", 0), ("wtm", 1)], w=[("ps", pi)])
                    if ca == 0:
                        S.op("act", lambda ob=ob, pst=pst, ca=ca, cw=cw: nc.scalar.copy(
                            out=ob[:, ca:ca + cw], in_=pst[:, 0:cw]), r=[("ps", pi)], w=[("otm", i % 2, 0)])
                    else:
                        S.op("dve", lambda ob=ob, pst=pst, ca=ca, cw=cw: nc.vector.tensor_copy(
                            out=ob[:, ca:ca + cw], in_=pst[:, 0:cw]), r=[("ps", pi)], w=[("otm", i % 2, 1)])
                S.op("sp", lambda ob=ob, i=i: nc.sync.dma_start(out=self.mTM[i * 128:(i + 1) * 128, :], in_=ob[:]),
                     r=[("otm", i % 2, 0), ("otm", i % 2, 1)], w=[("d_mTM", i)], dma=True)

            o16c = 0
            o32c = 0
            for gi, (c0, ncols, tiles) in enumerate(groups):
                if gi + 1 < len(groups):
                    load_w(gi + 1)
                wb = wbf[gi % 2]
                for (col, cw, dst, chunk, kind) in tiles:
                    off = col - c0
                    for n, (t0, tn) in enumerate(BLOCKS):
                        pi = pcount % 6
                        pcount += 1
                        pst = self.ps[pi]
                        S.op("pe", [lambda c=c, off=off, cw=cw, t0=t0, tn=tn, pst=pst, wb=wb: nc.tensor.matmul(
                            pst[0:cw, 0:tn], wb[:, c, off:off + cw], hn[:, c, t0:t0 + tn],
                            start=(c == 0), stop=(c == 7)) for c in range(8)],
                            r=[("hn", n), ("wbf", gi % 2)], w=[("ps", pi)])
                        if kind == "f32":
                            bi = o32c % 4
                            o32c += 1
                            ob = of32[bi]
                            okey = ("of32", bi)
                            S.op("dve", lambda ob=ob, pst=pst, cw=cw, tn=tn: nc.vector.tensor_copy(
                                out=ob[0:cw, 0:tn], in_=pst[0:cw, 0:tn]), r=[("ps", pi)], w=[okey])
                        else:
                            bi = o16c % 4
                            o16c += 1
                            ob = ob16[bi]
                            okey = ("ob16", bi)
                            fn = AF.Sigmoid if kind == "sig" else AF.Copy
                            S.op("act", lambda ob=ob, pst=pst, cw=cw, tn=tn, fn=fn: nc.scalar.activation(
                                out=ob[0:cw, 0:tn], in_=pst[0:cw, 0:tn], func=fn), r=[("ps", pi)], w=[okey])
                        if dst is self.krT:
                            dap = dst[:, t0:t0 + tn]
                        else:
                            dap = dst[chunk, :, t0:t0 + tn]
                        S.op("sp", lambda ob=ob, dap=dap, cw=cw, tn=tn: nc.sync.dma_start(
                            out=dap, in_=ob[0:cw, 0:tn]), r=[okey], w=[("d_fm", col, n)], dma=True)
            S.barrier()

    def _merge(self, l):
        nc, S = self.nc, self.S
        with contextlib.ExitStack() as ph:
            sb = lambda name, shape, dt: self.sb(ph, name, shape, dt)
            wbr = sb("wbr", [128, 10, D], BF16)
            wo = sb("wo", [128, 8, D], BF16)
            wst = [sb("mwst%d" % i, [128, D], F32) for i in range(2)]
            ys = [sb("mys%d" % i, [128, 10, 512], BF16) for i in range(2)]
            gt = [sb("mgt%d" % i, [128, 4, 512], BF16) for i in range(3)]
            tb = [sb("mtb%d" % i, [128, 512], F32) for i in range(4)]
            mg = sb("mmg", [128, 8, 512], BF16)
            hb = [sb("mhb%d" % i, [128, 8, 512], F32) for i in range(2)]
            o2 = sb("mo2", [128, 8, 512], F32)
            sq = sb("msq", [128, 8, 512], F32)
            rs = sb("mrs_", [128, 512], F32)
            hn = [sb("mhn%d" % i, [128, 8, 512], BF16) for i in range(2)]
            wc = 0
            for (src, dst, nchunk) in ((self.w_branch[l], wbr, 10), (self.w_out[l], wo, 8)):
                for c in range(nchunk):
                    b = wc % 2
                    wc += 1
                    S.op("sp", lambda b=b, c=c, src=src: nc.sync.dma_start(out=wst[b][:], in_=src[c * 128:(c + 1) * 128, :]),
                         w=[("wst", b)], dma=True)
                    S.op("pool", lambda b=b, c=c, dst=dst: nc.gpsimd.tensor_copy(out=dst[:, c, :], in_=wst[b][:]),
                         r=[("wst", b)], w=[("w", id(dst), c)])
            wbr_keys = [("w", id(wbr), c) for c in range(10)]
            wo_keys = [("w", id(wo), c) for c in range(8)]
            branch_chunks = [(0, 2), (2, 4), (4, 6), (6, 10)]
            gcount = 0
            pcount = 0
            for n, (t0, tn) in enumerate(BLOCKS):
                b = n % 2
                S.op("sp", lambda b=b, t0=t0, tn=tn: nc.sync.dma_start(
                    out=ys[b][:, :, 0:tn], in_=self.ysT[:, :, t0:t0 + tn].rearrange("c p t -> p c t")),
                    w=[("ys", b)], dma=True)
                S.op("sp", lambda b=b, t0=t0, tn=tn: nc.sync.dma_start(
                    out=hb[b][:, :, 0:tn], in_=self.hT[:, :, t0:t0 + tn].rearrange("c p t -> p c t")),
                    w=[("hb", b, c) for c in range(8)], dma=True)
                for dc in range(8):
                    gi = gcount % 3
                    gcount += 1
                    gtb = gt[gi]
                    S.op("sp", lambda gtb=gtb, dc=dc, t0=t0, tn=tn: nc.sync.dma_start(
                        out=gtb[:, :, 0:tn], in_=self.gT[dc:32:8, :, t0:t0 + tn].rearrange("c p t -> p c t")),
                        w=[("gt", gi)], dma=True)
                    for bi, (ca, cb_) in enumerate(branch_chunks):
                        pi = pcount % 8
                        pcount += 1
                        pst = self.ps[pi]
                        S.op("pe", [lambda kc=kc, pst=pst, dc=dc, b=b, tn=tn, ca=ca, cb_=cb_: nc.tensor.matmul(
                            pst[:, 0:tn], wbr[:, kc, dc * 128:(dc + 1) * 128], ys[b][:, kc, 0:tn],
                            start=(kc == ca), stop=(kc == cb_ - 1)) for kc in range(ca, cb_)],
                            r=[("ys", b)] + wbr_keys, w=[("ps", pi)])
                        S.op("dve", lambda pst=pst, bi=bi, gtb=gtb, tn=tn: nc.vector.tensor_tensor(
                            out=tb[bi][:, 0:tn], in0=pst[:, 0:tn], in1=gtb[:, bi, 0:tn], op=ALU.mult),
                            r=[("ps", pi), ("gt", gi)], w=[("tb", bi)])
                    S.op("pool", lambda tn=tn: nc.gpsimd.tensor_tensor(
                        out=tb[0][:, 0:tn], in0=tb[0][:, 0:tn], in1=tb[1][:, 0:tn], op=ALU.add),
                        r=[("tb", 0), ("tb", 1)], w=[("tb", 0)])
                    S.op("pool", lambda tn=tn: nc.gpsimd.tensor_tensor(
                        out=tb[2][:, 0:tn], in0=tb[2][:, 0:tn], in1=tb[3][:, 0:tn], op=ALU.add),
                        r=[("tb", 2), ("tb", 3)], w=[("tb", 2)])
                    S.op("pool", lambda tn=tn, dc=dc: nc.gpsimd.tensor_tensor(
                        out=mg[:, dc, 0:tn], in0=tb[0][:, 0:tn], in1=tb[2][:, 0:tn], op=ALU.add),
                        r=[("tb", 0), ("tb", 2)], w=[("mg", dc)])
                for dc2 in range(8):
                    pi = pcount % 8
                    pcount += 1
                    pst = self.ps[pi]
                    S.op("pe", [lambda dc=dc, pst=pst, dc2=dc2, tn=tn: nc.tensor.matmul(
                        pst[:, 0:tn], wo[:, dc, dc2 * 128:(dc2 + 1) * 128], mg[:, dc, 0:tn],
                        start=(dc == 0), stop=(dc == 7)) for dc in range(8)],
                        r=[("mg", dc) for dc in range(8)] + wo_keys, w=[("ps", pi)])
                    S.op("act", lambda pst=pst, dc2=dc2, tn=tn: nc.scalar.copy(out=o2[:, dc2, 0:tn], in_=pst[:, 0:tn]),
                         r=[("ps", pi)], w=[("o2", dc2)])
                self.rms_rstd(o2, 8, D, tn, sq, rs, 0, [("o2", c) for c in range(8)], "m1")
                pcount = 1
                for c in range(8):
                    S.op("dve", lambda c=c, tn=tn: nc.vector.scalar_tensor_tensor(
                        out=o2[:, c, 0:tn], in0=o2[:, c, 0:tn], scalar=self.gain(l, 1, c), in1=rs[:, 0:tn],
                        op0=ALU.mult, op1=ALU.mult), r=[("o2", c), ("rstd", "m1")], w=[("o2", c)])
                    S.op("pool", lambda c=c, b=b, tn=tn: nc.gpsimd.tensor_tensor(
                        out=hb[b][:, c, 0:tn], in0=hb[b][:, c, 0:tn], in1=o2[:, c, 0:tn], op=ALU.add),
                        r=[("hb", b, c), ("o2", c)], w=[("hb", b, c)])
                hkeys = [("hb", b, c) for c in range(8)]
                S.op("sp", lambda b=b, t0=t0, tn=tn: nc.sync.dma_start(
                    out=self.hT[:, :, t0:t0 + tn].rearrange("c p t -> p c t"), in_=hb[b][:, :, 0:tn]),
                    r=hkeys, w=[("d_hT", n)], dma=True)
                self.rms_rstd(hb[b], 8, D, tn, sq, rs, 0, hkeys, "m2")
                pcount = 1
                for c in range(8):
                    S.op("dve", lambda c=c, b=b, tn=tn: nc.vector.scalar_tensor_tensor(
                        out=hn[b][:, c, 0:tn], in0=hb[b][:, c, 0:tn], scalar=self.gain(l, 2, c), in1=rs[:, 0:tn],
                        op0=ALU.mult, op1=ALU.mult), r=[("hb", b, c), ("rstd", "m2")], w=[("hn", b, c)])
                S.op("sp", lambda b=b, t0=t0, tn=tn: nc.sync.dma_start(
                    out=self.hn2T[:, :, t0:t0 + tn].rearrange("c p t -> p c t"), in_=hn[b][:, :, 0:tn]),
                    r=[("hn", b, c) for c in range(8)], w=[("d_hn2", n)], dma=True)
            S.barrier()

    def _ffn1(self, l):
        nc, S = self.nc, self.S
        with contextlib.ExitStack() as ph:
            sb = lambda name, shape, dt: self.sb(ph, name, shape, dt)
            hn = sb("fhn", [128, 8, L], BF16)
            wst = [sb("fwst%d" % i, [128, 8, 512], F32) for i in range(2)]
            wbf = [sb("fwbf%d" % i, [128, 8, 512], BF16) for i in range(2)]
            rl = [sb("frl%d" % i, [128, 512], F32) for i in range(3)]
            ab = [sb("fab%d" % i, [128, 512], BF16) for i in range(4)]
            w_l = self.mlp_w1[l].rearrange("(c p) n -> p c n", p=128)
            for n, (t0, tn) in enumerate(BLOCKS):
                S.op("sp", lambda t0=t0, tn=tn: nc.sync.dma_start(
                    out=hn[:, :, t0:t0 + tn], in_=self.hn2T[:, :, t0:t0 + tn].rearrange("c p t -> p c t")),
                    w=[("hn", n)], dma=True)

            def load_w(gi):
                b = gi % 2
                S.op("sp", lambda: nc.sync.dma_start(out=wst[b][:], in_=w_l[:, :, gi * 512:(gi + 1) * 512]),
                     w=[("wst", b)], dma=True)
                S.op("pool", lambda: nc.gpsimd.tensor_copy(out=wbf[b][:], in_=wst[b][:]), r=[("wst", b)], w=[("wbf", b)])
            load_w(0)
            pcount = 0
            cnt = 0
            for gi in range(8):
                if gi + 1 < 8:
                    load_w(gi + 1)
                wb = wbf[gi % 2]
                for ti in range(4):
                    fc = gi * 4 + ti
                    for n, (t0, tn) in enumerate(BLOCKS):
                        pi = pcount % 8
                        pcount += 1
                        pst = self.ps[pi]
                        S.op("pe", [lambda c=c, ti=ti, t0=t0, tn=tn, pst=pst, wb=wb: nc.tensor.matmul(
                            pst[:, 0:tn], wb[:, c, ti * 128:(ti + 1) * 128], hn[:, c, t0:t0 + tn],
                            start=(c == 0), stop=(c == 7)) for c in range(8)],
                            r=[("hn", n), ("wbf", gi % 2)], w=[("ps", pi)])
                        ri, ai = cnt % 3, cnt % 4
                        cnt += 1
                        S.op("act", lambda pst=pst, ri=ri, tn=tn: nc.scalar.activation(
                            out=rl[ri][:, 0:tn], in_=pst[:, 0:tn], func=AF.Relu), r=[("ps", pi)], w=[("rl", ri)])
                        eng = "dve"
                        E = nc.vector if eng == "dve" else nc.gpsimd
                        S.op(eng, lambda E=E, ri=ri, ai=ai, tn=tn: E.tensor_tensor(
                            out=ab[ai][:, 0:tn], in0=rl[ri][:, 0:tn], in1=rl[ri][:, 0:tn], op=ALU.mult),
                            r=[("rl", ri)], w=[("ab", ai)])
                        S.op("sp", lambda fc=fc, ai=ai, t0=t0, tn=tn: nc.sync.dma_start(
                            out=self.aT[fc, :, t0:t0 + tn], in_=ab[ai][:, 0:tn]), r=[("ab", ai)], w=[("d_aT", fc, n)],
                            dma=True)
            S.barrier()

    def _ffn2(self, l):
        nc, S = self.nc, self.S
        with contextlib.ExitStack() as ph:
            sb = lambda name, shape, dt: self.sb(ph, name, shape, dt)
            w2 = sb("gw2", [128, 32, D], BF16)
            wst = [sb("gwst%d" % i, [128, D], F32) for i in range(2)]
            ab = [sb("gab%d" % i, [128, 32, 512], BF16) for i in range(2)]
            hb = [sb("ghb%d" % i, [128, 8, 512], F32) for i in range(2)]
            ff = sb("gff", [128, 8, 512], F32)
            sq = sb("gsq", [128, 8, 512], F32)
            rs = sb("grs", [128, 512], F32)
            for c in range(32):
                b = c % 2
                S.op("sp", lambda b=b, c=c: nc.sync.dma_start(out=wst[b][:], in_=self.mlp_w2[l, c * 128:(c + 1) * 128, :]),
                     w=[("wst", b)], dma=True)
                S.op("pool", lambda b=b, c=c: nc.gpsimd.tensor_copy(out=w2[:, c, :], in_=wst[b][:]),
                     r=[("wst", b)], w=[("w2", c)])
            w2keys = [("w2", c) for c in range(32)]
            pcount = 0
            for n, (t0, tn) in enumerate(BLOCKS):
                b = n % 2
                S.op("sp", lambda b=b, t0=t0, tn=tn: nc.sync.dma_start(
                    out=ab[b][:, :, 0:tn], in_=self.aT[:, :, t0:t0 + tn].rearrange("c p t -> p c t")),
                    w=[("ab", b)], dma=True)
                S.op("sp", lambda b=b, t0=t0, tn=tn: nc.sync.dma_start(
                    out=hb[b][:, :, 0:tn], in_=self.hT[:, :, t0:t0 + tn].rearrange("c p t -> p c t")),
                    w=[("hb", b, c) for c in range(8)], dma=True)
                for dc2 in range(8):
                    pi = 1 + pcount % 7
                    pcount += 1
                    pst = self.ps[pi]
                    S.op("pe", [lambda fc=fc, pst=pst, dc2=dc2, b=b, tn=tn: nc.tensor.matmul(
                        pst[:, 0:tn], w2[:, fc, dc2 * 128:(dc2 + 1) * 128], ab[b][:, fc, 0:tn],
                        start=(fc == 0), stop=(fc == 31)) for fc in range(32)],
                        r=[("ab", b)] + w2keys, w=[("ps", pi)])
                    S.op("act", lambda pst=pst, dc2=dc2, tn=tn: nc.scalar.copy(out=ff[:, dc2, 0:tn], in_=pst[:, 0:tn]),
                         r=[("ps", pi)], w=[("ff", dc2)])
                self.rms_rstd(ff, 8, D, tn, sq, rs, 0, [("ff", c) for c in range(8)], "f")
                for c in range(8):
                    S.op("dve", lambda c=c, tn=tn: nc.vector.scalar_tensor_tensor(
                        out=ff[:, c, 0:tn], in0=ff[:, c, 0:tn], scalar=self.gain(l, 3, c), in1=rs[:, 0:tn],
                        op0=ALU.mult, op1=ALU.mult), r=[("ff", c), ("rstd", "f")], w=[("ff", c)])
                    S.op("pool", lambda c=c, b=b, tn=tn: nc.gpsimd.tensor_tensor(
                        out=hb[b][:, c, 0:tn], in0=hb[b][:, c, 0:tn], in1=ff[:, c, 0:tn], op=ALU.add),
                        r=[("hb", b, c), ("ff", c)], w=[("hb", b, c)])
                S.op("sp", lambda b=b, t0=t0, tn=tn: nc.sync.dma_start(
                    out=self.hT[:, :, t0:t0 + tn].rearrange("c p t -> p c t"), in_=hb[b][:, :, 0:tn]),
                    r=[("hb", b, c) for c in range(8)], w=[("d_hT", n)], dma=True)
            S.barrier()

    def _epilogue(self):
        nc, S = self.nc, self.S
        with contextlib.ExitStack() as ph:
            hb = [self.sb(ph, "ehb%d" % i, [128, 8, 128], F32) for i in range(2)]
            ot = [self.sb(ph, "eot%d" % i, [128, D], F32) for i in range(2)]
            for i in range(NT):
                b = i % 2
                S.op("sp", lambda b=b, i=i: nc.sync.dma_start(
                    out=hb[b][:], in_=self.hT[:, :, i * 128:(i + 1) * 128].rearrange("c p t -> p c t")),
                    w=[("hb", b)], dma=True)
                for half in range(2):
                    pi = (2 * i + half) % 4
                    pst = self.ps[pi]
                    S.op("pe", [lambda c=c, half=half, pst=pst, b=b: nc.tensor.transpose(
                        pst[:, c * 128:(c + 1) * 128], hb[b][:, half * 4 + c, :], self.ident[:]) for c in range(4)],
                        r=[("hb", b)], w=[("ps", pi)])
                    if half == 0:
                        S.op("act", lambda pst=pst, b=b: nc.scalar.copy(out=ot[b][:, 0:512], in_=pst[:, :]),
                             r=[("ps", pi)], w=[("ot", b, 0)])
                    else:
                        S.op("dve", lambda pst=pst, b=b: nc.vector.tensor_copy(out=ot[b][:, 512:1024], in_=pst[:, :]),
                             r=[("ps", pi)], w=[("ot", b, 1)])
                lo = 128 * i - NMETA
                if i == 0:
                    src, dst = ot[b][NMETA:128, :], self.out[0:128 - NMETA, :]
                elif i == L // 128 - 1:
                    src, dst = ot[b][0:NMETA, :], self.out[lo:S_LEN, :]
                else:
                    src, dst = ot[b][:, :], self.out[lo:lo + 128, :]
                S.op("sp", lambda src=src, dst=dst: nc.sync.dma_start(out=dst, in_=src),
                     r=[("ot", b, 0), ("ot", b, 1)], w=[("d_out", i)], dma=True)
            S.barrier()


def _consts():
    pos = np.arange(L, dtype=np.float32)
    inv_freq = (10000.0 ** (-np.arange(0, 32, 2, dtype=np.float32) / 32.0)).astype(np.float32)
    ang = (pos[:, None] * inv_freq[None, :]).astype(np.float32)
    cos = np.cos(ang).astype(np.float32).T
    sin = np.sin(ang).astype(np.float32).T
    psw = np.zeros((32, 32), np.float32)
    for m in range(16):
        psw[16 + m, m] = -1.0
        psw[m, 16 + m] = 1.0
    a = np.arange(128)
    lim = np.where(a < 16, 16, np.where(a < 80, 80, 128))
    kb = np.arange(128)
    m_diag = (kb[:, None] < lim[None, :]).astype(np.float32)
    m_next = ((a[None, :] >= 80) & (kb[:, None] < 16)).astype(np.float32)
    mask = np.stack([m_diag, m_next], 0).astype(ml_dtypes.bfloat16)
    r_ = np.arange(128)
    tri = ((r_[:, None] <= r_[None, :]) & ((r_[:, None] // 64) == (r_[None, :] // 64))).astype(np.float32)
    mrs = ((r_[:, None] % 64) <= np.arange(64)[None, :]).astype(np.float32)
    obk = np.zeros((128, 2, 128), np.float32)
    obk[0:64, 0, :] = 1.0
    obk[64:128, 1, :] = 1.0
    return {"c_tri": tri, "c_mrs": mrs, "c_ob": obk, "c_iota": np.arange(512, dtype=np.float32), "c_ident": np.eye(128, dtype=np.float32),
            "c_cos": np.ascontiguousarray(np.concatenate([cos, cos], 0)),
            "c_sin": np.ascontiguousarray(np.concatenate([sin, sin], 0)),
            "c_psw": psw, "c_mask": mask}


_WEIGHT_KEYS = ["meta", "norm_gains", "w_in", "conv_w", "mlstm_gate_b", "mlstm_norm", "s5_a_re", "s5_a_im",
                "s5_log_step", "s5_b_re", "s5_b_im", "s5_c_re", "s5_c_im", "s5_d", "s5_glu", "mla_q_norm",
                "mla_kv_norm", "mla_w_uq", "mla_w_ukv", "w_branch", "w_out", "mlp_w1", "mlp_w2"]


def make_in_maps(inputs, n_cores=8):
    shared = {k: np.ascontiguousarray(np.asarray(inputs[k], dtype=np.float32)) for k in _WEIGHT_KEYS}
    shared.update(_consts())
    x = np.asarray(inputs["x"], dtype=np.float32)
    maps = []
    for b in range(n_cores):
        m = dict(shared)
        m["x"] = np.ascontiguousarray(x[b])
        maps.append(m)
    return maps


def kernel(**inputs):
    nc = Builder().build()
    in_maps = make_in_maps(inputs, 8)
    res = run_bass_kernel_spmd(nc, in_maps, core_ids=list(range(8)))
    return np.stack([np.asarray(r["out"], dtype=np.float32) for r in res.results], axis=0)
```
